# Optimizing a Trainium2 kernel written in Bass

```python
import jax, jax.numpy as jnp
from jax import lax
import numpy as np

D_MODEL = 1024
BATCH = 8
SEQ = 2048
DEPTH = 4
DEC_BATCH = 32
DEC_SEQ = 1
PAST_LEN = 16384
PAGE_SIZE = 128

H_A = 4
DK_A = 64
DV_A = 64
W_A = H_A * DV_A
CHUNK_A = 64
MIN_F = 1e-20
H_B = 8
Q_LORA = 384
KV_LORA = 256
NOPE_DIM = 64
ROPE_DIM = 32
V_DIM = 64
W_B = H_B * V_DIM
ROPE_BASE = 10000.0
Q_BLOCK = 128
SM_SCALE = (NOPE_DIM + ROPE_DIM) ** -0.5
MASK_VALUE = -1e30
W_C = 256
G_C = 4
CONV_W = 3
MIX_WIDTH = W_A + W_B + W_C
FFN_HIDDEN = -(-8 * D_MODEL // (3 * 256)) * 256
EPS = 1e-6

A_Q = 0
A_F = A_Q + H_A * DK_A
A_I = A_F + H_A * DK_A
A_G = A_I + W_A
B_Q = A_G + W_A
B_KV = B_Q + Q_LORA
B_R = B_KV + KV_LORA
C_B = B_R + ROPE_DIM
C_C = C_B + W_C
C_H = C_C + W_C
IN_WIDTH = C_H + W_C

kernel_name = 'hymba_hgrn2_mla_shortconv_step'


def rms_norm(x, g):
    xf = x.astype(jnp.float32)
    y = xf * lax.rsqrt(jnp.mean(xf * xf, axis=-1, keepdims=True) + EPS)
    return (y * g.astype(jnp.float32)).astype(x.dtype)


def apply_rope(x, pos):
    half = ROPE_DIM // 2
    inv = ROPE_BASE ** (-jnp.arange(half, dtype=jnp.float32) / half)
    ang = pos.astype(jnp.float32)[:, None] * inv[None, :]
    cos, sin = jnp.cos(ang), jnp.sin(ang)
    if x.ndim == 4:
        cos, sin = cos[:, None], sin[:, None]
    xf = x.astype(jnp.float32)
    x1, x2 = xf[..., :half], xf[..., half:]
    return jnp.concatenate([x1 * cos - x2 * sin, x2 * cos + x1 * sin], axis=-1).astype(x.dtype)


def hgrn2_chunk(S, q, k, v, logf):
    C = q.shape[1]
    b = jnp.cumsum(logf, axis=1)
    causal = jnp.tril(jnp.ones((C, C), dtype=bool))[None, :, :, None, None]
    diff = b[:, :, None] - b[:, None, :]
    decay = jnp.where(causal, jnp.exp(jnp.where(causal, diff, 0.0)), 0.0)
    scores = jnp.einsum('bthk,bshk,btshk->bhts', q, k, decay)
    o = (jnp.einsum('bchk,bhkv->bchv', q * jnp.exp(b), S)
         + jnp.einsum('bhts,bshv->bthv', scores, v))
    b_last = b[:, -1:]
    S_new = (jnp.exp(b_last[:, 0])[..., None] * S
             + jnp.einsum('bshk,bshv->bhkv', k * jnp.exp(b_last - b), v))
    return S_new, o


def hgrn2_scan(S0, q, k, v, logf):
    Bn, T = q.shape[:2]
    c = CHUNK_A if T % CHUNK_A == 0 else T
    n = T // c

    def to_chunks(a):
        return a.reshape(Bn, n, c, *a.shape[2:]).swapaxes(0, 1)

    S, o = lax.scan(lambda s, xs: hgrn2_chunk(s, *xs), S0,
                    (to_chunks(q), to_chunks(k), to_chunks(v), to_chunks(logf)))
    return S, o.swapaxes(0, 1).reshape(Bn, T, H_A, DV_A)


def hgrn2_group(z, S0, lb, onorm_g):
    f32 = jnp.float32
    Bn, T = z.shape[:2]
    q = jax.nn.silu(z[..., A_Q:A_F].astype(f32)).reshape(Bn, T, H_A, DK_A)
    fz = z[..., A_F:A_I].astype(f32).reshape(Bn, T, H_A, DK_A)
    lbh = lb.astype(f32).reshape(H_A, DK_A)
    f = lbh + (1.0 - lbh) * jax.nn.sigmoid(fz)
    logf = jnp.log(jnp.maximum(f, MIN_F))
    k = 1.0 - f
    v = z[..., A_I:A_G].astype(f32).reshape(Bn, T, H_A, DV_A)
    S, o = hgrn2_scan(S0.astype(f32), q, k, v, logf)
    gate = z[..., A_G:B_Q].astype(f32)
    out = rms_norm(o, onorm_g).reshape(Bn, T, W_A) * jax.nn.silu(gate)
    return out.astype(z.dtype), S


def mla_project(z, pos, p):
    cq = rms_norm(z[..., B_Q:B_KV], p['mla_q_norm_g'])
    q = jnp.einsum('btc,chd->bthd', cq, p['mla_w_q_up'])
    qg, kg = p['mla_q_head_g'], p['mla_k_head_g']
    q_nope = rms_norm(q[..., :NOPE_DIM], qg[:NOPE_DIM])
    q_rope = apply_rope(rms_norm(q[..., NOPE_DIM:], qg[NOPE_DIM:]), pos)
    c_kv = rms_norm(z[..., B_KV:B_R], p['mla_kv_norm_g'])
    k_rope = apply_rope(rms_norm(z[..., B_R:C_B], kg[NOPE_DIM:]), pos)
    return q_nope, q_rope, c_kv, k_rope


def mla_attend_prompt(q_nope, q_rope, k_nope, k_rope, v):
    Bn, T = q_nope.shape[:2]
    kpos = jnp.arange(T)

    def block(i):
        s0 = i * Q_BLOCK
        qn = lax.dynamic_slice_in_dim(q_nope, s0, Q_BLOCK, axis=1)
        qr = lax.dynamic_slice_in_dim(q_rope, s0, Q_BLOCK, axis=1)
        s = (jnp.einsum('bqhd,bkhd->bhqk', qn, k_nope)
             + jnp.einsum('bqhr,bkr->bhqk', qr, k_rope)).astype(jnp.float32) * SM_SCALE
        qpos = s0 + jnp.arange(Q_BLOCK)
        s = jnp.where(kpos[None, :] <= qpos[:, None], s, MASK_VALUE)
        pr = jax.nn.softmax(s, axis=-1).astype(v.dtype)
        return jnp.einsum('bhqk,bkhd->bqhd', pr, v)

    o = lax.map(block, jnp.arange(T // Q_BLOCK))
    return o.swapaxes(0, 1).reshape(Bn, T, H_B, V_DIM)


def mla_attend_sample(q_nope, q_rope, c_all, kr_all, w_uk, w_uv, k_nope_g):
    T = q_nope.shape[1]
    L = c_all.shape[1]
    k_nope = rms_norm(jnp.einsum('blc,chd->blhd', c_all, w_uk), k_nope_g)
    s = (jnp.einsum('bqhd,blhd->bhql', q_nope, k_nope)
         + jnp.einsum('bqhr,blr->bhql', q_rope, kr_all)).astype(jnp.float32) * SM_SCALE
    kpos = jnp.arange(L)
    qpos = (L - T) + jnp.arange(T)
    s = jnp.where(kpos[None, :] <= qpos[:, None], s, MASK_VALUE)
    pr = jax.nn.softmax(s, axis=-1).astype(c_all.dtype)
    o_lat = jnp.einsum('bhql,blc->bqhc', pr, c_all)
    return jnp.einsum('bqhc,chd->bqhd', o_lat, w_uv)


def conv_group(z, prev, conv_w):
    u = z[..., C_C:C_H] * z[..., C_H:IN_WIDTH]
    up = jnp.concatenate([prev.astype(u.dtype), u], axis=1)
    y = lax.conv_general_dilated(up, conv_w.astype(u.dtype).reshape(CONV_W, 1, W_C),
                                 window_strides=(1,), padding='VALID',
                                 dimension_numbers=('NWC', 'WIO', 'NWC'),
                                 feature_group_count=W_C)
    return z[..., C_B:C_C] * y, up[:, -(CONV_W - 1):]


def trunk_layer(x, pos, hgrn_s0, conv_prev, past_ckv, past_kr, lb, p):
    Bn, T = x.shape[:2]
    h = rms_norm(x, p['norm_mix_g'])
    z = jnp.einsum('btd,de->bte', h, p['w_in'])
    a_out, hgrn_s = hgrn2_group(z, hgrn_s0, lb, p['hgrn_onorm_g'])
    q_nope, q_rope, c_kv, k_rope = mla_project(z, pos, p)
    w_uk = p['mla_w_kv_up'][..., :NOPE_DIM]
    w_uv = p['mla_w_kv_up'][..., NOPE_DIM:]
    k_nope_g = p['mla_k_head_g'][:NOPE_DIM]
    if past_ckv is None:
        k_nope = rms_norm(jnp.einsum('btc,chd->bthd', c_kv, w_uk), k_nope_g)
        v = jnp.einsum('btc,chd->bthd', c_kv, w_uv)
        b_out = mla_attend_prompt(q_nope, q_rope, k_nope, k_rope, v)
    else:
        c_all = jnp.concatenate([past_ckv.astype(c_kv.dtype), c_kv], axis=1)
        kr_all = jnp.concatenate([past_kr.astype(k_rope.dtype), k_rope], axis=1)
        b_out = mla_attend_sample(q_nope, q_rope, c_all, kr_all, w_uk, w_uv, k_nope_g)
    b_out = rms_norm(b_out, p['mla_onorm_g'].reshape(H_B, V_DIM)).reshape(Bn, T, W_B)
    c_out, conv_s = conv_group(z, conv_prev, p['conv_w'])
    c_out = rms_norm(c_out.reshape(Bn, T, G_C, W_C // G_C),
                     p['conv_onorm_g'].reshape(G_C, W_C // G_C)).reshape(Bn, T, W_C)
    mixed = jnp.concatenate([a_out, b_out.astype(z.dtype), c_out.astype(z.dtype)], axis=-1)
    x = x + jnp.einsum('bte,ed->btd', mixed, p['w_out'])
    h2 = rms_norm(x, p['norm_ffn_g'])
    ff = jax.nn.silu(h2 @ p['w_gate']) * (h2 @ p['w_up'])
    x = x + ff @ p['w_down']
    return x, (c_kv, k_rope, hgrn_s, conv_s)


def _normal(k, shape, scale):
    return jax.random.normal(k, shape, jnp.float32) * scale


def _gain(k, shape):
    return 1.0 + 0.1 * jax.random.normal(k, shape, jnp.float32)


def setup_inputs(seed: int = 0) -> dict:
    key = jax.random.key(seed)
    ks = jax.random.split(key, 26)
    n_pages = PAST_LEN // PAGE_SIZE
    n_pool = (5 * DEC_BATCH * n_pages + 3) // 4
    perm = jax.random.permutation(ks[0], n_pool)
    page_table = perm[:DEC_BATCH * n_pages].reshape(DEC_BATCH, n_pages).astype(jnp.int32)
    return {
        'x_prompt': _normal(ks[1], (BATCH, SEQ, D_MODEL), 1.0),
        'x_sample': _normal(ks[2], (DEC_BATCH, DEC_SEQ, D_MODEL), 1.0),
        'cache_kv_latent': _normal(ks[3], (DEPTH, n_pool, PAGE_SIZE, KV_LORA), 1.0),
        'cache_k_rope': _normal(ks[4], (DEPTH, n_pool, PAGE_SIZE, ROPE_DIM), 1.0),
        'state_hgrn': _normal(ks[5], (DEPTH, DEC_BATCH, H_A, DK_A, DV_A), 0.3),
        'state_conv': _normal(ks[6], (DEPTH, DEC_BATCH, CONV_W - 1, W_C), 1.0),
        'page_table': page_table,
        'norm_mix_g': _gain(ks[7], (DEPTH, D_MODEL)),
        'w_in': _normal(ks[8], (DEPTH, D_MODEL, IN_WIDTH), D_MODEL ** -0.5),
        'hgrn_lb': _gain(ks[9], (DEPTH, H_A * DK_A)),
        'hgrn_onorm_g': _gain(ks[10], (DEPTH, DV_A)),
        'mla_q_norm_g': _gain(ks[11], (DEPTH, Q_LORA)),
        'mla_w_q_up': _normal(ks[12], (DEPTH, Q_LORA, H_B, NOPE_DIM + ROPE_DIM), Q_LORA ** -0.5),
        'mla_kv_norm_g': _gain(ks[13], (DEPTH, KV_LORA)),
        'mla_w_kv_up': _normal(ks[14], (DEPTH, KV_LORA, H_B, NOPE_DIM + V_DIM), KV_LORA ** -0.5),
        'mla_q_head_g': _gain(ks[15], (DEPTH, NOPE_DIM + ROPE_DIM)),
        'mla_k_head_g': _gain(ks[16], (DEPTH, NOPE_DIM + ROPE_DIM)),
        'mla_onorm_g': _gain(ks[17], (DEPTH, W_B)),
        'conv_w': _normal(ks[18], (DEPTH, CONV_W, W_C), CONV_W ** -0.5),
        'conv_onorm_g': _gain(ks[19], (DEPTH, W_C)),
        'w_out': _normal(ks[20], (DEPTH, MIX_WIDTH, D_MODEL), MIX_WIDTH ** -0.5),
        'norm_ffn_g': _gain(ks[21], (DEPTH, D_MODEL)),
        'w_gate': _normal(ks[22], (DEPTH, D_MODEL, FFN_HIDDEN), D_MODEL ** -0.5),
        'w_up': _normal(ks[23], (DEPTH, D_MODEL, FFN_HIDDEN), D_MODEL ** -0.5),
        'w_down': _normal(ks[24], (DEPTH, FFN_HIDDEN, D_MODEL), FFN_HIDDEN ** -0.5),
    }


def reference(x_prompt, x_sample, cache_kv_latent, cache_k_rope, state_hgrn, state_conv, page_table,
              norm_mix_g, w_in, hgrn_lb, hgrn_onorm_g, mla_q_norm_g, mla_w_q_up, mla_kv_norm_g,
              mla_w_kv_up, mla_q_head_g, mla_k_head_g, mla_onorm_g, conv_w, conv_onorm_g, w_out,
              norm_ffn_g, w_gate, w_up, w_down):
    f32 = jnp.float32
    lb_w = jax.nn.softmax(hgrn_lb.astype(f32), axis=0)
    lb_all = jnp.cumsum(lb_w, axis=0) - lb_w[0:1]
    Bp, Tp = x_prompt.shape[:2]
    Bs, Ts = x_sample.shape[:2]
    past_len = page_table.shape[1] * cache_kv_latent.shape[2]
    pos_p = jnp.arange(Tp)
    pos_s = past_len + jnp.arange(Ts)
    xp, xs = x_prompt, x_sample
    st_p, st_s = [], []
    for l in range(DEPTH):
        p = {
            'norm_mix_g': norm_mix_g[l], 'w_in': w_in[l], 'hgrn_onorm_g': hgrn_onorm_g[l],
            'mla_q_norm_g': mla_q_norm_g[l], 'mla_w_q_up': mla_w_q_up[l],
            'mla_kv_norm_g': mla_kv_norm_g[l], 'mla_w_kv_up': mla_w_kv_up[l],
            'mla_q_head_g': mla_q_head_g[l], 'mla_k_head_g': mla_k_head_g[l],
            'mla_onorm_g': mla_onorm_g[l], 'conv_w': conv_w[l], 'conv_onorm_g': conv_onorm_g[l],
            'w_out': w_out[l], 'norm_ffn_g': norm_ffn_g[l], 'w_gate': w_gate[l], 'w_up': w_up[l],
            'w_down': w_down[l],
        }
        xp, sp = trunk_layer(xp, pos_p, jnp.zeros((Bp, H_A, DK_A, DV_A), f32),
                             jnp.zeros((Bp, CONV_W - 1, W_C), xp.dtype), None, None, lb_all[l], p)
        past_ckv = cache_kv_latent[l][page_table].reshape(Bs, past_len, KV_LORA)
        past_kr = cache_k_rope[l][page_table].reshape(Bs, past_len, ROPE_DIM)
        xs, ss = trunk_layer(xs, pos_s, state_hgrn[l], state_conv[l], past_ckv, past_kr, lb_all[l], p)
        st_p.append(sp)
        st_s.append(ss)
    kv_latent_prompt = jnp.stack([s[0] for s in st_p])
    k_rope_prompt = jnp.stack([s[1] for s in st_p])
    hgrn_prompt = jnp.stack([s[2] for s in st_p])
    conv_prompt = jnp.stack([s[3] for s in st_p])
    kv_latent_sample = jnp.stack([s[0] for s in st_s])
    k_rope_sample = jnp.stack([s[1] for s in st_s])
    hgrn_sample = jnp.stack([s[2] for s in st_s])
    conv_sample = jnp.stack([s[3] for s in st_s])
    return (xp, xs, kv_latent_prompt, k_rope_prompt, hgrn_prompt, conv_prompt,
            kv_latent_sample, k_rope_sample, hgrn_sample, conv_sample)
```

```python
import numpy as np
from contextlib import ExitStack
import concourse.bass as bass
import concourse.mybir as mybir
from concourse.bass_utils import run_bass_kernel_spmd

F32 = mybir.dt.float32
BF16 = mybir.dt.bfloat16
I32 = mybir.dt.int32
ALU = mybir.AluOpType
AF = mybir.ActivationFunctionType
AX = mybir.AxisListType

NL = 4
T = 2048
NS = 4
SEG = 512
EPS = 1e-6
SM_SCALE = 96 ** -0.5
NPAGE = 128
POOL = 5120
POOLN = [5120]


class _Op:
    __slots__ = ("eng", "fn", "dma", "deps", "sig", "sigval")


class Prog:
    ENGS = ("pe", "act", "dve", "pool", "sp")
    BLK = {"pe": "tensor", "act": "scalar", "dve": "vector", "pool": "gpsimd", "sp": "sync"}

    def __init__(self, nc):
        self.nc = nc
        self.ops = []
        self.last_writer = {}
        self.readers = {}

    def add(self, eng, fn, reads=(), writes=(), dma=None):
        o = _Op()
        o.eng, o.fn, o.dma, o.sig, o.sigval = eng, fn, dma, False, 0
        deps = set()
        for r in reads:
            w = self.last_writer.get(r)
            if w is not None:
                deps.add(w)
        for w in writes:
            lw = self.last_writer.get(w)
            if lw is not None:
                deps.add(lw)
            for rd in self.readers.get(w, ()):
                if rd.eng == eng and rd.dma is None and dma is None:
                    continue
                deps.add(rd)
        if eng == "pe" and dma is None:
            deps = {d for d in deps if not (d.eng == "pe" and d.dma is None)}
        for d in deps:
            d.sig = True
        o.deps = deps
        for r in reads:
            self.readers.setdefault(r, []).append(o)
        for w in writes:
            self.last_writer[w] = o
            self.readers[w] = []
        self.ops.append(o)
        return o

    def emit(self, stack):
        nc = self.nc
        cnt = {e: 0 for e in self.ENGS}
        dcnt = {}
        for o in self.ops:
            if o.dma is not None:
                dcnt[o.dma] = dcnt.get(o.dma, 0) + 16
                o.sigval = dcnt[o.dma]
            elif o.sig:
                cnt[o.eng] += 1
                o.sigval = cnt[o.eng]
        esem = {e: stack.enter_context(nc.semaphore("e_" + e)) for e in self.ENGS}
        dsem = {k: stack.enter_context(nc.semaphore("d_%d" % i)) for i, k in enumerate(dcnt)}
        block = stack.enter_context(nc.Block())
        for eng in self.ENGS:
            ops = [o for o in self.ops if o.eng == eng]

            def body(e, ops=ops, eng=eng):
                waited = {}
                for o in ops:
                    need = {}
                    for d in o.deps:
                        key = ("d", d.dma) if d.dma is not None else ("e", d.eng)
                        if d.sigval > need.get(key, 0):
                            need[key] = d.sigval
                    for key, v in need.items():
                        if waited.get(key, 0) >= v:
                            continue
                        e.wait_ge(dsem[key[1]] if key[0] == "d" else esem[key[1]], v)
                        waited[key] = v
                    ins = o.fn(e)
                    if o.dma is not None:
                        ins.then_inc(dsem[o.dma], 16)
                    elif o.sig:
                        ins.then_inc(esem[o.eng], 1)

            getattr(block, self.BLK[eng])(body)


G_MIX, G_FFN, G_QN, G_KVN, G_QHN, G_QHR, G_KHN, G_KHR, G_ON, G_CON, G_CW, G_HON, G_QHNM, G_ONP = 0, 8, 16, 19, 21, 22, 23, 24, 25, 33, 35, 41, 42, 44
NG = 48
C_ID, C_ROT, C_BO64, C_BO32, C_ONE, C_BD, C_I2, C_IND, C_CM, C_TRI = 0, 128, 256, 384, 512, 640, 768, 832, 864, 896
C_EB = 1024
NCSTB = 1536
C_INVF, C_PM, C_MA, C_MS = 1536, 1537, 1569, 1571
NCST = 1575
NLAYERS = NL
MAXOPS = None
SMIX = 'ABC'
FINAL_FILTER = None
DBG = dict(A=True, B=True, C=True, ATT=True, FFN=True, WOUT=True, S=True)

W_AQ, W_AF, W_AI, W_AG, W_BQ, W_BKV, W_BR, W_CB, W_CC, W_CH, W_KR3 = 0, 256, 512, 768, 1024, 1408, 1664, 1696, 1952, 2208, 2464
WIN_COLS = 2464 + 128


def build_nc():
    nc = bass.Bass("TRN2", target_bir_lowering=False)
    D = {}

    def din(name, shape, dt=F32):
        D[name] = nc.dram_tensor(name, list(shape), dt, kind="ExternalInput").ap()

    def dout(name, shape):
        D[name] = nc.dram_tensor(name, list(shape), F32, kind="ExternalOutput").ap()

    din("xp", [T, 1024]); din("xs", [NS, 1024])
    for l in range(NL):
        din("ckv%d" % l, [POOLN[0] * 16, 2048]); din("ckr%d" % l, [POOLN[0] * 2, 2048])
    din("sh", [NL, NS, 2, 128, 64]); din("sc", [NL, NS, 2, 256]); din("ptT", [128, NS], I32); din("wukT", [NL, 128, 4, 256])
    din("win", [NL, 128, 8, WIN_COLS]); din("wqn", [NL, 128, 3, 512]); din("wqr", [NL, 128, 3, 256])
    din("wuk", [NL, 128, 2, 512]); din("wuv", [NL, 128, 2, 512]); din("wo", [NL, 128, 8, 1024])
    din("wg", [NL, 128, 8, 2816]); din("wu", [NL, 128, 8, 2816]); din("wd", [NL, 128, 22, 1024])
    din("gv", [128, NL, NG]); din("lball", [128, 2, NL]); din("cst", [128, NCST])
    dout("yp", [T, 1024]); dout("ys", [NS, 1024]); dout("kvp", [NL, T, 256]); dout("krp", [NL, T, 32])
    dout("hp", [NL, 2, 128, 64]); dout("cp", [NL, 2, 256]); dout("kvs", [NL, NS, 256]); dout("krs", [NL, NS, 32])
    dout("hs", [NL, NS, 2, 128, 64]); dout("cs", [NL, NS, 2, 256])

    st = ExitStack()
    with st:
        P = Prog(nc)
        sbn = [0]

        def sb(shape, dt=F32, stack=st):
            sbn[0] += 1
            return stack.enter_context(nc.sbuf_tensor("t%d" % sbn[0], list(shape), dt))

        PS = [st.enter_context(nc.psum_tensor("ps%d" % i, [128, 512], F32)) for i in range(8)]
        psn = [0]

        def nps(lo=0, hi=6):
            i = lo + psn[0] % (hi - lo)
            psn[0] += 1
            return i

        outkeys = []
        dman = [0]

        def mm(out, lhsT, rhs, start, stop, rd, wr):
            P.add("pe", lambda e: e.matmul(out, lhsT=lhsT, rhs=rhs, start=start, stop=stop), reads=rd, writes=wr)

        def tr(out, in_, ident, rd, wr):
            P.add("pe", lambda e: e.transpose(out=out, in_=in_, identity=ident), reads=rd, writes=wr)

        def act(out, in_, func, rd, wr, scale=None, bias=None):
            kw = {}
            if scale is not None:
                kw["scale"] = scale
            if bias is not None:
                kw["bias"] = bias
            P.add("act", lambda e: e.activation(out=out, in_=in_, func=func, **kw), reads=rd, writes=wr)

        def tt(out, in0, in1, op, rd, wr, eng="dve"):
            P.add(eng, lambda e: e.tensor_tensor(out=out, in0=in0, in1=in1, op=op), reads=rd, writes=wr)

        def ts(out, in0, s1, s2, op0, op1, rd, wr, eng="dve"):
            if op1 is None:
                P.add(eng, lambda e: e.tensor_scalar(out=out, in0=in0, scalar1=s1, scalar2=None, op0=op0), reads=rd, writes=wr)
            else:
                P.add(eng, lambda e: e.tensor_scalar(out=out, in0=in0, scalar1=s1, scalar2=s2, op0=op0, op1=op1), reads=rd, writes=wr)

        def stt(out, in0, scalar, in1, op0, op1, rd, wr):
            P.add("dve", lambda e: e.scalar_tensor_tensor(out=out, in0=in0, scalar=scalar, in1=in1, op0=op0, op1=op1), reads=rd, writes=wr)

        def cp(out, in_, rd, wr, eng="dve"):
            P.add(eng, lambda e: e.tensor_copy(out=out, in_=in_), reads=rd, writes=wr)

        def mset(ap, v, wr, eng="dve"):
            P.add(eng, lambda e: e.memset(ap, v), writes=wr)

        def dma(out, in_, rd, wr, key=None, eng="sp", slow=False):
            if key is None:
                dman[0] += 1
                key = "dma%d" % dman[0]
            if slow:
                P.add(eng, lambda e: e.dma_start(out=out, in_=in_, allow_slow_non_contiguous=True), reads=rd, writes=wr, dma=key)
            else:
                P.add(eng, lambda e: e.dma_start(out=out, in_=in_), reads=rd, writes=wr, dma=key)

        def gather(out, table, idx_ap, rd, wr, key):
            P.add("pool", lambda e: e.indirect_dma_start(out=out, out_offset=None, in_=table,
                                                         in_offset=bass.IndirectOffsetOnAxis(ap=idx_ap, axis=0)), reads=rd, writes=wr, dma=key)

        def treduce(out, in_, rd, wr):
            P.add("dve", lambda e: e.tensor_reduce(out=out, in_=in_, axis=AX.X, op=ALU.add), reads=rd, writes=wr)

        def store(out, in_, rd, key, sem, slow=False):
            ok = "O_" + key
            outkeys.append(ok)
            if slow:
                P.add("sp", lambda e: e.dma_start(out=out, in_=in_, allow_slow_non_contiguous=True), reads=rd, writes=[ok], dma="S_" + sem)
            else:
                P.add("sp", lambda e: e.dma_start(out=out, in_=in_), reads=rd, writes=[ok], dma="S_" + sem)

        def scan_add(out, ones_ap, in_, rd, wr):
            P.add("dve", lambda e: e.tensor_tensor_scan(out=out, data0=ones_ap, data1=in_, initial=0.0, op0=ALU.mult, op1=ALU.add), reads=rd, writes=wr)

        def recip(out, in_, rd, wr):
            P.add("dve", lambda e: e.reciprocal(out=out, in_=in_), reads=rd, writes=wr)

        def xk(c0):
            return "xT%d" % min(c0 // SEG, 4)

        NT = T + NS
        xT = sb([128, 8, NT])
        cst = sb([128, NCST])
        cstb = sb([128, NCSTB], BF16)
        gv = sb([128, NL, NG])
        lbt = sb([128, 2, NL]); lbe = sb([128, 2, NL]); lba = sb([128, 2, NL]); oml = sb([128, 2, NL]); lbs = sb([128, 2])
        cosT = sb([128, T], BF16); sinT = sb([128, T], BF16); cosS = sb([128, 1]); sinS = sb([128, 1])
        onesf = sb([128, SEG]); epsc = sb([128, 1])
        UW = 31360
        U = sb([128, UW])

        class Buf:
            pass

        class Arena:
            def __init__(self):
                self.top = 0

            def alloc(self, shape, dt=F32):
                n = int(np.prod(shape))
                words = n if dt in (F32, I32) else (n + 1) // 2
                words = (words + 63) // 64 * 64
                off = self.top
                self.top += words
                assert self.top <= UW, ("arena overflow", self.top)
                ap = U[:, off:off + words]
                if dt != F32:
                    ap = ap.bitcast(dt)
                ap = ap[:, 0:n]
                if len(shape) == 2:
                    ap = ap.rearrange("p (a b) -> p a b", a=shape[0])
                elif len(shape) == 3:
                    ap = ap.rearrange("p (a b c) -> p a b c", a=shape[0], b=shape[1])
                b = Buf()
                b.ap = ap
                b.k = [("U", g) for g in range(off // 64, (off + words) // 64)]
                return b

        AR = Arena()

        dma(cst[:], D["cst"], [], ["cst"]); dma(gv[:], D["gv"], [], ["gv"]); dma(lbt[:], D["lball"], [], ["lbt"])
        cp(cstb[:], cst[:, 0:NCSTB], ["cst"], ["cstb"])
        mset(onesf[:], 1.0, ["onesf"]); mset(epsc[:], EPS, ["epsc"])
        identf = cst[:, C_ID:C_ID + 128]; identb = cstb[:, C_ID:C_ID + 128]; rotf = cst[:, C_ROT:C_ROT + 128]
        bo64 = cstb[:, C_BO64:C_BO64 + 128]; bo32 = cstb[:, C_BO32:C_BO32 + 128]; onesb = cstb[:, C_ONE:C_ONE + 128]
        bdm = cst[:, C_BD:C_BD + 128]; i2f = cst[:, C_I2:C_I2 + 64]; indb = cstb[:, C_IND:C_IND + 32]; cmf = cst[0:32, C_CM:C_CM + 32]
        invf = cst[:, C_INVF:C_INVF + 1]; pm4 = cst[0:4, C_PM:C_PM + 32]; trib = cstb[:, C_TRI:C_TRI + 128]
        bo64f = cst[:, C_BO64:C_BO64 + 128]

        ptT = sb([128, NS], I32); ptf = sb([128, NS]); io16 = sb([128, 16], I32); io16f = sb([128, 16])
        idxf = sb([128, NS, 16]); idx16 = sb([128, NS, 16], I32); idx2f = sb([128, NS, 2]); idx2 = sb([128, NS, 2], I32)
        dma(ptT[:], D["ptT"], [], ["ptT"])
        P.add("pool", lambda e: e.iota(io16[:], pattern=[[1, 16]], base=0, channel_multiplier=0), writes=["io16"])
        cp(io16f[:], io16[:], ["io16"], ["io16f"]); cp(ptf[:], ptT[:], ["ptT"], ["ptf"])
        for b in range(NS):
            ts(idxf[:, b, :], io16f[:], 0.0, ptf[:, b:b + 1], ALU.mult, ALU.add, ["io16f", "ptf"], ["idxf"])
            stt(idxf[:, b, :], idxf[:, b, :], 16.0, io16f[:], ALU.mult, ALU.add, ["idxf", "io16f"], ["idxf"])
            ts(idx2f[:, b, :], io16f[:, 0:2], 0.0, ptf[:, b:b + 1], ALU.mult, ALU.add, ["io16f", "ptf"], ["idx2f"])
            stt(idx2f[:, b, :], idx2f[:, b, :], 2.0, io16f[:, 0:2], ALU.mult, ALU.add, ["idx2f", "io16f"], ["idx2f"])
        cp(idx16[:], idxf[:], ["idxf"], ["idx16"]); cp(idx2[:], idx2f[:], ["idx2f"], ["idx2"])

        act(lbe[:], lbt[:], AF.Exp, ["lbt"], ["lbe"])
        P.add("dve", lambda e: e.tensor_reduce(out=lbs[:], in_=lbe[:], axis=AX.X, op=ALU.add), reads=["lbe"], writes=["lbs"])
        P.add("dve", lambda e: e.reciprocal(out=lbs[:], in_=lbs[:]), reads=["lbs"], writes=["lbs"])
        for pr in range(2):
            ts(lbe[:, pr, :], lbe[:, pr, :], lbs[:, pr:pr + 1], None, ALU.mult, None, ["lbe", "lbs"], ["lbe"])
        mset(lba[:, :, 0:1], 0.0, ["lba"])
        for l in range(1, NL):
            tt(lba[:, :, l:l + 1], lba[:, :, l - 1:l], lbe[:, :, l:l + 1], ALU.add, ["lba", "lbe"], ["lba"])
        ts(oml[:], lba[:], -1.0, 1.0, ALU.mult, ALU.add, ["lba"], ["oml"])

        m0 = AR.top
        rt_n = AR.alloc([T]); rt_i = AR.alloc([T], I32); rt_q = AR.alloc([T]); posP = AR.alloc([T]); posS = AR.alloc([1])

        def ropetab(dst, pos, n, off):
            tn = rt_n.ap[:, 0:n]; ti = rt_i.ap[:, 0:n]; tq = rt_q.ap[:, 0:n]
            kn, ki, kq, kp = rt_n.k, rt_i.k, rt_q.k, posP.k + posS.k
            ts(tn, pos, invf, off, ALU.mult, ALU.add, kp + ["cst"], kn)
            cp(ti, tn, kn, ki)
            cp(tq, ti, ki, kq)
            tt(tn, tn, tq, ALU.subtract, kn + kq, kn)
            ts(tq, tn, 0.5, None, ALU.is_gt, None, kn, kq)
            tt(tn, tn, tq, ALU.subtract, kn + kq, kn)
            ts(tq, tn, -0.5, None, ALU.is_lt, None, kn, kq)
            tt(tn, tn, tq, ALU.add, kn + kq, kn)
            act(dst, tn, AF.Sin, kn, ["tab"], scale=-2.0 * np.pi)

        P.add("pool", lambda e: e.iota(rt_i.ap, pattern=[[1, T]], base=0, channel_multiplier=0), writes=rt_i.k)
        cp(posP.ap, rt_i.ap, rt_i.k, posP.k)
        mset(posS.ap, float(NPAGE * 128), posS.k)
        ropetab(sinT[:], posP.ap, T, 0.5); ropetab(cosT[:], posP.ap, T, 0.75)
        ropetab(sinS[:], posS.ap, 1, 0.5); ropetab(cosS[:], posS.ap, 1, 0.75)
        AR.top = m0

        NBLK = [(i * SEG, SEG) for i in range(T // SEG)] + [(T, NS)]
        xin = [AR.alloc([1024]) for _ in range(2)]
        for tb in range(17):
            rows = 128 if tb < 16 else NS
            src = D["xp"][tb * 128:(tb + 1) * 128, :] if tb < 16 else D["xs"]
            xi = xin[tb % 2]
            dma(xi.ap[0:rows, :], src, [], xi.k, key="xin%d" % (tb % 2))
            for half in range(2):
                pb = nps()
                for k in range(4):
                    dc = half * 4 + k
                    tr(PS[pb][:, k * 128:k * 128 + rows], xi.ap[0:rows, dc * 128:(dc + 1) * 128], identf[0:rows, 0:rows],
                       xi.k + ["cst"], [("ps", pb)])
                cp(xT[:, half * 4:half * 4 + 4, tb * 128:tb * 128 + rows],
                   PS[pb][:].rearrange("p (k t) -> p k t", k=4)[:, :, 0:rows], [("ps", pb)], [xk(tb * 128)])
        AR.top = m0

        WSL = 2
        wstg = [AR.alloc([1024]) for _ in range(WSL)]
        wbf = [AR.alloc([1024], BF16) for _ in range(WSL)]
        wn = [0]

        def wload(src, kp, kc, ncols):
            i = wn[0] % WSL
            wn[0] += 1
            n = kc * ncols
            assert n <= 1024, n
            sv = wstg[i].ap[0:kp, 0:n].rearrange("p (a b) -> p a b", a=kc)
            bv = wbf[i].ap[0:kp, 0:n].rearrange("p (a b) -> p a b", a=kc)
            dma(sv, src, [], wstg[i].k, key="wst%d" % i)
            cp(bv, sv, wstg[i].k, wbf[i].k, eng="pool")
            return bv, wbf[i].k

        sqT = AR.alloc([8, SEG], BF16); lnvT = AR.alloc([SEG]); rstdT = AR.alloc([SEG])

        def rmsnorm_fm(ins, inkeys, n, ones_lhsT, dim, gcols, outs, outkeys_, p0=0, p1=128, kfull=True, dup=False):
            pb = nps()
            r0, r1 = (0, 128) if kfull else (p0, p1)
            nsq = 1 if dup else len(ins)
            for k, a in enumerate(ins[:nsq]):
                act(sqT.ap[r0:r1, k, 0:n], a(r0, r1), AF.Square, inkeys, sqT.k)
            for k in range(nsq):
                mm(PS[pb][r0:r1, 0:n], ones_lhsT, sqT.ap[r0:r1, k, 0:n], k == 0, k == nsq - 1, sqT.k + ["cstb"], [("ps", pb)])
            act(lnvT.ap[p0:p1, 0:n], PS[pb][p0:p1, 0:n], AF.Ln, [("ps", pb), "epsc"], lnvT.k, scale=1.0 / dim, bias=epsc[p0:p1, 0:1])
            act(rstdT.ap[p0:p1, 0:n], lnvT.ap[p0:p1, 0:n], AF.Exp, lnvT.k, rstdT.k, scale=-0.5)
            for k, a in enumerate(ins):
                stt(outs[k], a(p0, p1), gcols[k], rstdT.ap[p0:p1, 0:n], ALU.mult, ALU.mult, inkeys + rstdT.k + ["gv"], outkeys_)

        def psrows(pb, n):
            return lambda a, b: PS[pb][a:b, 0:n]

        def gcol(l, c, p0=0, p1=128):
            return gv[p0:p1, l, c:c + 1]

        obs = [AR.alloc([288]) for _ in range(2)]
        obn = [0]
        mlayer = AR.top

        for l in range(NLAYERS):
            AR.top = mlayer
            uh = AR.alloc([2, SEG + 2])
            Sst = [[AR.alloc([128]) for _ in range(2)] for _ in range(2)]
            Sbf0 = [AR.alloc([128], BF16) for _ in range(2)]
            mbig = AR.top
            knT = AR.alloc([4, T], BF16); vtok = AR.alloc([16, 512], BF16); krT3 = AR.alloc([T], BF16)
            mset(uh.ap[:, :, 0:2], 0.0, uh.k)
            for pr in range(2):
                mset(Sst[pr][0].ap, 0.0, Sst[pr][0].k)
                mset(Sbf0[pr].ap, 0.0, Sbf0[pr].k)
            spp = [0, 0]
            mseg = AR.top
            for (c0, n) in NBLK:
                AR.top = mseg
                prompt = c0 < T
                hT = AR.alloc([8, n], BF16)
                mixed = AR.alloc([8, n], BF16)
                rmsnorm_fm([(lambda a, b, k=k: xT[a:b, k, c0:c0 + n]) for k in range(8)], [xk(c0)], n, onesb, 1024.0,
                           [gcol(l, G_MIX + k) for k in range(8)], [hT.ap[:, k, :] for k in range(8)], hT.k)

                def proj(col0, m, wsrc=None, kc=8, rhs=None, rk=None):
                    w, wk = wload(D["win"][l, :, :, col0:col0 + m] if wsrc is None else wsrc, 128, kc, m)
                    pb = nps()
                    r_ = hT.ap if rhs is None else rhs
                    for k in range(kc):
                        mm(PS[pb][0:m, 0:n], w[:, k, :], r_[:, k, :], k == 0, k == kc - 1, wk + (hT.k if rk is None else rk), [("ps", pb)])
                    return pb

                mtemp = AR.top
                if prompt and DBG['C']:
                    cc_ = AR.alloc([2, n]); zc = AR.alloc([n]); yy = AR.alloc([n])
                    for ch in range(2):
                        pc = proj(W_CC + ch * 128, 128)
                        act(zc.ap, PS[pc][:, 0:n], AF.Copy, [("ps", pc)], zc.k)
                        ph = proj(W_CH + ch * 128, 128)
                        tt(uh.ap[:, ch, 2:2 + n], zc.ap, PS[ph][:, 0:n], ALU.mult, zc.k + [("ps", ph)], uh.k)
                        ts(yy.ap, uh.ap[:, ch, 2:2 + n], gcol(l, G_CW + ch * 3 + 2), None, ALU.mult, None, uh.k + ["gv"], yy.k)
                        stt(yy.ap, uh.ap[:, ch, 1:1 + n], gcol(l, G_CW + ch * 3 + 1), yy.ap, ALU.mult, ALU.add, uh.k + yy.k + ["gv"], yy.k)
                        stt(yy.ap, uh.ap[:, ch, 0:n], gcol(l, G_CW + ch * 3 + 0), yy.ap, ALU.mult, ALU.add, uh.k + yy.k + ["gv"], yy.k)
                        pbb = proj(W_CB + ch * 128, 128)
                        tt(cc_.ap[:, ch, :], PS[pbb][:, 0:n], yy.ap, ALU.mult, [("ps", pbb)] + yy.k, cc_.k)
                    for ch in range(2):
                        rmsnorm_fm([lambda a, b, ch=ch: cc_.ap[a:b, ch, :]], cc_.k, n, bo64, 64.0,
                                   [gcol(l, G_CON + ch)], [mixed.ap[:, 6 + ch, :]], mixed.k)
                    if c0 + n == T:
                        for ch in range(2):
                            for jj in range(2):
                                store(D["cp"][l, jj:jj + 1, ch * 128:(ch + 1) * 128].rearrange("j p -> p j"), uh.ap[:, ch, n + jj:n + jj + 1], uh.k,
                                      "cp%d_%d_%d" % (l, ch, jj), "uh", slow=True)
                    cp(uh.ap[:, :, 0:2], uh.ap[:, :, n:n + 2], uh.k, uh.k)
                elif (not prompt) and DBG['S'] and 'C' in SMIX:
                    prv = AR.alloc([2, 2, NS]); uS = AR.alloc([2, NS]); cc_ = AR.alloc([2, NS]); zc = AR.alloc([NS]); yy = AR.alloc([NS])
                    for ch in range(2):
                        for jj in range(2):
                            dma(prv.ap[:, ch, jj, :], D["sc"][l, :, jj, ch * 128:(ch + 1) * 128].rearrange("b p -> p b"), [], prv.k, key="prv", slow=True)
                    for ch in range(2):
                        pc = proj(W_CC + ch * 128, 128)
                        act(zc.ap, PS[pc][:, 0:n], AF.Copy, [("ps", pc)], zc.k)
                        ph = proj(W_CH + ch * 128, 128)
                        tt(uS.ap[:, ch, :], zc.ap, PS[ph][:, 0:n], ALU.mult, zc.k + [("ps", ph)], uS.k)
                        ts(yy.ap, uS.ap[:, ch, :], gcol(l, G_CW + ch * 3 + 2), None, ALU.mult, None, uS.k + ["gv"], yy.k)
                        stt(yy.ap, prv.ap[:, ch, 1, :], gcol(l, G_CW + ch * 3 + 1), yy.ap, ALU.mult, ALU.add, prv.k + yy.k + ["gv"], yy.k)
                        stt(yy.ap, prv.ap[:, ch, 0, :], gcol(l, G_CW + ch * 3 + 0), yy.ap, ALU.mult, ALU.add, prv.k + yy.k + ["gv"], yy.k)
                        pbb = proj(W_CB + ch * 128, 128)
                        tt(cc_.ap[:, ch, :], PS[pbb][:, 0:n], yy.ap, ALU.mult, [("ps", pbb)] + yy.k, cc_.k)
                    for ch in range(2):
                        rmsnorm_fm([lambda a, b, ch=ch: cc_.ap[a:b, ch, :]], cc_.k, n, bo64, 64.0,
                                   [gcol(l, G_CON + ch)], [mixed.ap[:, 6 + ch, :]], mixed.k)
                        store(D["cs"][l, :, 0, ch * 128:(ch + 1) * 128].rearrange("b p -> p b"), prv.ap[:, ch, 1, :], prv.k, "cs%d_%d_0" % (l, ch), "prvo", slow=True)
                        store(D["cs"][l, :, 1, ch * 128:(ch + 1) * 128].rearrange("b p -> p b"), uS.ap[:, ch, :], uS.k, "cs%d_%d_1" % (l, ch), "uSo", slow=True)
                else:
                    for ch in range(2):
                        mset(mixed.ap[:, 6 + ch, :], 0.0, mixed.k)
                AR.top = mtemp

                if prompt and DBG['A']:
                    NCH = n // 32
                    for pr in range(2):
                        AR.top = mtemp
                        q = AR.alloc([n]); f = AR.alloc([n]); lg = AR.alloc([n]); kk = AR.alloc([n]); Bc = AR.alloc([n]); tmp = f; ex = lg
                        Bp = AR.alloc([NCH + 1]); El = AR.alloc([NCH])
                        qs = AR.alloc([n], BF16); ksbm = [AR.alloc([n], BF16) for _ in range(2)]; kh = AR.alloc([n], BF16); gs = AR.alloc([n], BF16)
                        vA = AR.alloc([NCH, 128], BF16); khT = AR.alloc([NCH, 128], BF16); Asb = AR.alloc([NCH * 64], BF16)
                        Sbf = AR.alloc([NCH + 1, 128], BF16); oT = q; oN = kk
                        mset(vA.ap, 0.0, vA.k, eng="pool"); mset(khT.ap, 0.0, khT.k, eng="pool"); mset(Asb.ap, 0.0, Asb.k, eng="pool")
                        pq = proj(W_AQ + pr * 128, 128)
                        act(q.ap, PS[pq][:, 0:n], AF.Silu, [("ps", pq)], q.k)
                        pf = proj(W_AF + pr * 128, 128)
                        act(f.ap, PS[pf][:, 0:n], AF.Sigmoid, [("ps", pf)], f.k)
                        ts(f.ap, f.ap, oml[:, pr, l:l + 1], lba[:, pr, l:l + 1], ALU.mult, ALU.add, f.k + ["oml", "lba"], f.k)
                        ts(f.ap, f.ap, 1e-20, None, ALU.max, None, f.k, f.k)
                        act(lg.ap, f.ap, AF.Ln, f.k, lg.k)
                        ts(kk.ap, f.ap, -1.0, 1.0, ALU.mult, ALU.add, f.k, kk.k)
                        pg = proj(W_AG + pr * 128, 128)
                        act(gs.ap, PS[pg][:, 0:n], AF.Silu, [("ps", pg)], gs.k)
                        wv, wvk = wload(D["win"][l, :, :, W_AI + pr * 128:W_AI + pr * 128 + 128], 128, 8, 128)
                        for c4 in range(0, NCH, 4):
                            pv = nps()
                            for ci in range(4):
                                c = c4 + ci
                                for k in range(8):
                                    mm(PS[pv][0:32, ci * 128:(ci + 1) * 128], hT.ap[:, k, c * 32:(c + 1) * 32], wv[:, k, :], k == 0, k == 7,
                                       hT.k + wvk, [("ps", pv)])
                            cp(vA.ap[0:32, c4:c4 + 4, :], PS[pv][0:32, :].rearrange("p (a b) -> p a b", a=4), [("ps", pv)], vA.k)
                        scan_add(Bc.ap, onesf[:, 0:n], lg.ap, lg.k + ["onesf"], Bc.k)
                        mset(Bp.ap[:, 0:1], 0.0, Bp.k)
                        Bv = Bc.ap.rearrange("p (c t) -> p c t", t=32)
                        cp(Bp.ap[:, 1:NCH + 1], Bv[:, :, 31], Bc.k, Bp.k)
                        t3 = tmp.ap.rearrange("p (c t) -> p c t", t=32); e3 = ex.ap.rearrange("p (c t) -> p c t", t=32)
                        Bp3 = Bp.ap.rearrange("p (c o) -> p c o", o=1)
                        bprev = Bp3[:, 0:NCH, :].to_broadcast([128, NCH, 32])
                        bend = Bp3[:, 1:NCH + 1, :].to_broadcast([128, NCH, 32])
                        tt(t3, Bv, bprev, ALU.subtract, Bc.k + Bp.k, tmp.k)
                        act(ex.ap, tmp.ap, AF.Exp, tmp.k, ex.k)
                        tt(qs.ap, q.ap, ex.ap, ALU.mult, q.k + ex.k, qs.k)
                        act(ex.ap, tmp.ap, AF.Exp, tmp.k, ex.k, scale=-1.0)
                        for a in range(2):
                            stt(ksbm[a].ap, kk.ap, cst[:, C_MA + a:C_MA + a + 1], ex.ap, ALU.mult, ALU.mult, kk.k + ex.k + ["cst"], ksbm[a].k)
                        tt(t3, bend, Bv, ALU.subtract, Bc.k + Bp.k, tmp.k)
                        act(ex.ap, tmp.ap, AF.Exp, tmp.k, ex.k)
                        tt(kh.ap, kk.ap, ex.ap, ALU.mult, kk.k + ex.k, kh.k)
                        tt(El.ap, Bp.ap[:, 1:NCH + 1], Bp.ap[:, 0:NCH], ALU.subtract, Bp.k, El.k)
                        act(El.ap, El.ap, AF.Exp, El.k, El.k)
                        for c8 in range(0, NCH, 8):
                            pa = nps()
                            for ci in range(8):
                                c = c8 + ci
                                for a in range(2):
                                    mm(PS[pa][0:32, (ci * 2 + a) * 32:(ci * 2 + a + 1) * 32], ksbm[a].ap[:, c * 32:(c + 1) * 32],
                                       qs.ap[:, c * 32:(c + 1) * 32], True, True, ksbm[a].k + qs.k, [("ps", pa)])
                            tt(Asb.ap[0:32, c8 * 64:(c8 + 8) * 64].rearrange("p (a b) -> p a b", b=32),
                               PS[pa][0:32, :].rearrange("p (a b) -> p a b", b=32), cmf.rearrange("p (o t) -> p o t", o=1).to_broadcast([32, 16, 32]), ALU.mult,
                               [("ps", pa), "cst"], Asb.k)
                        for c8 in range(0, NCH, 8):
                            pt_ = nps()
                            pvb = PS[pt_][:].bitcast(BF16)
                            for ci in range(8):
                                c = c8 + ci
                                tr(pvb[0:32, ci * 128:(ci + 1) * 128], kh.ap[:, c * 32:(c + 1) * 32], identb, kh.k + ["cstb"], [("ps", pt_)])
                            cp(khT.ap[0:32, c8:c8 + 8, :], pvb[0:32, 0:1024].rearrange("p (a b) -> p a b", a=8), [("ps", pt_)], khT.k)
                        cp(Sbf.ap[:, 0, :], Sbf0[pr].ap, Sbf0[pr].k, Sbf.k)
                        for c4 in range(0, NCH, 4):
                            pu = nps()
                            for ci in range(4):
                                c = c4 + ci
                                mm(PS[pu][:, ci * 128:(ci + 1) * 128], khT.ap[:, c, :], vA.ap[:, c, :], True, True, khT.k + vA.k, [("ps", pu)])
                            for ci in range(4):
                                c = c4 + ci
                                so, sn = Sst[pr][spp[pr] % 2], Sst[pr][(spp[pr] + 1) % 2]
                                spp[pr] += 1
                                stt(sn.ap, so.ap, El.ap[:, c:c + 1], PS[pu][:, ci * 128:(ci + 1) * 128], ALU.mult, ALU.add,
                                    so.k + El.k + [("ps", pu)], sn.k)
                                tt(Sbf.ap[:, c + 1, :], sn.ap, bdm, ALU.mult, sn.k + ["cst"], Sbf.k)
                        cp(Sbf0[pr].ap, Sbf.ap[:, NCH, :], Sbf.k, Sbf0[pr].k)
                        po = [nps(), nps()]
                        for c in range(NCH):
                            for a in range(2):
                                mm(PS[po[a]][:, c * 32:(c + 1) * 32], vA.ap[:, c, :], Asb.ap[:, (c * 2 + a) * 32:(c * 2 + a + 1) * 32], True, False,
                                   vA.k + Asb.k, [("ps", po[a])])
                                mm(PS[po[a]][:, c * 32:(c + 1) * 32], Sbf.ap[:, c, :], qs.ap[:, c * 32:(c + 1) * 32], False, True,
                                   Sbf.k + qs.k, [("ps", po[a])])
                        for a in range(2):
                            act(oT.ap[64 * a:64 * a + 64, :], PS[po[a]][64 * a:64 * a + 64, 0:n], AF.Copy, [("ps", po[a])], oT.k)
                        rmsnorm_fm([lambda a, b: oT.ap[a:b, :]], oT.k, n, bo64, 64.0, [gcol(l, G_HON)], [oN.ap], oN.k)
                        tt(mixed.ap[:, pr, :], oN.ap, gs.ap, ALU.mult, oN.k + gs.k, mixed.k)
                        if c0 + n == T:
                            sl_ = Sst[pr][spp[pr] % 2]
                            for a in range(2):
                                store(D["hp"][l, pr, 64 * a:64 * a + 64, :], sl_.ap[64 * a:64 * a + 64, 64 * a:64 * a + 64], sl_.k,
                                      "hp%d_%d_%d" % (l, pr, a), "hp%d" % pr)
                elif (not prompt) and DBG['S'] and 'A' in SMIX:
                    for pr in range(2):
                        AR.top = mtemp
                        q = AR.alloc([NS]); f = AR.alloc([NS]); kk = AR.alloc([NS]); vT = AR.alloc([NS]); oT = AR.alloc([NS]); oN = AR.alloc([NS])
                        gs = AR.alloc([NS], BF16); qb = AR.alloc([NS], BF16)
                        S0 = [AR.alloc([64]) for _ in range(2)]; S1 = [AR.alloc([64]) for _ in range(2)]
                        rhsV = AR.alloc([64]); tV = AR.alloc([64]); Sbd = AR.alloc([128], BF16)
                        pq = proj(W_AQ + pr * 128, 128)
                        act(q.ap, PS[pq][:, 0:n], AF.Silu, [("ps", pq)], q.k)
                        cp(qb.ap, q.ap, q.k, qb.k)
                        pf = proj(W_AF + pr * 128, 128)
                        act(f.ap, PS[pf][:, 0:n], AF.Sigmoid, [("ps", pf)], f.k)
                        ts(f.ap, f.ap, oml[:, pr, l:l + 1], lba[:, pr, l:l + 1], ALU.mult, ALU.add, f.k + ["oml", "lba"], f.k)
                        ts(f.ap, f.ap, 1e-20, None, ALU.max, None, f.k, f.k)
                        ts(kk.ap, f.ap, -1.0, 1.0, ALU.mult, ALU.add, f.k, kk.k)
                        pg = proj(W_AG + pr * 128, 128)
                        act(gs.ap, PS[pg][:, 0:n], AF.Silu, [("ps", pg)], gs.k)
                        pv = proj(W_AI + pr * 128, 128)
                        act(vT.ap, PS[pv][:, 0:n], AF.Copy, [("ps", pv)], vT.k)
                        po1 = nps()
                        for b in range(NS):
                            s0, s1 = S0[b % 2], S1[b % 2]
                            dma(s0.ap, D["sh"][l, b, pr], [], s0.k, key="sh%d" % (b % 2))
                            ts(rhsV.ap, i2f, vT.ap[:, b:b + 1], None, ALU.mult, None, vT.k + ["cst"], rhsV.k)
                            pvb = nps()
                            mm(PS[pvb][:, 0:64], bo64f, rhsV.ap, True, True, rhsV.k + ["cst"], [("ps", pvb)])
                            ts(tV.ap, PS[pvb][:, 0:64], kk.ap[:, b:b + 1], None, ALU.mult, None, [("ps", pvb)] + kk.k, tV.k)
                            stt(s1.ap, s0.ap, f.ap[:, b:b + 1], tV.ap, ALU.mult, ALU.add, s0.k + f.k + tV.k, s1.k)
                            store(D["hs"][l, b, pr], s1.ap, s1.k, "hs%d_%d_%d" % (l, b, pr), "hs%d" % (b % 2))
                            ts(Sbd.ap[:, 0:64], s1.ap, cst[:, C_MA:C_MA + 1], None, ALU.mult, None, s1.k + ["cst"], Sbd.k)
                            ts(Sbd.ap[:, 64:128], s1.ap, cst[:, C_MA + 1:C_MA + 2], None, ALU.mult, None, s1.k + ["cst"], Sbd.k)
                            mm(PS[po1][:, b:b + 1], Sbd.ap, qb.ap[:, b:b + 1], True, True, Sbd.k + qb.k, [("ps", po1)])
                        act(oT.ap, PS[po1][:, 0:NS], AF.Copy, [("ps", po1)], oT.k)
                        rmsnorm_fm([lambda a, b: oT.ap[a:b, :]], oT.k, n, bo64, 64.0, [gcol(l, G_HON)], [oN.ap], oN.k)
                        tt(mixed.ap[:, pr, :], oN.ap, gs.ap, ALU.mult, oN.k + gs.k, mixed.k)
                else:
                    for pr in range(2):
                        mset(mixed.ap[:, pr, :], 0.0, mixed.k)
                AR.top = mtemp

                cqn = AR.alloc([3, n], BF16); ckvn = AR.alloc([2, n]); ckvb = AR.alloc([2, n], BF16)
                krn = AR.alloc([n]); kro = AR.alloc([n]); krt = AR.alloc([n])
                pq3 = [proj(W_BQ + j * 128, 128) for j in range(3)]
                rmsnorm_fm([psrows(pq3[j], n) for j in range(3)], [("ps", b_) for b_ in pq3], n, onesb, 384.0,
                           [gcol(l, G_QN + j) for j in range(3)], [cqn.ap[:, j, :] for j in range(3)], cqn.k)
                pk2 = [proj(W_BKV + j * 128, 128) for j in range(2)]
                rmsnorm_fm([psrows(pk2[j], n) for j in range(2)], [("ps", b_) for b_ in pk2], n, onesb, 256.0,
                           [gcol(l, G_KVN + j) for j in range(2)], [ckvn.ap[:, j, :] for j in range(2)], ckvn.k)
                cp(ckvb.ap, ckvn.ap, ckvn.k, ckvb.k, eng="pool")
                pkr = proj(W_KR3, 128)
                rmsnorm_fm([psrows(pkr, n)], [("ps", pkr)], n, bo32, 32.0, [gcol(l, G_KHR)], [krn.ap], krn.k)
                prr = nps()
                mm(PS[prr][:, 0:n], rotf, krn.ap, True, True, krn.k + ["cst"], [("ps", prr)])
                if prompt:
                    tt(kro.ap, krn.ap, cosT[:, c0:c0 + n], ALU.mult, krn.k + ["tab"], kro.k)
                    tt(krt.ap, PS[prr][:, 0:n], sinT[:, c0:c0 + n], ALU.mult, [("ps", prr), "tab"], krt.k)
                else:
                    ts(kro.ap, krn.ap, cosS[:, 0:1], None, ALU.mult, None, krn.k + ["tab"], kro.k)
                    ts(krt.ap, PS[prr][:, 0:n], sinS[:, 0:1], None, ALU.mult, None, [("ps", prr), "tab"], krt.k)
                tt(kro.ap, kro.ap, krt.ap, ALU.add, kro.k + krt.k, kro.k)
                for t0 in range(0, n, 128):
                    r = min(128, n - t0)
                    po_ = nps()
                    for k in range(2):
                        tr(PS[po_][0:r, k * 128:(k + 1) * 128], ckvn.ap[:, k, t0:t0 + r], identf, ckvn.k + ["cst"], [("ps", po_)])
                    tr(PS[po_][0:r, 256:288], kro.ap[0:32, t0:t0 + r], identf[0:32, 0:32], kro.k + ["cst"], [("ps", po_)])
                    ob = obs[obn[0] % 2]; osem = "ob%d" % (obn[0] % 2); obn[0] += 1
                    cp(ob.ap[0:r, 0:288], PS[po_][0:r, 0:288], [("ps", po_)], ob.k)
                    if prompt:
                        store(D["kvp"][l, c0 + t0:c0 + t0 + r, :], ob.ap[0:r, 0:256], ob.k, "kvp%d_%d" % (l, c0 + t0), osem)
                        store(D["krp"][l, c0 + t0:c0 + t0 + r, :], ob.ap[0:r, 256:288], ob.k, "krp%d_%d" % (l, c0 + t0), osem)
                    else:
                        store(D["kvs"][l, :, :], ob.ap[0:r, 0:256], ob.k, "kvs%d" % l, osem)
                        store(D["krs"][l, :, :], ob.ap[0:r, 256:288], ob.k, "krs%d" % l, osem)
                if prompt and DBG['B']:
                    cp(krT3.ap[:, c0:c0 + n], kro.ap, kro.k, krT3.k)
                    for j in range(4):
                        wk_, wkk = wload(D["wuk"][l, :, :, j * 128:(j + 1) * 128], 128, 2, 128)
                        pb = nps()
                        for cc in range(2):
                            mm(PS[pb][:, 0:n], wk_[:, cc, :], ckvb.ap[:, cc, :], cc == 0, cc == 1, wkk + ckvb.k, [("ps", pb)])
                        rmsnorm_fm([psrows(pb, n)], [("ps", pb)], n, bo64, 64.0, [gcol(l, G_KHN)], [knT.ap[:, j, c0:c0 + n]], knT.k)
                    wv0, wvk0 = wload(D["wuv"][l, :, 0:1, :], 128, 1, 512)
                    wv1, wvk1 = wload(D["wuv"][l, :, 1:2, :], 128, 1, 512)
                    for t4 in range(n // 128):
                        pb = nps()
                        mm(PS[pb][:, :], ckvb.ap[:, 0, t4 * 128:(t4 + 1) * 128], wv0[:, 0, :], True, False, wvk0 + ckvb.k, [("ps", pb)])
                        mm(PS[pb][:, :], ckvb.ap[:, 1, t4 * 128:(t4 + 1) * 128], wv1[:, 0, :], False, True, wvk1 + ckvb.k, [("ps", pb)])
                        act(vtok.ap[:, c0 // 128 + t4, :], PS[pb][:, :], AF.Copy, [("ps", pb)], vtok.k)
                    qnm = AR.alloc([2, 4, n], BF16); qrm = AR.alloc([8, n], BF16); qrn = krn; qra = kro; qrb = krt
                    for j in range(4):
                        wq_, wqk = wload(D["wqn"][l, :, :, j * 128:(j + 1) * 128], 128, 3, 128)
                        pb = nps()
                        for cc in range(3):
                            mm(PS[pb][:, 0:n], wq_[:, cc, :], cqn.ap[:, cc, :], cc == 0, cc == 2, wqk + cqn.k, [("ps", pb)])
                        rmsnorm_fm([psrows(pb, n), psrows(pb, n)], [("ps", pb)], n, bo64, 64.0, [gcol(l, G_QHNM), gcol(l, G_QHNM + 1)],
                                   [qnm.ap[:, 0, j, :], qnm.ap[:, 1, j, :]], qnm.k, dup=True)
                    for j in range(2):
                        wq_, wqk = wload(D["wqr"][l, :, :, j * 128:(j + 1) * 128], 128, 3, 128)
                        pb = nps()
                        for cc in range(3):
                            mm(PS[pb][:, 0:n], wq_[:, cc, :], cqn.ap[:, cc, :], cc == 0, cc == 2, wqk + cqn.k, [("ps", pb)])
                        rmsnorm_fm([psrows(pb, n)], [("ps", pb)], n, bo32, 32.0, [gcol(l, G_QHR)], [qrn.ap], qrn.k)
                        pr2 = nps()
                        mm(PS[pr2][:, 0:n], rotf, qrn.ap, True, True, qrn.k + ["cst"], [("ps", pr2)])
                        tt(qra.ap, qrn.ap, cosT[:, c0:c0 + n], ALU.mult, qrn.k + ["tab"], qra.k)
                        tt(qrb.ap, PS[pr2][:, 0:n], sinT[:, c0:c0 + n], ALU.mult, [("ps", pr2), "tab"], qrb.k)
                        tt(qra.ap, qra.ap, qrb.ap, ALU.add, qra.k + qrb.k, qra.k)
                        for s4 in range(4):
                            ts(qrm.ap[:, j * 4 + s4, :], qra.ap, cst[:, C_MS + s4:C_MS + s4 + 1], None, ALU.mult, None, qra.k + ["cst"], qrm.k)
                    PT = [AR.alloc([n], BF16) for _ in range(3)]
                    rl = krn; oNn = kro; oB = krt
                    nkb = (c0 + n) // 128
                    ptn = 0
                    for h in (range(8) if DBG['ATT'] else []):
                        j, a = h // 2, h % 2
                        pO, pL = 6, 7
                        for kb in range(nkb):
                            qlo = max(0, kb * 128 - c0)
                            N = n - qlo
                            pS = nps()
                            mm(PS[pS][:, 0:N], knT.ap[:, j, kb * 128:(kb + 1) * 128], qnm.ap[:, a, j, qlo:n], True, False,
                               knT.k + qnm.k, [("ps", pS)])
                            mm(PS[pS][:, 0:N], krT3.ap[:, kb * 128:(kb + 1) * 128], qrm.ap[:, h, qlo:n], False, True,
                               krT3.k + qrm.k, [("ps", pS)])
                            pt_b = PT[ptn % 3]; ptn += 1
                            act(pt_b.ap[:, 0:N], PS[pS][:, 0:N], AF.Exp, [("ps", pS)], pt_b.k, scale=SM_SCALE)
                            if kb * 128 >= c0:
                                tt(pt_b.ap[:, 0:128], pt_b.ap[:, 0:128], trib, ALU.mult, pt_b.k + ["cstb"], pt_b.k, eng="pool")
                            mm(PS[pO][:, qlo:n], vtok.ap[:, kb, j * 128:(j + 1) * 128], pt_b.ap[:, 0:N], kb == 0, kb == nkb - 1, vtok.k + pt_b.k, [("ps", pO)])
                            mm(PS[pL][:, qlo:n], onesb, pt_b.ap[:, 0:N], kb == 0, kb == nkb - 1, pt_b.k + ["cstb"], [("ps", pL)])
                        recip(rl.ap, PS[pL][:, 0:n], [("ps", pL)], rl.k)
                        tt(oNn.ap, PS[pO][:, 0:n], rl.ap, ALU.mult, [("ps", pO)] + rl.k, oNn.k)
                        rmsnorm_fm([lambda a_, b_: oNn.ap[a_:b_, :]], oNn.k, n, bo64, 64.0, [gcol(l, G_ON + h)], [oB.ap], oB.k)
                        cp(mixed.ap[64 * a:64 * a + 64, 2 + j, :], oB.ap[64 * a:64 * a + 64, :], oB.k, mixed.k, eng="pool")
                elif (not prompt) and DBG['S'] and 'B' in SMIX:
                    cur = AR.top
                    AR.top = mbig
                    raws = [AR.alloc([8, 256], BF16) for _ in range(2)]
                    krraw = AR.alloc([128, 32], BF16); sqb = AR.alloc([4, 1024], BF16); cT = AR.alloc([2, 1024], BF16); ropeD = AR.alloc([128, 8])
                    assert AR.top <= mseg, (AR.top, mseg)
                    AR.top = cur
                    rtmp = AR.alloc([64, 32], BF16); QrB = AR.alloc([NS, 256], BF16); wukS = AR.alloc([2, 512], BF16); wuvS = AR.alloc([2, 512], BF16)
                    QabsT = AR.alloc([2, NS, 8], BF16); qg32 = AR.alloc([2, 4, NS]); qgm = AR.alloc([2, 4, NS], BF16)
                    qr32 = AR.alloc([2, NS]); qrtok = AR.alloc([256], BF16); sqn = AR.alloc([4, NS], BF16)
                    e64a = AR.alloc([64]); e64b = AR.alloc([64]); e64c = AR.alloc([64]); pT = AR.alloc([64], BF16)
                    pnew = AR.alloc([NS, 8], BF16); cnew = AR.alloc([257], BF16); krtok = AR.alloc([32]); tmpR = AR.alloc([32, 32]); ropeN = AR.alloc([32])
                    n8 = AR.alloc([8]); n32a = AR.alloc([32]); n32b = AR.alloc([32])
                    olat = AR.alloc([257]); olatn = AR.alloc([256]); rl8 = AR.alloc([1]); olT = AR.alloc([2, 8], BF16); oS = AR.alloc([4, NS])
                    for cc in range(2):
                        w_, wk_ = wload(D["wuk"][l, :, cc:cc + 1, :], 128, 1, 512)
                        cp(wukS.ap[:, cc, :], w_[:, 0, :], wk_, wukS.k, eng="pool")
                        w_, wk_ = wload(D["wuv"][l, :, cc:cc + 1, :], 128, 1, 512)
                        cp(wuvS.ap[:, cc, :], w_[:, 0, :], wk_, wuvS.k, eng="pool")
                    for j in range(4):
                        wq_, wqk = wload(D["wqn"][l, :, :, j * 128:(j + 1) * 128], 128, 3, 128)
                        pb = nps()
                        for cc in range(3):
                            mm(PS[pb][:, 0:n], wq_[:, cc, :], cqn.ap[:, cc, :], cc == 0, cc == 2, wqk + cqn.k, [("ps", pb)])
                        rmsnorm_fm([psrows(pb, n), psrows(pb, n)], [("ps", pb)], n, bo64, 64.0, [gcol(l, G_QHNM), gcol(l, G_QHNM + 1)],
                                   [qg32.ap[:, 0, j, :], qg32.ap[:, 1, j, :]], qg32.k, dup=True)
                    ts(qgm.ap, qg32.ap, gcol(l, G_KHN), None, ALU.mult, None, qg32.k + ["gv"], qgm.k)
                    pqa = nps()
                    for j in range(4):
                        wt_, wtk = wload(D["wukT"][l, :, j:j + 1, :], 128, 1, 256)
                        for a in range(2):
                            for cc in range(2):
                                col = (cc * 8 + 2 * j + a) * NS
                                mm(PS[pqa][:, col:col + NS], wt_[:, 0, cc * 128:(cc + 1) * 128], qgm.ap[:, a, j, :], True, True, wtk + qgm.k, [("ps", pqa)])
                    cp(QabsT.ap.rearrange("p c b h -> p c h b"), PS[pqa][:, 0:64].rearrange("p (c h b) -> p c h b", c=2, h=8), [("ps", pqa)], QabsT.k)
                    for j in range(2):
                        wq_, wqk = wload(D["wqr"][l, :, :, j * 128:(j + 1) * 128], 128, 3, 128)
                        pb = nps()
                        for cc in range(3):
                            mm(PS[pb][:, 0:n], wq_[:, cc, :], cqn.ap[:, cc, :], cc == 0, cc == 2, wqk + cqn.k, [("ps", pb)])
                        rmsnorm_fm([psrows(pb, n)], [("ps", pb)], n, bo32, 32.0, [gcol(l, G_QHR)], [krn.ap], krn.k)
                        pr2 = nps()
                        mm(PS[pr2][:, 0:n], rotf, krn.ap, True, True, krn.k + ["cst"], [("ps", pr2)])
                        ts(krt.ap, krn.ap, cosS[:, 0:1], None, ALU.mult, None, krn.k + ["tab"], krt.k)
                        stt(qr32.ap[:, j, :], PS[pr2][:, 0:n], sinS[:, 0:1], krt.ap, ALU.mult, ALU.add, [("ps", pr2), "tab"] + krt.k, qr32.k)
                    mset(qrtok.ap, 0.0, qrtok.k)
                    ptq = nps()
                    for j in range(2):
                        tr(PS[ptq][0:NS, j * 128:(j + 1) * 128], qr32.ap[:, j, :], identf, qr32.k + ["cst"], [("ps", ptq)])
                    cp(qrtok.ap[0:NS, :], PS[ptq][0:NS, 0:256], [("ps", ptq)], qrtok.k)
                    for b in range(NS):
                        pbq = nps()
                        mm(PS[pbq][:, 0:256], cstb[:, C_EB + b * 128:C_EB + (b + 1) * 128], qrtok.ap, True, True, qrtok.k + ["cstb"], [("ps", pbq)])
                        act(QrB.ap[:, b, :], PS[pbq][:, 0:256], AF.Copy, [("ps", pbq)], QrB.k)
                    pk_ = nps()
                    for m in range(4):
                        for cc in range(2):
                            mm(PS[pk_][:, m * NS:(m + 1) * NS], wukS.ap[:, cc, m * 128:(m + 1) * 128], ckvb.ap[:, cc, :], cc == 0, cc == 1, wukS.k + ckvb.k, [("ps", pk_)])
                    act(sqn.ap, PS[pk_][:, 0:4 * NS].rearrange("p (m b) -> p m b", m=4), AF.Square, [("ps", pk_)], sqn.k)
                    pn_ = nps()
                    for m in range(4):
                        mm(PS[pn_][0:NS, 0:8], sqn.ap[:, m, :], indb[:, m * 8:(m + 1) * 8], m == 0, m == 3, sqn.k + ["cstb"], [("ps", pn_)])
                    for cc in range(2):
                        mm(PS[pn_][0:NS, 8:40], ckvb.ap[:, cc, :], QabsT.ap[:, cc, :, :].rearrange("p b h -> p (b h)"), cc == 0, cc == 1, ckvb.k + QabsT.k, [("ps", pn_)])
                    ptk = nps()
                    tr(PS[ptk][0:NS, 0:32], kro.ap[0:32, :], identf[0:32, 0:32], kro.k + ["cst"], [("ps", ptk)])
                    for cc in range(2):
                        tr(PS[ptk][0:NS, 128 + cc * 128:256 + cc * 128], ckvn.ap[:, cc, :], identf, ckvn.k + ["cst"], [("ps", ptk)])
                    cp(krtok.ap[0:NS, :], PS[ptk][0:NS, 0:32], [("ps", ptk)], krtok.k)
                    mset(cnew.ap, 0.0, cnew.k)
                    mset(cnew.ap[0:NS, 256:257], 1.0, cnew.k)
                    cp(cnew.ap[0:NS, 0:256], PS[ptk][0:NS, 128:384], [("ps", ptk)], cnew.k)
                    tt(tmpR.ap[0:NS, :, :], QrB.ap[0:NS, :, :].rearrange("p b (h r) -> p (b h) r", r=32),
                       krtok.ap[0:NS, :].rearrange("p (o r) -> p o r", o=1).to_broadcast([NS, 32, 32]), ALU.mult, QrB.k + krtok.k, tmpR.k)
                    treduce(ropeN.ap[0:NS, :], tmpR.ap[0:NS, :, :], tmpR.k, ropeN.k)
                    act(n8.ap[0:NS, :], PS[pn_][0:NS, 0:8], AF.Ln, [("ps", pn_), "epsc"], n8.k, scale=1.0 / 64.0, bias=epsc[0:NS, 0:1])
                    act(n8.ap[0:NS, :], n8.ap[0:NS, :], AF.Exp, n8.k, n8.k, scale=-0.5)
                    tt(n32a.ap[0:NS, :].rearrange("p (b h) -> p b h", h=8), PS[pn_][0:NS, 8:40].rearrange("p (b h) -> p b h", h=8),
                       n8.ap[0:NS, :].rearrange("p (o h) -> p o h", o=1).to_broadcast([NS, NS, 8]), ALU.mult, [("ps", pn_)] + n8.k, n32a.k)
                    tt(n32a.ap[0:NS, :], n32a.ap[0:NS, :], ropeN.ap[0:NS, :], ALU.add, n32a.k + ropeN.k, n32a.k)
                    act(n32b.ap[0:NS, :], n32a.ap[0:NS, :], AF.Exp, n32a.k, n32b.k, scale=SM_SCALE)
                    mset(pnew.ap, 0.0, pnew.k)
                    tt(pnew.ap[0:NS, :, :].rearrange("p b h -> p (b h)"), n32b.ap[0:NS, :], pm4, ALU.mult, n32b.k + ["cst"], pnew.k)
                    PACC, PLS = 7, 6
                    for b in range(NS):
                        for half in range(2):
                            gather(krraw.ap.rearrange("p t r -> p (t r)")[:, half * 2048:(half + 1) * 2048], D["ckr%d" % l], idx2[:, b, half:half + 1], ["idx2"], krraw.k, "krraw")
                        for h in range(8):
                            for half in range(2):
                                tt(rtmp.ap, krraw.ap[:, half * 64:(half + 1) * 64, :],
                                   QrB.ap[:, b, h * 32:(h + 1) * 32].rearrange("p (o r) -> p o r", o=1).to_broadcast([128, 64, 32]), ALU.mult,
                                   krraw.k + QrB.k, rtmp.k)
                                treduce(ropeD.ap[:, half * 64:(half + 1) * 64, h], rtmp.ap, rtmp.k, ropeD.k)
                        for ch in range(16):
                            raw = raws[ch % 2]
                            gather(raw.ap.rearrange("p t c -> p (t c)"), D["ckv%d" % l], idx16[:, b, ch:ch + 1], ["idx16"], raw.k, "raw%d" % (ch % 2))
                            ptr = [nps(), nps()]
                            for cc in range(2):
                                pvb_ = PS[ptr[cc]][:].bitcast(BF16)
                                for j in range(8):
                                    tr(pvb_[:, j * 128:(j + 1) * 128], raw.ap[:, j, cc * 128:(cc + 1) * 128], identb, raw.k + ["cstb"], [("ps", ptr[cc])])
                                if cc == 0:
                                    cp(cT.ap[:, cc, :], pvb_[:, 0:1024], [("ps", ptr[cc])], cT.k)
                                else:
                                    act(cT.ap[:, cc, :], pvb_[:, 0:1024], AF.Copy, [("ps", ptr[cc])], cT.k)
                            for half in range(2):
                                for m in range(4):
                                    pk2 = nps()
                                    for cc in range(2):
                                        mm(PS[pk2][:, :], wukS.ap[:, cc, m * 128:(m + 1) * 128], cT.ap[:, cc, half * 512:(half + 1) * 512], cc == 0, cc == 1,
                                           wukS.k + cT.k, [("ps", pk2)])
                                    act(sqb.ap[:, m, half * 512:(half + 1) * 512], PS[pk2][:, :], AF.Square, [("ps", pk2)], sqb.k)
                            pss, psd = nps(), nps()
                            for j in range(8):
                                for m in range(4):
                                    mm(PS[pss][:, j * 8:(j + 1) * 8], sqb.ap[:, m, j * 128:(j + 1) * 128], indb[:, m * 8:(m + 1) * 8], m == 0, m == 3,
                                       sqb.k + ["cstb"], [("ps", pss)])
                                for cc in range(2):
                                    mm(PS[psd][:, j * 8:(j + 1) * 8], cT.ap[:, cc, j * 128:(j + 1) * 128], QabsT.ap[:, cc, b, :], cc == 0, cc == 1,
                                       cT.k + QabsT.k, [("ps", psd)])
                            act(e64a.ap, PS[pss][:, 0:64], AF.Ln, [("ps", pss), "epsc"], e64a.k, scale=1.0 / 64.0, bias=epsc[:, 0:1])
                            act(e64a.ap, e64a.ap, AF.Exp, e64a.k, e64a.k, scale=-0.5)
                            tt(e64b.ap, PS[psd][:, 0:64], e64a.ap, ALU.mult, [("ps", psd)] + e64a.k, e64b.k)
                            tt(e64b.ap.rearrange("p (t h) -> p t h", h=8), e64b.ap.rearrange("p (t h) -> p t h", h=8), ropeD.ap[:, ch * 8:(ch + 1) * 8, :], ALU.add,
                               e64b.k + ropeD.k, e64b.k)
                            act(pT.ap, e64b.ap, AF.Exp, e64b.k, pT.k, scale=SM_SCALE)
                            for j in range(8):
                                mm(PS[PACC][0:8, 0:256], pT.ap[:, j * 8:(j + 1) * 8], raw.ap[:, j, :], ch == 0 and j == 0, False, pT.k + raw.k, [("ps", PACC)])
                                mm(PS[PLS][0:8, 0:1], pT.ap[:, j * 8:(j + 1) * 8], onesb[:, 0:1], ch == 0 and j == 0, False, pT.k + ["cstb"], [("ps", PLS)])
                        mm(PS[PACC][0:8, 0:256], pnew.ap[:, b, :], cnew.ap[:, 0:256], False, True, pnew.k + cnew.k, [("ps", PACC)])
                        mm(PS[PLS][0:8, 0:1], pnew.ap[:, b, :], cnew.ap[:, 256:257], False, True, pnew.k + cnew.k, [("ps", PLS)])
                        act(olat.ap[0:8, 0:256], PS[PACC][0:8, 0:256], AF.Copy, [("ps", PACC)], olat.k)
                        recip(rl8.ap[0:8, :], PS[PLS][0:8, 0:1], [("ps", PLS)], rl8.k)
                        ts(olatn.ap[0:8, :], olat.ap[0:8, 0:256], rl8.ap[0:8, 0:1], None, ALU.mult, None, olat.k + rl8.k, olatn.k)
                        pt2 = nps()
                        for cc in range(2):
                            tr(PS[pt2][:, cc * 8:(cc + 1) * 8], olatn.ap[0:8, cc * 128:(cc + 1) * 128], identf[0:8, 0:8], olatn.k + ["cst"], [("ps", pt2)])
                        cp(olT.ap.rearrange("p c h -> p (c h)"), PS[pt2][:, 0:16], [("ps", pt2)], olT.k)
                        pu2 = nps()
                        for h in range(8):
                            for cc in range(2):
                                mm(PS[pu2][:, h:h + 1], wuvS.ap[:, cc, (h // 2) * 128:(h // 2 + 1) * 128], olT.ap[:, cc, h:h + 1], cc == 0, cc == 1,
                                   wuvS.k + olT.k, [("ps", pu2)])
                        for h in range(8):
                            a = h % 2
                            act(oS.ap[64 * a:64 * a + 64, h // 2, b:b + 1], PS[pu2][64 * a:64 * a + 64, h:h + 1], AF.Copy, [("ps", pu2)], oS.k)
                    for j in range(4):
                        rmsnorm_fm([lambda a_, b_, j=j: oS.ap[a_:b_, j, :]], oS.k, n, bo64, 64.0, [gcol(l, G_ONP + j)], [mixed.ap[:, 2 + j, :]], mixed.k)
                else:
                    for j in range(4):
                        mset(mixed.ap[:, 2 + j, :], 0.0, mixed.k)
                AR.top = mtemp

                for dj in (range(8) if DBG['WOUT'] else []):
                    wo_, wok = wload(D["wo"][l, :, :, dj * 128:(dj + 1) * 128], 128, 8, 128)
                    pb = nps()
                    for kc in range(8):
                        mm(PS[pb][:, 0:n], wo_[:, kc, :], mixed.ap[:, kc, :], kc == 0, kc == 7, wok + mixed.k, [("ps", pb)])
                    tt(xT[:, dj, c0:c0 + n], xT[:, dj, c0:c0 + n], PS[pb][:, 0:n], ALU.add, [xk(c0), ("ps", pb)], [xk(c0)])

            AR.top = mlayer
            h2T = AR.alloc([8, NT], BF16); ffT = AR.alloc([6, NT], BF16); wdq = AR.alloc([6, 1024], BF16); sg = [AR.alloc([SEG]) for _ in range(2)]
            for (c0, n) in NBLK:
                rmsnorm_fm([(lambda a, b, k=k: xT[a:b, k, c0:c0 + n]) for k in range(8)], [xk(c0)], n, onesb, 1024.0,
                           [gcol(l, G_FFN + k) for k in range(8)], [h2T.ap[:, k, c0:c0 + n] for k in range(8)], h2T.k)
            fq = [(0, 6), (6, 6), (12, 5), (17, 5)] if DBG['FFN'] else []
            sgn = 0
            for (f0, nf) in fq:
                for fi in range(nf):
                    fch = f0 + fi
                    wg_, wgk = wload(D["wg"][l, :, :, fch * 128:(fch + 1) * 128], 128, 8, 128)
                    wu_, wuk_ = wload(D["wu"][l, :, :, fch * 128:(fch + 1) * 128], 128, 8, 128)
                    for (c0, n) in NBLK:
                        pg_, pu_ = nps(), nps()
                        for k in range(8):
                            mm(PS[pg_][:, 0:n], wg_[:, k, :], h2T.ap[:, k, c0:c0 + n], k == 0, k == 7, wgk + h2T.k, [("ps", pg_)])
                        for k in range(8):
                            mm(PS[pu_][:, 0:n], wu_[:, k, :], h2T.ap[:, k, c0:c0 + n], k == 0, k == 7, wuk_ + h2T.k, [("ps", pu_)])
                        s_ = sg[sgn % 2]; sgn += 1
                        act(s_.ap[:, 0:n], PS[pg_][:, 0:n], AF.Silu, [("ps", pg_)], s_.k)
                        tt(ffT.ap[:, fi, c0:c0 + n], s_.ap[:, 0:n], PS[pu_][:, 0:n], ALU.mult, s_.k + [("ps", pu_)], ffT.k)
                for f2 in range(nf):
                    wd_, wdk = wload(D["wd"][l, :, f0 + f2:f0 + f2 + 1, :], 128, 1, 1024)
                    cp(wdq.ap[:, f2:f2 + 1, :], wd_, wdk, wdq.k, eng="pool")
                for dj in range(8):
                    for (c0, n) in NBLK:
                        pb = nps()
                        for fi in range(nf):
                            mm(PS[pb][:, 0:n], wdq.ap[:, fi, dj * 128:(dj + 1) * 128], ffT.ap[:, fi, c0:c0 + n], fi == 0, fi == nf - 1, wdq.k + ffT.k, [("ps", pb)])
                        tt(xT[:, dj, c0:c0 + n], xT[:, dj, c0:c0 + n], PS[pb][:, 0:n], ALU.add, [xk(c0), ("ps", pb)], [xk(c0)])

        AR.top = mlayer
        yos = [AR.alloc([1024]) for _ in range(2)]
        for tb in range(17):
            rows = 128 if tb < 16 else NS
            yo = yos[tb % 2]
            for half in range(2):
                pb = nps()
                for k in range(4):
                    dc = half * 4 + k
                    tr(PS[pb][0:rows, k * 128:(k + 1) * 128], xT[:, dc, tb * 128:tb * 128 + rows], identf, [xk(tb * 128), "cst"], [("ps", pb)])
                cp(yo.ap[0:rows, half * 512:(half + 1) * 512], PS[pb][0:rows, :], [("ps", pb)], yo.k)
            if tb < 16:
                store(D["yp"][tb * 128:(tb + 1) * 128, :], yo.ap[0:rows, :], yo.k, "yp%d" % tb, "yo%d" % (tb % 2))
            else:
                store(D["ys"], yo.ap[0:rows, :], yo.k, "ys", "yo%d" % (tb % 2))

        if MAXOPS is not None:
            P.ops = P.ops[:MAXOPS]
        else:
            P.add("sp", lambda e: e.nop(), reads=[k for k in outkeys if FINAL_FILTER is None or FINAL_FILTER in k])
        P.emit(st)
        print("ops", len(P.ops), "arena top", AR.top)
    return nc


def _host_consts():
    c = np.zeros((128, NCST), np.float32)
    p = np.arange(128)
    c[:, C_ID:C_ID + 128] = np.eye(128)
    for m in range(128):
        if m % 32 < 16:
            c[m + 16, C_ROT + m] = -1.0
        else:
            c[m - 16, C_ROT + m] = 1.0
    c[:, C_BO64:C_BO64 + 128] = (p[:, None] // 64 == p[None, :] // 64)
    c[:, C_BO32:C_BO32 + 128] = (p[:, None] // 32 == p[None, :] // 32)
    c[:, C_ONE:C_ONE + 128] = 1.0
    c[:, C_BD:C_BD + 128] = (p[:, None] // 64 == p[None, :] // 64)
    c[:, C_I2:C_I2 + 64] = (p[:, None] % 64 == np.arange(64)[None, :])
    for m in range(4):
        for j in range(8):
            c[:, C_IND + m * 8 + j] = (j == 2 * m + p // 64)
    c[0:32, C_CM:C_CM + 32] = (np.arange(32)[:, None] <= np.arange(32)[None, :])
    c[:, C_TRI:C_TRI + 128] = (p[:, None] <= p[None, :])
    c[:, C_MA] = (p < 64)
    c[:, C_MA + 1] = (p >= 64)
    for s4 in range(4):
        c[:, C_MS + s4] = (p // 32 == s4)
    for b in range(4):
        c[b, C_EB + b * 128:C_EB + (b + 1) * 128] = 1.0
    inv = 10000.0 ** (-np.arange(16, dtype=np.float64) / 16.0)
    c[:, C_INVF] = (inv[p % 16] / (2.0 * np.pi)).astype(np.float32)
    for b in range(4):
        c[b, C_PM + b * 8:C_PM + b * 8 + 8] = 1.0
    return c


def _pcol(v, mod=None):
    v = np.asarray(v, np.float32)
    if mod is not None:
        return v[np.arange(128) % mod][:, None]
    return np.ascontiguousarray(v.reshape(-1, 128).T)


_NC_CACHE = {}


def kernel(x_prompt, x_sample, cache_kv_latent, cache_k_rope, state_hgrn, state_conv, page_table,
           norm_mix_g, w_in, hgrn_lb, hgrn_onorm_g, mla_q_norm_g, mla_w_q_up, mla_kv_norm_g,
           mla_w_kv_up, mla_q_head_g, mla_k_head_g, mla_onorm_g, conv_w, conv_onorm_g, w_out,
           norm_ffn_g, w_gate, w_up, w_down):
    f = lambda a: np.asarray(a, np.float32)
    ncores = 8
    if "nc" not in _NC_CACHE:
        _NC_CACHE["nc"] = build_nc()
    nc = _NC_CACHE["nc"]
    w_in = f(w_in)
    win = w_in.reshape(NL, 8, 128, 2464).transpose(0, 2, 1, 3)
    win = np.ascontiguousarray(np.concatenate([win] + [win[..., W_BR:W_BR + 32]] * 4, axis=-1))
    gvt = np.zeros((128, NL, NG), np.float32)
    for l in range(NL):
        gvt[:, l, G_MIX:G_MIX + 8] = _pcol(f(norm_mix_g)[l])
        gvt[:, l, G_FFN:G_FFN + 8] = _pcol(f(norm_ffn_g)[l])
        gvt[:, l, G_QN:G_QN + 3] = _pcol(f(mla_q_norm_g)[l])
        gvt[:, l, G_KVN:G_KVN + 2] = _pcol(f(mla_kv_norm_g)[l])
        gvt[:, l, G_QHN:G_QHN + 1] = _pcol(f(mla_q_head_g)[l][:64], 64)
        gvt[:, l, G_QHR:G_QHR + 1] = _pcol(f(mla_q_head_g)[l][64:], 32)
        gvt[:, l, G_KHN:G_KHN + 1] = _pcol(f(mla_k_head_g)[l][:64], 64)
        gvt[:, l, G_KHR:G_KHR + 1] = _pcol(f(mla_k_head_g)[l][64:], 32)
        gvt[:, l, G_ON:G_ON + 8] = f(mla_onorm_g)[l].reshape(8, 64).T[np.arange(128) % 64]
        gvt[:, l, G_CON:G_CON + 2] = _pcol(f(conv_onorm_g)[l])
        gvt[:, l, G_CW:G_CW + 6] = f(conv_w)[l].reshape(3, 2, 128).transpose(2, 1, 0).reshape(128, 6)
        gvt[:, l, G_HON:G_HON + 1] = _pcol(f(hgrn_onorm_g)[l], 64)
        gvt[:, l, G_ONP:G_ONP + 4] = _pcol(f(mla_onorm_g)[l])
        gvt[0:64, l, G_QHNM] = f(mla_q_head_g)[l][:64]
        gvt[64:128, l, G_QHNM + 1] = f(mla_q_head_g)[l][:64]
    lball = np.ascontiguousarray(f(hgrn_lb).reshape(NL, 2, 128).transpose(2, 1, 0))
    kmaj = lambda w, kc: np.ascontiguousarray(w.reshape(NL, kc, 128, -1).transpose(0, 2, 1, 3))
    qup = f(mla_w_q_up)
    wqn = kmaj(qup[..., :64].reshape(NL, 384, 512), 3)
    wqr = kmaj(np.ascontiguousarray(qup[..., 64:]).reshape(NL, 384, 256), 3)
    kvup = f(mla_w_kv_up)
    wuk = kmaj(kvup[..., :64].reshape(NL, 256, 512), 2)
    wuv = kmaj(kvup[..., 64:].reshape(NL, 256, 512), 2)
    wukT = np.ascontiguousarray(kvup[..., :64].transpose(0, 2, 3, 1).reshape(NL, 4, 128, 256).transpose(0, 2, 1, 3))
    wo = kmaj(f(w_out), 8); wg = kmaj(f(w_gate), 8); wu = kmaj(f(w_up), 8); wd = kmaj(f(w_down), 22)
    cst = _host_consts()
    pt = np.asarray(page_table, np.int32)
    ckv = f(cache_kv_latent); ckr = f(cache_k_rope); shs = f(state_hgrn); scs = f(state_conv)
    npool = ckv.shape[1]
    assert npool == POOLN[0]
    in_maps = []
    for c in range(ncores):
        m = {"ckv%d" % l: ckv[l].reshape(npool * 16, 2048) for l in range(NL)}
        m.update({"ckr%d" % l: ckr[l].reshape(npool * 2, 2048) for l in range(NL)})
        m.update({"sh": np.ascontiguousarray(shs[:, 4 * c:4 * c + 4].reshape(NL, 4, 2, 128, 64)), "sc": np.ascontiguousarray(scs[:, 4 * c:4 * c + 4]),
                  "ptT": np.ascontiguousarray(pt[4 * c:4 * c + 4].T), "wukT": wukT})
        in_maps.append(m)
        in_maps[-1].update({
            "xp": np.ascontiguousarray(f(x_prompt)[c]),
            "xs": np.ascontiguousarray(f(x_sample)[4 * c:4 * c + 4, 0, :]),
            "win": win, "wqn": wqn, "wqr": wqr, "wuk": wuk, "wuv": wuv, "wo": wo, "wg": wg, "wu": wu, "wd": wd,
            "gv": gvt, "lball": lball, "cst": cst,
        })
    res = run_bass_kernel_spmd(nc, in_maps, core_ids=list(range(ncores))).results
    g = lambda k: np.stack([r[k] for r in res])
    y_prompt = g("yp")
    y_sample = g("ys").reshape(32, 1, 1024)
    kvp = g("kvp").transpose(1, 0, 2, 3)
    krp = g("krp").transpose(1, 0, 2, 3)
    hp = g("hp").transpose(1, 0, 2, 3, 4).reshape(NL, 8, 4, 64, 64)
    cpo = g("cp").transpose(1, 0, 2, 3)
    kvs = g("kvs").transpose(1, 0, 2, 3).reshape(NL, 32, 1, 256)
    krs = g("krs").transpose(1, 0, 2, 3).reshape(NL, 32, 1, 32)
    hs = g("hs").transpose(1, 0, 2, 3, 4, 5).reshape(NL, 32, 4, 64, 64)
    cs = g("cs").transpose(1, 0, 2, 3, 4).reshape(NL, 32, 2, 256)
    return tuple(np.ascontiguousarray(a, dtype=np.float32) for a in (y_prompt, y_sample, kvp, krp, hp, cpo, kvs, krs, hs, cs))
```

```python
import numpy as np
from contextlib import ExitStack
import concourse.bass as bass
import concourse.mybir as mybir
from concourse.bass_utils import run_bass_kernel_spmd

F32 = mybir.dt.float32
BF16 = mybir.dt.bfloat16
I32 = mybir.dt.int32
ALU = mybir.AluOpType
AF = mybir.ActivationFunctionType
AX = mybir.AxisListType

NL = 4
T = 2048
NS = 4
SEG = 512
EPS = 1e-6
SM_SCALE = 96 ** -0.5
NPAGE = 128
POOL = 5120
POOLN = [5120]


class _Op:
    __slots__ = ("eng", "fn", "dma", "deps", "sig", "sigval")


class Prog:
    ENGS = ("pe", "act", "dve", "pool", "sp")
    BLK = {"pe": "tensor", "act": "scalar", "dve": "vector", "pool": "gpsimd", "sp": "sync"}

    def __init__(self, nc):
        self.nc = nc
        self.ops = []
        self.last_writer = {}
        self.readers = {}

    def add(self, eng, fn, reads=(), writes=(), dma=None):
        o = _Op()
        o.eng, o.fn, o.dma, o.sig, o.sigval = eng, fn, dma, False, 0
        deps = set()
        for r in reads:
            w = self.last_writer.get(r)
            if w is not None:
                deps.add(w)
        for w in writes:
            lw = self.last_writer.get(w)
            if lw is not None:
                deps.add(lw)
            for rd in self.readers.get(w, ()):
                if rd.eng == eng and rd.dma is None and dma is None:
                    continue
                deps.add(rd)
        if eng == "pe" and dma is None:
            deps = {d for d in deps if not (d.eng == "pe" and d.dma is None)}
        for d in deps:
            d.sig = True
        o.deps = deps
        for r in reads:
            self.readers.setdefault(r, []).append(o)
        for w in writes:
            self.last_writer[w] = o
            self.readers[w] = []
        self.ops.append(o)
        return o

    def emit(self, stack):
        nc = self.nc
        cnt = {e: 0 for e in self.ENGS}
        dcnt = {}
        for o in self.ops:
            if o.dma is not None:
                dcnt[o.dma] = dcnt.get(o.dma, 0) + 16
                o.sigval = dcnt[o.dma]
            elif o.sig:
                cnt[o.eng] += 1
                o.sigval = cnt[o.eng]
        esem = {e: stack.enter_context(nc.semaphore("e_" + e)) for e in self.ENGS}
        dsem = {k: stack.enter_context(nc.semaphore("d_%d" % i)) for i, k in enumerate(dcnt)}
        block = stack.enter_context(nc.Block())
        for eng in self.ENGS:
            ops = [o for o in self.ops if o.eng == eng]

            def body(e, ops=ops, eng=eng):
                waited = {}
                for o in ops:
                    need = {}
                    for d in o.deps:
                        key = ("d", d.dma) if d.dma is not None else ("e", d.eng)
                        if d.sigval > need.get(key, 0):
                            need[key] = d.sigval
                    for key, v in need.items():
                        if waited.get(key, 0) >= v:
                            continue
                        e.wait_ge(dsem[key[1]] if key[0] == "d" else esem[key[1]], v)
                        waited[key] = v
                    ins = o.fn(e)
                    if o.dma is not None:
                        ins.then_inc(dsem[o.dma], 16)
                    elif o.sig:
                        ins.then_inc(esem[o.eng], 1)

            getattr(block, self.BLK[eng])(body)


G_MIX, G_FFN, G_QN, G_KVN, G_QHN, G_QHR, G_KHN, G_KHR, G_ON, G_CON, G_CW, G_HON, G_QHNM, G_ONP = 0, 8, 16, 19, 21, 22, 23, 24, 25, 33, 35, 41, 42, 44
NG = 48
C_ID, C_ROT, C_BO64, C_BO32, C_ONE, C_BD, C_I2, C_IND, C_CM, C_TRI = 0, 128, 256, 384, 512, 640, 768, 832, 864, 896
C_EB = 1024
NCSTB = 1536
C_INVF, C_PM, C_MA, C_MS = 1536, 1537, 1569, 1571
NCST = 1575
NLAYERS = NL
MAXOPS = None
SMIX = 'ABC'
FINAL_FILTER = None
DBG = dict(A=True, B=True, C=True, ATT=True, FFN=True, WOUT=True, S=True)

W_AQ, W_AF, W_AI, W_AG, W_BQ, W_BKV, W_BR, W_CB, W_CC, W_CH, W_KR3 = 0, 256, 512, 768, 1024, 1408, 1664, 1696, 1952, 2208, 2464
WIN_COLS = 2464 + 128


def build_nc():
    nc = bass.Bass("TRN2", target_bir_lowering=False)
    D = {}

    def din(name, shape, dt=F32):
        D[name] = nc.dram_tensor(name, list(shape), dt, kind="ExternalInput").ap()

    def dout(name, shape):
        D[name] = nc.dram_tensor(name, list(shape), F32, kind="ExternalOutput").ap()

    din("xp", [T, 1024]); din("xs", [NS, 1024])
    for l in range(NL):
        din("ckv%d" % l, [POOLN[0] * 16, 2048]); din("ckr%d" % l, [POOLN[0] * 2, 2048])
    din("sh", [NL, NS, 2, 128, 64]); din("sc", [NL, NS, 2, 256]); din("ptT", [128, NS], I32); din("wukT", [NL, 128, 4, 256])
    din("win", [NL, 128, 8, WIN_COLS]); din("wqn", [NL, 128, 3, 512]); din("wqr", [NL, 128, 3, 256])
    din("wuk", [NL, 128, 2, 512]); din("wuv", [NL, 128, 2, 512]); din("wo", [NL, 128, 8, 1024])
    din("wg", [NL, 128, 8, 2816]); din("wu", [NL, 128, 8, 2816]); din("wd", [NL, 128, 22, 1024])
    din("gv", [128, NL, NG]); din("lball", [128, 2, NL]); din("cst", [128, NCST])
    dout("yp", [T, 1024]); dout("ys", [NS, 1024]); dout("kvp", [NL, T, 256]); dout("krp", [NL, T, 32])
    dout("hp", [NL, 2, 128, 64]); dout("cp", [NL, 2, 256]); dout("kvs", [NL, NS, 256]); dout("krs", [NL, NS, 32])
    dout("hs", [NL, NS, 2, 128, 64]); dout("cs", [NL, NS, 2, 256])

    st = ExitStack()
    with st:
        P = Prog(nc)
        sbn = [0]

        def sb(shape, dt=F32, stack=st):
            sbn[0] += 1
            return stack.enter_context(nc.sbuf_tensor("t%d" % sbn[0], list(shape), dt))

        PS = [st.enter_context(nc.psum_tensor("ps%d" % i, [128, 512], F32)) for i in range(8)]
        psn = [0]

        def nps(lo=0, hi=6):
            i = lo + psn[0] % (hi - lo)
            psn[0] += 1
            return i

        outkeys = []
        dman = [0]

        def mm(out, lhsT, rhs, start, stop, rd, wr):
            P.add("pe", lambda e: e.matmul(out, lhsT=lhsT, rhs=rhs, start=start, stop=stop), reads=rd, writes=wr)

        def tr(out, in_, ident, rd, wr):
            P.add("pe", lambda e: e.transpose(out=out, in_=in_, identity=ident), reads=rd, writes=wr)

        def act(out, in_, func, rd, wr, scale=None, bias=None):
            kw = {}
            if scale is not None:
                kw["scale"] = scale
            if bias is not None:
                kw["bias"] = bias
            P.add("act", lambda e: e.activation(out=out, in_=in_, func=func, **kw), reads=rd, writes=wr)

        def tt(out, in0, in1, op, rd, wr, eng="dve"):
            P.add(eng, lambda e: e.tensor_tensor(out=out, in0=in0, in1=in1, op=op), reads=rd, writes=wr)

        def ts(out, in0, s1, s2, op0, op1, rd, wr, eng="dve"):
            if op1 is None:
                P.add(eng, lambda e: e.tensor_scalar(out=out, in0=in0, scalar1=s1, scalar2=None, op0=op0), reads=rd, writes=wr)
            else:
                P.add(eng, lambda e: e.tensor_scalar(out=out, in0=in0, scalar1=s1, scalar2=s2, op0=op0, op1=op1), reads=rd, writes=wr)

        def stt(out, in0, scalar, in1, op0, op1, rd, wr):
            P.add("dve", lambda e: e.scalar_tensor_tensor(out=out, in0=in0, scalar=scalar, in1=in1, op0=op0, op1=op1), reads=rd, writes=wr)

        def cp(out, in_, rd, wr, eng="dve"):
            P.add(eng, lambda e: e.tensor_copy(out=out, in_=in_), reads=rd, writes=wr)

        def mset(ap, v, wr, eng="dve"):
            P.add(eng, lambda e: e.memset(ap, v), writes=wr)

        def dma(out, in_, rd, wr, key=None, eng="sp", slow=False):
            if key is None:
                dman[0] += 1
                key = "dma%d" % dman[0]
            if slow:
                P.add(eng, lambda e: e.dma_start(out=out, in_=in_, allow_slow_non_contiguous=True), reads=rd, writes=wr, dma=key)
            else:
                P.add(eng, lambda e: e.dma_start(out=out, in_=in_), reads=rd, writes=wr, dma=key)

        def gather(out, table, idx_ap, rd, wr, key):
            P.add("pool", lambda e: e.indirect_dma_start(out=out, out_offset=None, in_=table,
                                                         in_offset=bass.IndirectOffsetOnAxis(ap=idx_ap, axis=0)), reads=rd, writes=wr, dma=key)

        def treduce(out, in_, rd, wr):
            P.add("dve", lambda e: e.tensor_reduce(out=out, in_=in_, axis=AX.X, op=ALU.add), reads=rd, writes=wr)

        def store(out, in_, rd, key, sem, slow=False):
            ok = "O_" + key
            outkeys.append(ok)
            if slow:
                P.add("act", lambda e: e.dma_start(out=out, in_=in_, allow_slow_non_contiguous=True), reads=rd, writes=[ok], dma="S_" + sem)
            else:
                P.add("act", lambda e: e.dma_start(out=out, in_=in_), reads=rd, writes=[ok], dma="S_" + sem)

        def scan_add(out, ones_ap, in_, rd, wr):
            P.add("dve", lambda e: e.tensor_tensor_scan(out=out, data0=ones_ap, data1=in_, initial=0.0, op0=ALU.mult, op1=ALU.add), reads=rd, writes=wr)

        def recip(out, in_, rd, wr):
            P.add("dve", lambda e: e.reciprocal(out=out, in_=in_), reads=rd, writes=wr)

        def xk(c0):
            return "xT%d" % min(c0 // SEG, 4)

        NT = T + NS
        xT = sb([128, 8, NT])
        cst = sb([128, NCST])
        cstb = sb([128, NCSTB], BF16)
        gv = sb([128, NL, NG])
        lbt = sb([128, 2, NL]); lbe = sb([128, 2, NL]); lba = sb([128, 2, NL]); oml = sb([128, 2, NL]); lbs = sb([128, 2])
        cosT = sb([128, T], BF16); sinT = sb([128, T], BF16); cosS = sb([128, 1]); sinS = sb([128, 1])
        onesf = sb([128, SEG]); epsc = sb([128, 1])
        UW = 31360
        U = sb([128, UW])

        class Buf:
            pass

        class Arena:
            def __init__(self):
                self.top = 0

            def alloc(self, shape, dt=F32):
                n = int(np.prod(shape))
                words = n if dt in (F32, I32) else (n + 1) // 2
                words = (words + 63) // 64 * 64
                off = self.top
                self.top += words
                self.peak = max(getattr(self, "peak", 0), self.top)
                assert self.top <= UW, ("arena overflow", self.top)
                ap = U[:, off:off + words]
                if dt != F32:
                    ap = ap.bitcast(dt)
                ap = ap[:, 0:n]
                if len(shape) == 2:
                    ap = ap.rearrange("p (a b) -> p a b", a=shape[0])
                elif len(shape) == 3:
                    ap = ap.rearrange("p (a b c) -> p a b c", a=shape[0], b=shape[1])
                b = Buf()
                b.ap = ap
                b.k = [("U", g) for g in range(off // 64, (off + words) // 64)]
                return b

        AR = Arena()

        dma(cst[:], D["cst"], [], ["cst"]); dma(gv[:], D["gv"], [], ["gv"]); dma(lbt[:], D["lball"], [], ["lbt"])
        cp(cstb[:], cst[:, 0:NCSTB], ["cst"], ["cstb"])
        mset(onesf[:], 1.0, ["onesf"]); mset(epsc[:], EPS, ["epsc"])
        identf = cst[:, C_ID:C_ID + 128]; identb = cstb[:, C_ID:C_ID + 128]; rotf = cst[:, C_ROT:C_ROT + 128]
        bo64 = cstb[:, C_BO64:C_BO64 + 128]; bo32 = cstb[:, C_BO32:C_BO32 + 128]; onesb = cstb[:, C_ONE:C_ONE + 128]
        bdm = cst[:, C_BD:C_BD + 128]; i2f = cst[:, C_I2:C_I2 + 64]; indb = cstb[:, C_IND:C_IND + 32]; cmf = cst[0:32, C_CM:C_CM + 32]
        invf = cst[:, C_INVF:C_INVF + 1]; pm4 = cst[0:4, C_PM:C_PM + 32]; trib = cstb[:, C_TRI:C_TRI + 128]
        bo64f = cst[:, C_BO64:C_BO64 + 128]

        ptT = sb([128, NS], I32); ptf = sb([128, NS]); io16 = sb([128, 16], I32); io16f = sb([128, 16])
        idxf = sb([128, NS, 16]); idx16 = sb([128, NS, 16], I32); idx2f = sb([128, NS, 2]); idx2 = sb([128, NS, 2], I32)
        dma(ptT[:], D["ptT"], [], ["ptT"])
        P.add("pool", lambda e: e.iota(io16[:], pattern=[[1, 16]], base=0, channel_multiplier=0), writes=["io16"])
        cp(io16f[:], io16[:], ["io16"], ["io16f"]); cp(ptf[:], ptT[:], ["ptT"], ["ptf"])
        for b in range(NS):
            ts(idxf[:, b, :], io16f[:], 0.0, ptf[:, b:b + 1], ALU.mult, ALU.add, ["io16f", "ptf"], ["idxf"])
            stt(idxf[:, b, :], idxf[:, b, :], 16.0, io16f[:], ALU.mult, ALU.add, ["idxf", "io16f"], ["idxf"])
            ts(idx2f[:, b, :], io16f[:, 0:2], 0.0, ptf[:, b:b + 1], ALU.mult, ALU.add, ["io16f", "ptf"], ["idx2f"])
            stt(idx2f[:, b, :], idx2f[:, b, :], 2.0, io16f[:, 0:2], ALU.mult, ALU.add, ["idx2f", "io16f"], ["idx2f"])
        cp(idx16[:], idxf[:], ["idxf"], ["idx16"]); cp(idx2[:], idx2f[:], ["idx2f"], ["idx2"])

        act(lbe[:], lbt[:], AF.Exp, ["lbt"], ["lbe"])
        P.add("dve", lambda e: e.tensor_reduce(out=lbs[:], in_=lbe[:], axis=AX.X, op=ALU.add), reads=["lbe"], writes=["lbs"])
        P.add("dve", lambda e: e.reciprocal(out=lbs[:], in_=lbs[:]), reads=["lbs"], writes=["lbs"])
        for pr in range(2):
            ts(lbe[:, pr, :], lbe[:, pr, :], lbs[:, pr:pr + 1], None, ALU.mult, None, ["lbe", "lbs"], ["lbe"])
        mset(lba[:, :, 0:1], 0.0, ["lba"])
        for l in range(1, NL):
            tt(lba[:, :, l:l + 1], lba[:, :, l - 1:l], lbe[:, :, l:l + 1], ALU.add, ["lba", "lbe"], ["lba"])
        ts(oml[:], lba[:], -1.0, 1.0, ALU.mult, ALU.add, ["lba"], ["oml"])

        m0 = AR.top
        rt_n = AR.alloc([T]); rt_i = AR.alloc([T], I32); rt_q = AR.alloc([T]); posP = AR.alloc([T]); posS = AR.alloc([1])

        def ropetab(dst, pos, n, off):
            tn = rt_n.ap[:, 0:n]; ti = rt_i.ap[:, 0:n]; tq = rt_q.ap[:, 0:n]
            kn, ki, kq, kp = rt_n.k, rt_i.k, rt_q.k, posP.k + posS.k
            ts(tn, pos, invf, off, ALU.mult, ALU.add, kp + ["cst"], kn)
            cp(ti, tn, kn, ki)
            cp(tq, ti, ki, kq)
            tt(tn, tn, tq, ALU.subtract, kn + kq, kn)
            ts(tq, tn, 0.5, None, ALU.is_gt, None, kn, kq)
            tt(tn, tn, tq, ALU.subtract, kn + kq, kn)
            ts(tq, tn, -0.5, None, ALU.is_lt, None, kn, kq)
            tt(tn, tn, tq, ALU.add, kn + kq, kn)
            act(dst, tn, AF.Sin, kn, ["tab"], scale=-2.0 * np.pi)

        P.add("pool", lambda e: e.iota(rt_i.ap, pattern=[[1, T]], base=0, channel_multiplier=0), writes=rt_i.k)
        cp(posP.ap, rt_i.ap, rt_i.k, posP.k)
        mset(posS.ap, float(NPAGE * 128), posS.k)
        ropetab(sinT[:], posP.ap, T, 0.5); ropetab(cosT[:], posP.ap, T, 0.75)
        ropetab(sinS[:], posS.ap, 1, 0.5); ropetab(cosS[:], posS.ap, 1, 0.75)
        AR.top = m0

        NBLK = [(i * SEG, SEG) for i in range(T // SEG)] + [(T, NS)]
        xin = [AR.alloc([1024]) for _ in range(2)]
        for tb in range(17):
            rows = 128 if tb < 16 else NS
            src = D["xp"][tb * 128:(tb + 1) * 128, :] if tb < 16 else D["xs"]
            xi = xin[tb % 2]
            dma(xi.ap[0:rows, :], src, [], xi.k, key="xin%d" % (tb % 2))
            for half in range(2):
                pb = nps()
                for k in range(4):
                    dc = half * 4 + k
                    tr(PS[pb][:, k * 128:k * 128 + rows], xi.ap[0:rows, dc * 128:(dc + 1) * 128], identf[0:rows, 0:rows],
                       xi.k + ["cst"], [("ps", pb)])
                cp(xT[:, half * 4:half * 4 + 4, tb * 128:tb * 128 + rows],
                   PS[pb][:].rearrange("p (k t) -> p k t", k=4)[:, :, 0:rows], [("ps", pb)], [xk(tb * 128)])
        AR.top = m0

        wstg = [AR.alloc([1024]) for _ in range(3)]
        wbf = [AR.alloc([1024], BF16) for _ in range(3)]
        wn = [0]

        castn = [0]

        def wcast(out, in_, rd, wr):
            e = ("act", "dve", "act", "dve", "pool")[castn[0] % 5]
            castn[0] += 1
            if e == "act":
                act(out, in_, AF.Copy, rd, wr)
            else:
                cp(out, in_, rd, wr, eng=e)

        def wload(src, kp, kc, ncols):
            i = wn[0] % len(wstg)
            i2 = wn[0] % len(wbf)
            wn[0] += 1
            n = kc * ncols
            assert n <= 1024, n
            sv = wstg[i].ap[0:kp, 0:n].rearrange("p (a b) -> p a b", a=kc)
            bv = wbf[i2].ap[0:kp, 0:n].rearrange("p (a b) -> p a b", a=kc)
            dma(sv, src, [], wstg[i].k, key="wst%d" % i)
            wcast(bv, sv, wstg[i].k, wbf[i2].k)
            return bv, wbf[i2].k

        sqT3 = AR.alloc([3, SEG], BF16); lnvT = AR.alloc([SEG]); rstdT = AR.alloc([SEG])

        def rmsnorm_fm(ins, inkeys, n, ones_lhsT, dim, gcols, outs, outkeys_, p0=0, p1=128, kfull=True, dup=False, sq=None):
            sqT = sqT3 if sq is None else sq
            pb = nps()
            r0, r1 = (0, 128) if kfull else (p0, p1)
            nsq = 1 if dup else len(ins)
            for k, a in enumerate(ins[:nsq]):
                act(sqT.ap[r0:r1, k, 0:n], a(r0, r1), AF.Square, inkeys, sqT.k)
            for k in range(nsq):
                mm(PS[pb][r0:r1, 0:n], ones_lhsT, sqT.ap[r0:r1, k, 0:n], k == 0, k == nsq - 1, sqT.k + ["cstb"], [("ps", pb)])
            act(lnvT.ap[p0:p1, 0:n], PS[pb][p0:p1, 0:n], AF.Ln, [("ps", pb), "epsc"], lnvT.k, scale=1.0 / dim, bias=epsc[p0:p1, 0:1])
            act(rstdT.ap[p0:p1, 0:n], lnvT.ap[p0:p1, 0:n], AF.Exp, lnvT.k, rstdT.k, scale=-0.5)
            for k, a in enumerate(ins):
                stt(outs[k], a(p0, p1), gcols[k], rstdT.ap[p0:p1, 0:n], ALU.mult, ALU.mult, inkeys + rstdT.k + ["gv"], outkeys_)

        def psrows(pb, n):
            return lambda a, b: PS[pb][a:b, 0:n]

        def gcol(l, c, p0=0, p1=128):
            return gv[p0:p1, l, c:c + 1]

        obs = [AR.alloc([288]) for _ in range(2)]
        obn = [0]
        mlayer = AR.top

        for l in range(NLAYERS):
            AR.top = mlayer
            uh = AR.alloc([2, SEG + 2])
            Sst = [[AR.alloc([128]) for _ in range(2)] for _ in range(2)]
            Sbf0 = [AR.alloc([128], BF16) for _ in range(2)]
            mbig = AR.top
            knT = AR.alloc([4, T], BF16); vtok = AR.alloc([16, 512], BF16); krT3 = AR.alloc([T], BF16)
            mset(uh.ap[:, :, 0:2], 0.0, uh.k)
            for pr in range(2):
                mset(Sst[pr][0].ap, 0.0, Sst[pr][0].k)
                mset(Sbf0[pr].ap, 0.0, Sbf0[pr].k)
            spp = [0, 0]
            mseg = AR.top
            for (c0, n) in NBLK:
                AR.top = mseg
                prompt = c0 < T
                hT = AR.alloc([8, n], BF16)
                mixed = AR.alloc([8, n], BF16)
                rmsnorm_fm([(lambda a, b, k=k: xT[a:b, k, c0:c0 + n]) for k in range(8)], [xk(c0)], n, onesb, 1024.0,
                           [gcol(l, G_MIX + k) for k in range(8)], [hT.ap[:, k, :] for k in range(8)], hT.k, sq=mixed)

                def proj(col0, m, wsrc=None, kc=8, rhs=None, rk=None):
                    w, wk = wload(D["win"][l, :, :, col0:col0 + m] if wsrc is None else wsrc, 128, kc, m)
                    pb = nps()
                    r_ = hT.ap if rhs is None else rhs
                    for k in range(kc):
                        mm(PS[pb][0:m, 0:n], w[:, k, :], r_[:, k, :], k == 0, k == kc - 1, wk + (hT.k if rk is None else rk), [("ps", pb)])
                    return pb

                mtemp = AR.top
                if prompt and DBG['C']:
                    cc_ = AR.alloc([2, n]); zc = AR.alloc([n]); yy = AR.alloc([n])
                    for ch in range(2):
                        pc = proj(W_CC + ch * 128, 128)
                        act(zc.ap, PS[pc][:, 0:n], AF.Copy, [("ps", pc)], zc.k)
                        ph = proj(W_CH + ch * 128, 128)
                        tt(uh.ap[:, ch, 2:2 + n], zc.ap, PS[ph][:, 0:n], ALU.mult, zc.k + [("ps", ph)], uh.k)
                        ts(yy.ap, uh.ap[:, ch, 2:2 + n], gcol(l, G_CW + ch * 3 + 2), None, ALU.mult, None, uh.k + ["gv"], yy.k)
                        stt(yy.ap, uh.ap[:, ch, 1:1 + n], gcol(l, G_CW + ch * 3 + 1), yy.ap, ALU.mult, ALU.add, uh.k + yy.k + ["gv"], yy.k)
                        stt(yy.ap, uh.ap[:, ch, 0:n], gcol(l, G_CW + ch * 3 + 0), yy.ap, ALU.mult, ALU.add, uh.k + yy.k + ["gv"], yy.k)
                        pbb = proj(W_CB + ch * 128, 128)
                        tt(cc_.ap[:, ch, :], PS[pbb][:, 0:n], yy.ap, ALU.mult, [("ps", pbb)] + yy.k, cc_.k)
                    for ch in range(2):
                        rmsnorm_fm([lambda a, b, ch=ch: cc_.ap[a:b, ch, :]], cc_.k, n, bo64, 64.0,
                                   [gcol(l, G_CON + ch)], [mixed.ap[:, 6 + ch, :]], mixed.k)
                    if c0 + n == T:
                        for ch in range(2):
                            for jj in range(2):
                                store(D["cp"][l, jj:jj + 1, ch * 128:(ch + 1) * 128].rearrange("j p -> p j"), uh.ap[:, ch, n + jj:n + jj + 1], uh.k,
                                      "cp%d_%d_%d" % (l, ch, jj), "uh", slow=True)
                    cp(uh.ap[:, :, 0:2], uh.ap[:, :, n:n + 2], uh.k, uh.k)
                elif (not prompt) and DBG['S'] and 'C' in SMIX:
                    prv = AR.alloc([2, 2, NS]); uS = AR.alloc([2, NS]); cc_ = AR.alloc([2, NS]); zc = AR.alloc([NS]); yy = AR.alloc([NS])
                    for ch in range(2):
                        for jj in range(2):
                            dma(prv.ap[:, ch, jj, :], D["sc"][l, :, jj, ch * 128:(ch + 1) * 128].rearrange("b p -> p b"), [], prv.k, key="prv", slow=True)
                    for ch in range(2):
                        pc = proj(W_CC + ch * 128, 128)
                        act(zc.ap, PS[pc][:, 0:n], AF.Copy, [("ps", pc)], zc.k)
                        ph = proj(W_CH + ch * 128, 128)
                        tt(uS.ap[:, ch, :], zc.ap, PS[ph][:, 0:n], ALU.mult, zc.k + [("ps", ph)], uS.k)
                        ts(yy.ap, uS.ap[:, ch, :], gcol(l, G_CW + ch * 3 + 2), None, ALU.mult, None, uS.k + ["gv"], yy.k)
                        stt(yy.ap, prv.ap[:, ch, 1, :], gcol(l, G_CW + ch * 3 + 1), yy.ap, ALU.mult, ALU.add, prv.k + yy.k + ["gv"], yy.k)
                        stt(yy.ap, prv.ap[:, ch, 0, :], gcol(l, G_CW + ch * 3 + 0), yy.ap, ALU.mult, ALU.add, prv.k + yy.k + ["gv"], yy.k)
                        pbb = proj(W_CB + ch * 128, 128)
                        tt(cc_.ap[:, ch, :], PS[pbb][:, 0:n], yy.ap, ALU.mult, [("ps", pbb)] + yy.k, cc_.k)
                    for ch in range(2):
                        rmsnorm_fm([lambda a, b, ch=ch: cc_.ap[a:b, ch, :]], cc_.k, n, bo64, 64.0,
                                   [gcol(l, G_CON + ch)], [mixed.ap[:, 6 + ch, :]], mixed.k)
                        store(D["cs"][l, :, 0, ch * 128:(ch + 1) * 128].rearrange("b p -> p b"), prv.ap[:, ch, 1, :], prv.k, "cs%d_%d_0" % (l, ch), "prvo", slow=True)
                        store(D["cs"][l, :, 1, ch * 128:(ch + 1) * 128].rearrange("b p -> p b"), uS.ap[:, ch, :], uS.k, "cs%d_%d_1" % (l, ch), "uSo", slow=True)
                else:
                    for ch in range(2):
                        mset(mixed.ap[:, 6 + ch, :], 0.0, mixed.k)
                AR.top = mtemp

                if prompt and DBG['A']:
                    NCH = n // 32
                    for pr in range(2):
                        AR.top = mtemp
                        q = AR.alloc([n]); f = AR.alloc([n]); lg = AR.alloc([n]); kk = AR.alloc([n]); Bc = AR.alloc([n]); tmp = f; ex = lg
                        Bp = AR.alloc([NCH + 1]); El = AR.alloc([NCH])
                        qs = AR.alloc([n], BF16); ksbm = [AR.alloc([n], BF16) for _ in range(2)]; kh = AR.alloc([n], BF16); gs = AR.alloc([n], BF16)
                        vA = AR.alloc([NCH, 128], BF16); khT = AR.alloc([NCH, 128], BF16); Asb = AR.alloc([NCH * 64], BF16)
                        Sbf = AR.alloc([NCH + 1, 128], BF16); oT = q; oN = kk
                        mset(vA.ap, 0.0, vA.k, eng="pool"); mset(khT.ap, 0.0, khT.k, eng="pool"); mset(Asb.ap, 0.0, Asb.k, eng="pool")
                        pq = proj(W_AQ + pr * 128, 128)
                        act(q.ap, PS[pq][:, 0:n], AF.Silu, [("ps", pq)], q.k)
                        pf = proj(W_AF + pr * 128, 128)
                        act(f.ap, PS[pf][:, 0:n], AF.Sigmoid, [("ps", pf)], f.k)
                        ts(f.ap, f.ap, oml[:, pr, l:l + 1], lba[:, pr, l:l + 1], ALU.mult, ALU.add, f.k + ["oml", "lba"], f.k)
                        ts(f.ap, f.ap, 1e-20, None, ALU.max, None, f.k, f.k)
                        act(lg.ap, f.ap, AF.Ln, f.k, lg.k)
                        ts(kk.ap, f.ap, -1.0, 1.0, ALU.mult, ALU.add, f.k, kk.k)
                        pg = proj(W_AG + pr * 128, 128)
                        act(gs.ap, PS[pg][:, 0:n], AF.Silu, [("ps", pg)], gs.k)
                        wv, wvk = wload(D["win"][l, :, :, W_AI + pr * 128:W_AI + pr * 128 + 128], 128, 8, 128)
                        for c4 in range(0, NCH, 4):
                            pv = nps()
                            for ci in range(4):
                                c = c4 + ci
                                for k in range(8):
                                    mm(PS[pv][0:32, ci * 128:(ci + 1) * 128], hT.ap[:, k, c * 32:(c + 1) * 32], wv[:, k, :], k == 0, k == 7,
                                       hT.k + wvk, [("ps", pv)])
                            cp(vA.ap[0:32, c4:c4 + 4, :], PS[pv][0:32, :].rearrange("p (a b) -> p a b", a=4), [("ps", pv)], vA.k)
                        scan_add(Bc.ap, onesf[:, 0:n], lg.ap, lg.k + ["onesf"], Bc.k)
                        mset(Bp.ap[:, 0:1], 0.0, Bp.k)
                        Bv = Bc.ap.rearrange("p (c t) -> p c t", t=32)
                        cp(Bp.ap[:, 1:NCH + 1], Bv[:, :, 31], Bc.k, Bp.k)
                        t3 = tmp.ap.rearrange("p (c t) -> p c t", t=32); e3 = ex.ap.rearrange("p (c t) -> p c t", t=32)
                        Bp3 = Bp.ap.rearrange("p (c o) -> p c o", o=1)
                        bprev = Bp3[:, 0:NCH, :].to_broadcast([128, NCH, 32])
                        bend = Bp3[:, 1:NCH + 1, :].to_broadcast([128, NCH, 32])
                        tt(t3, Bv, bprev, ALU.subtract, Bc.k + Bp.k, tmp.k)
                        act(ex.ap, tmp.ap, AF.Exp, tmp.k, ex.k)
                        tt(qs.ap, q.ap, ex.ap, ALU.mult, q.k + ex.k, qs.k)
                        act(ex.ap, tmp.ap, AF.Exp, tmp.k, ex.k, scale=-1.0)
                        for a in range(2):
                            stt(ksbm[a].ap, kk.ap, cst[:, C_MA + a:C_MA + a + 1], ex.ap, ALU.mult, ALU.mult, kk.k + ex.k + ["cst"], ksbm[a].k)
                        tt(t3, bend, Bv, ALU.subtract, Bc.k + Bp.k, tmp.k)
                        act(ex.ap, tmp.ap, AF.Exp, tmp.k, ex.k)
                        tt(kh.ap, kk.ap, ex.ap, ALU.mult, kk.k + ex.k, kh.k)
                        tt(El.ap, Bp.ap[:, 1:NCH + 1], Bp.ap[:, 0:NCH], ALU.subtract, Bp.k, El.k)
                        act(El.ap, El.ap, AF.Exp, El.k, El.k)
                        for c8 in range(0, NCH, 8):
                            pa = nps()
                            for ci in range(8):
                                c = c8 + ci
                                for a in range(2):
                                    mm(PS[pa][0:32, (ci * 2 + a) * 32:(ci * 2 + a + 1) * 32], ksbm[a].ap[:, c * 32:(c + 1) * 32],
                                       qs.ap[:, c * 32:(c + 1) * 32], True, True, ksbm[a].k + qs.k, [("ps", pa)])
                            tt(Asb.ap[0:32, c8 * 64:(c8 + 8) * 64].rearrange("p (a b) -> p a b", b=32),
                               PS[pa][0:32, :].rearrange("p (a b) -> p a b", b=32), cmf.rearrange("p (o t) -> p o t", o=1).to_broadcast([32, 16, 32]), ALU.mult,
                               [("ps", pa), "cst"], Asb.k)
                        for c8 in range(0, NCH, 8):
                            pt_ = nps()
                            pvb = PS[pt_][:].bitcast(BF16)
                            for ci in range(8):
                                c = c8 + ci
                                tr(pvb[0:32, ci * 128:(ci + 1) * 128], kh.ap[:, c * 32:(c + 1) * 32], identb, kh.k + ["cstb"], [("ps", pt_)])
                            cp(khT.ap[0:32, c8:c8 + 8, :], pvb[0:32, 0:1024].rearrange("p (a b) -> p a b", a=8), [("ps", pt_)], khT.k)
                        cp(Sbf.ap[:, 0, :], Sbf0[pr].ap, Sbf0[pr].k, Sbf.k)
                        for c4 in range(0, NCH, 4):
                            pu = nps()
                            for ci in range(4):
                                c = c4 + ci
                                mm(PS[pu][:, ci * 128:(ci + 1) * 128], khT.ap[:, c, :], vA.ap[:, c, :], True, True, khT.k + vA.k, [("ps", pu)])
                            for ci in range(4):
                                c = c4 + ci
                                so, sn = Sst[pr][spp[pr] % 2], Sst[pr][(spp[pr] + 1) % 2]
                                spp[pr] += 1
                                stt(sn.ap, so.ap, El.ap[:, c:c + 1], PS[pu][:, ci * 128:(ci + 1) * 128], ALU.mult, ALU.add,
                                    so.k + El.k + [("ps", pu)], sn.k)
                                tt(Sbf.ap[:, c + 1, :], sn.ap, bdm, ALU.mult, sn.k + ["cst"], Sbf.k)
                        cp(Sbf0[pr].ap, Sbf.ap[:, NCH, :], Sbf.k, Sbf0[pr].k)
                        po = [nps(), nps()]
                        for c in range(NCH):
                            for a in range(2):
                                mm(PS[po[a]][:, c * 32:(c + 1) * 32], vA.ap[:, c, :], Asb.ap[:, (c * 2 + a) * 32:(c * 2 + a + 1) * 32], True, False,
                                   vA.k + Asb.k, [("ps", po[a])])
                                mm(PS[po[a]][:, c * 32:(c + 1) * 32], Sbf.ap[:, c, :], qs.ap[:, c * 32:(c + 1) * 32], False, True,
                                   Sbf.k + qs.k, [("ps", po[a])])
                        for a in range(2):
                            act(oT.ap[64 * a:64 * a + 64, :], PS[po[a]][64 * a:64 * a + 64, 0:n], AF.Copy, [("ps", po[a])], oT.k)
                        rmsnorm_fm([lambda a, b: oT.ap[a:b, :]], oT.k, n, bo64, 64.0, [gcol(l, G_HON)], [oN.ap], oN.k)
                        tt(mixed.ap[:, pr, :], oN.ap, gs.ap, ALU.mult, oN.k + gs.k, mixed.k)
                        if c0 + n == T:
                            sl_ = Sst[pr][spp[pr] % 2]
                            for a in range(2):
                                store(D["hp"][l, pr, 64 * a:64 * a + 64, :], sl_.ap[64 * a:64 * a + 64, 64 * a:64 * a + 64], sl_.k,
                                      "hp%d_%d_%d" % (l, pr, a), "hp%d" % pr)
                elif (not prompt) and DBG['S'] and 'A' in SMIX:
                    for pr in range(2):
                        AR.top = mtemp
                        q = AR.alloc([NS]); f = AR.alloc([NS]); kk = AR.alloc([NS]); vT = AR.alloc([NS]); oT = AR.alloc([NS]); oN = AR.alloc([NS])
                        gs = AR.alloc([NS], BF16); qb = AR.alloc([NS], BF16)
                        S0 = [AR.alloc([64]) for _ in range(2)]; S1 = [AR.alloc([64]) for _ in range(2)]
                        rhsV = AR.alloc([64]); tV = AR.alloc([64]); Sbd = AR.alloc([128], BF16)
                        pq = proj(W_AQ + pr * 128, 128)
                        act(q.ap, PS[pq][:, 0:n], AF.Silu, [("ps", pq)], q.k)
                        cp(qb.ap, q.ap, q.k, qb.k)
                        pf = proj(W_AF + pr * 128, 128)
                        act(f.ap, PS[pf][:, 0:n], AF.Sigmoid, [("ps", pf)], f.k)
                        ts(f.ap, f.ap, oml[:, pr, l:l + 1], lba[:, pr, l:l + 1], ALU.mult, ALU.add, f.k + ["oml", "lba"], f.k)
                        ts(f.ap, f.ap, 1e-20, None, ALU.max, None, f.k, f.k)
                        ts(kk.ap, f.ap, -1.0, 1.0, ALU.mult, ALU.add, f.k, kk.k)
                        pg = proj(W_AG + pr * 128, 128)
                        act(gs.ap, PS[pg][:, 0:n], AF.Silu, [("ps", pg)], gs.k)
                        pv = proj(W_AI + pr * 128, 128)
                        act(vT.ap, PS[pv][:, 0:n], AF.Copy, [("ps", pv)], vT.k)
                        po1 = nps()
                        for b in range(NS):
                            s0, s1 = S0[b % 2], S1[b % 2]
                            dma(s0.ap, D["sh"][l, b, pr], [], s0.k, key="sh%d" % (b % 2))
                            ts(rhsV.ap, i2f, vT.ap[:, b:b + 1], None, ALU.mult, None, vT.k + ["cst"], rhsV.k)
                            pvb = (po1 + 1 + b % 3) % 6
                            mm(PS[pvb][:, 0:64], bo64f, rhsV.ap, True, True, rhsV.k + ["cst"], [("ps", pvb)])
                            ts(tV.ap, PS[pvb][:, 0:64], kk.ap[:, b:b + 1], None, ALU.mult, None, [("ps", pvb)] + kk.k, tV.k)
                            stt(s1.ap, s0.ap, f.ap[:, b:b + 1], tV.ap, ALU.mult, ALU.add, s0.k + f.k + tV.k, s1.k)
                            store(D["hs"][l, b, pr], s1.ap, s1.k, "hs%d_%d_%d" % (l, b, pr), "hs%d" % (b % 2))
                            ts(Sbd.ap[:, 0:64], s1.ap, cst[:, C_MA:C_MA + 1], None, ALU.mult, None, s1.k + ["cst"], Sbd.k)
                            ts(Sbd.ap[:, 64:128], s1.ap, cst[:, C_MA + 1:C_MA + 2], None, ALU.mult, None, s1.k + ["cst"], Sbd.k)
                            mm(PS[po1][:, b:b + 1], Sbd.ap, qb.ap[:, b:b + 1], True, True, Sbd.k + qb.k, [("ps", po1)])
                        act(oT.ap, PS[po1][:, 0:NS], AF.Copy, [("ps", po1)], oT.k)
                        rmsnorm_fm([lambda a, b: oT.ap[a:b, :]], oT.k, n, bo64, 64.0, [gcol(l, G_HON)], [oN.ap], oN.k)
                        tt(mixed.ap[:, pr, :], oN.ap, gs.ap, ALU.mult, oN.k + gs.k, mixed.k)
                else:
                    for pr in range(2):
                        mset(mixed.ap[:, pr, :], 0.0, mixed.k)
                AR.top = mtemp

                cqn = AR.alloc([3, n], BF16); ckvn = AR.alloc([2, n]); ckvb = AR.alloc([2, n], BF16)
                krn = AR.alloc([n]); kro = AR.alloc([n]); krt = AR.alloc([n])
                pq3 = [proj(W_BQ + j * 128, 128) for j in range(3)]
                rmsnorm_fm([psrows(pq3[j], n) for j in range(3)], [("ps", b_) for b_ in pq3], n, onesb, 384.0,
                           [gcol(l, G_QN + j) for j in range(3)], [cqn.ap[:, j, :] for j in range(3)], cqn.k)
                pk2 = [proj(W_BKV + j * 128, 128) for j in range(2)]
                rmsnorm_fm([psrows(pk2[j], n) for j in range(2)], [("ps", b_) for b_ in pk2], n, onesb, 256.0,
                           [gcol(l, G_KVN + j) for j in range(2)], [ckvn.ap[:, j, :] for j in range(2)], ckvn.k)
                cp(ckvb.ap, ckvn.ap, ckvn.k, ckvb.k, eng="pool")
                pkr = proj(W_KR3, 128)
                rmsnorm_fm([psrows(pkr, n)], [("ps", pkr)], n, bo32, 32.0, [gcol(l, G_KHR)], [krn.ap], krn.k)
                prr = nps()
                mm(PS[prr][:, 0:n], rotf, krn.ap, True, True, krn.k + ["cst"], [("ps", prr)])
                if prompt:
                    tt(kro.ap, krn.ap, cosT[:, c0:c0 + n], ALU.mult, krn.k + ["tab"], kro.k)
                    tt(krt.ap, PS[prr][:, 0:n], sinT[:, c0:c0 + n], ALU.mult, [("ps", prr), "tab"], krt.k)
                else:
                    ts(kro.ap, krn.ap, cosS[:, 0:1], None, ALU.mult, None, krn.k + ["tab"], kro.k)
                    ts(krt.ap, PS[prr][:, 0:n], sinS[:, 0:1], None, ALU.mult, None, [("ps", prr), "tab"], krt.k)
                tt(kro.ap, kro.ap, krt.ap, ALU.add, kro.k + krt.k, kro.k)
                for t0 in range(0, n, 128):
                    r = min(128, n - t0)
                    po_ = nps()
                    for k in range(2):
                        tr(PS[po_][0:r, k * 128:(k + 1) * 128], ckvn.ap[:, k, t0:t0 + r], identf, ckvn.k + ["cst"], [("ps", po_)])
                    tr(PS[po_][0:r, 256:288], kro.ap[0:32, t0:t0 + r], identf[0:32, 0:32], kro.k + ["cst"], [("ps", po_)])
                    ob = obs[obn[0] % 2]; osem = "ob%d" % (obn[0] % 2); obn[0] += 1
                    cp(ob.ap[0:r, 0:288], PS[po_][0:r, 0:288], [("ps", po_)], ob.k)
                    if prompt:
                        store(D["kvp"][l, c0 + t0:c0 + t0 + r, :], ob.ap[0:r, 0:256], ob.k, "kvp%d_%d" % (l, c0 + t0), osem)
                        store(D["krp"][l, c0 + t0:c0 + t0 + r, :], ob.ap[0:r, 256:288], ob.k, "krp%d_%d" % (l, c0 + t0), osem)
                    else:
                        store(D["kvs"][l, :, :], ob.ap[0:r, 0:256], ob.k, "kvs%d" % l, osem)
                        store(D["krs"][l, :, :], ob.ap[0:r, 256:288], ob.k, "krs%d" % l, osem)
                if prompt and DBG['B']:
                    cp(krT3.ap[:, c0:c0 + n], kro.ap, kro.k, krT3.k)
                    for j in range(4):
                        wk_, wkk = wload(D["wuk"][l, :, :, j * 128:(j + 1) * 128], 128, 2, 128)
                        pb = nps()
                        for cc in range(2):
                            mm(PS[pb][:, 0:n], wk_[:, cc, :], ckvb.ap[:, cc, :], cc == 0, cc == 1, wkk + ckvb.k, [("ps", pb)])
                        rmsnorm_fm([psrows(pb, n)], [("ps", pb)], n, bo64, 64.0, [gcol(l, G_KHN)], [knT.ap[:, j, c0:c0 + n]], knT.k)
                    wv0, wvk0 = wload(D["wuv"][l, :, 0:1, :], 128, 1, 512)
                    wv1, wvk1 = wload(D["wuv"][l, :, 1:2, :], 128, 1, 512)
                    for t4 in range(n // 128):
                        pb = nps()
                        mm(PS[pb][:, :], ckvb.ap[:, 0, t4 * 128:(t4 + 1) * 128], wv0[:, 0, :], True, False, wvk0 + ckvb.k, [("ps", pb)])
                        mm(PS[pb][:, :], ckvb.ap[:, 1, t4 * 128:(t4 + 1) * 128], wv1[:, 0, :], False, True, wvk1 + ckvb.k, [("ps", pb)])
                        act(vtok.ap[:, c0 // 128 + t4, :], PS[pb][:, :], AF.Copy, [("ps", pb)], vtok.k)
                    qnm = AR.alloc([2, 4, n], BF16); qrm = AR.alloc([8, n], BF16); qrn = krn; qra = kro; qrb = krt
                    for j in range(4):
                        wq_, wqk = wload(D["wqn"][l, :, :, j * 128:(j + 1) * 128], 128, 3, 128)
                        pb = nps()
                        for cc in range(3):
                            mm(PS[pb][:, 0:n], wq_[:, cc, :], cqn.ap[:, cc, :], cc == 0, cc == 2, wqk + cqn.k, [("ps", pb)])
                        rmsnorm_fm([psrows(pb, n), psrows(pb, n)], [("ps", pb)], n, bo64, 64.0, [gcol(l, G_QHNM), gcol(l, G_QHNM + 1)],
                                   [qnm.ap[:, 0, j, :], qnm.ap[:, 1, j, :]], qnm.k, dup=True)
                    for j in range(2):
                        wq_, wqk = wload(D["wqr"][l, :, :, j * 128:(j + 1) * 128], 128, 3, 128)
                        pb = nps()
                        for cc in range(3):
                            mm(PS[pb][:, 0:n], wq_[:, cc, :], cqn.ap[:, cc, :], cc == 0, cc == 2, wqk + cqn.k, [("ps", pb)])
                        rmsnorm_fm([psrows(pb, n)], [("ps", pb)], n, bo32, 32.0, [gcol(l, G_QHR)], [qrn.ap], qrn.k)
                        pr2 = nps()
                        mm(PS[pr2][:, 0:n], rotf, qrn.ap, True, True, qrn.k + ["cst"], [("ps", pr2)])
                        tt(qra.ap, qrn.ap, cosT[:, c0:c0 + n], ALU.mult, qrn.k + ["tab"], qra.k)
                        tt(qrb.ap, PS[pr2][:, 0:n], sinT[:, c0:c0 + n], ALU.mult, [("ps", pr2), "tab"], qrb.k)
                        tt(qra.ap, qra.ap, qrb.ap, ALU.add, qra.k + qrb.k, qra.k)
                        for s4 in range(4):
                            ts(qrm.ap[:, j * 4 + s4, :], qra.ap, cst[:, C_MS + s4:C_MS + s4 + 1], None, ALU.mult, None, qra.k + ["cst"], qrm.k)
                    PT = [AR.alloc([n], BF16) for _ in range(3)]
                    rl = krn; oNn = kro; oB = krt
                    nkb = (c0 + n) // 128
                    ptn = 0
                    for h in (range(8) if DBG['ATT'] else []):
                        j, a = h // 2, h % 2
                        pO, pL = (4, 5) if h % 2 == 0 else (6, 7)
                        for kb in range(nkb):
                            qlo = max(0, kb * 128 - c0)
                            N = n - qlo
                            pS = nps(0, 4)
                            mm(PS[pS][:, 0:N], knT.ap[:, j, kb * 128:(kb + 1) * 128], qnm.ap[:, a, j, qlo:n], True, False,
                               knT.k + qnm.k, [("ps", pS)])
                            mm(PS[pS][:, 0:N], krT3.ap[:, kb * 128:(kb + 1) * 128], qrm.ap[:, h, qlo:n], False, True,
                               krT3.k + qrm.k, [("ps", pS)])
                            pt_b = PT[ptn % 3]; ptn += 1
                            act(pt_b.ap[:, 0:N], PS[pS][:, 0:N], AF.Exp, [("ps", pS)], pt_b.k, scale=SM_SCALE)
                            if kb * 128 >= c0:
                                tt(pt_b.ap[:, 0:128], pt_b.ap[:, 0:128], trib, ALU.mult, pt_b.k + ["cstb"], pt_b.k, eng="pool")
                            mm(PS[pO][:, qlo:n], vtok.ap[:, kb, j * 128:(j + 1) * 128], pt_b.ap[:, 0:N], kb == 0, kb == nkb - 1, vtok.k + pt_b.k, [("ps", pO)])
                            mm(PS[pL][:, qlo:n], onesb, pt_b.ap[:, 0:N], kb == 0, kb == nkb - 1, pt_b.k + ["cstb"], [("ps", pL)])
                        recip(rl.ap, PS[pL][:, 0:n], [("ps", pL)], rl.k)
                        tt(oNn.ap, PS[pO][:, 0:n], rl.ap, ALU.mult, [("ps", pO)] + rl.k, oNn.k)
                        rmsnorm_fm([lambda a_, b_: oNn.ap[a_:b_, :]], oNn.k, n, bo64, 64.0, [gcol(l, G_ON + h)], [oB.ap], oB.k)
                        cp(mixed.ap[64 * a:64 * a + 64, 2 + j, :], oB.ap[64 * a:64 * a + 64, :], oB.k, mixed.k, eng="pool")
                elif (not prompt) and DBG['S'] and 'B' in SMIX:
                    cur = AR.top
                    AR.top = mbig
                    raws = [AR.alloc([8, 256], BF16) for _ in range(2)]
                    krraws = [AR.alloc([128, 32], BF16)]; sqbs = [AR.alloc([4, 1024], BF16)]; cTs = [AR.alloc([2, 1024], BF16)]; ropeDs = [AR.alloc([128, 8])]
                    assert AR.top <= mseg, (AR.top, mseg)
                    AR.top = cur
                    rtmp = AR.alloc([64, 32], BF16); QrB = AR.alloc([NS, 256], BF16); wukS = AR.alloc([2, 512], BF16); wuvS = AR.alloc([2, 512], BF16)
                    QabsT = AR.alloc([2, NS, 8], BF16); qg32 = AR.alloc([2, 4, NS]); qgm = AR.alloc([2, 4, NS], BF16)
                    qr32 = AR.alloc([2, NS]); qrtok = AR.alloc([256], BF16); sqn = AR.alloc([4, NS], BF16)
                    sqbs.append(AR.alloc([4, 1024], BF16)); cTs.append(AR.alloc([2, 1024], BF16)); krraws.append(AR.alloc([128, 32], BF16)); ropeDs.append(AR.alloc([128, 8]))
                    e64as = [AR.alloc([64]) for _ in range(2)]; e64bs = [AR.alloc([64]) for _ in range(2)]; pTs = [AR.alloc([64], BF16) for _ in range(2)]
                    pnew = AR.alloc([NS, 8], BF16); cnew = AR.alloc([257], BF16); krtok = AR.alloc([32]); tmpR = AR.alloc([32, 32]); ropeN = AR.alloc([32])
                    n8 = AR.alloc([8]); n32a = AR.alloc([32]); n32b = AR.alloc([32])
                    olat = AR.alloc([257]); olatn = AR.alloc([256]); rl8 = AR.alloc([1]); olT = AR.alloc([2, 8], BF16); oS = AR.alloc([4, NS])
                    for cc in range(2):
                        w_, wk_ = wload(D["wuk"][l, :, cc:cc + 1, :], 128, 1, 512)
                        cp(wukS.ap[:, cc, :], w_[:, 0, :], wk_, wukS.k, eng="pool")
                        w_, wk_ = wload(D["wuv"][l, :, cc:cc + 1, :], 128, 1, 512)
                        cp(wuvS.ap[:, cc, :], w_[:, 0, :], wk_, wuvS.k, eng="pool")
                    for j in range(4):
                        wq_, wqk = wload(D["wqn"][l, :, :, j * 128:(j + 1) * 128], 128, 3, 128)
                        pb = nps()
                        for cc in range(3):
                            mm(PS[pb][:, 0:n], wq_[:, cc, :], cqn.ap[:, cc, :], cc == 0, cc == 2, wqk + cqn.k, [("ps", pb)])
                        rmsnorm_fm([psrows(pb, n), psrows(pb, n)], [("ps", pb)], n, bo64, 64.0, [gcol(l, G_QHNM), gcol(l, G_QHNM + 1)],
                                   [qg32.ap[:, 0, j, :], qg32.ap[:, 1, j, :]], qg32.k, dup=True)
                    ts(qgm.ap, qg32.ap, gcol(l, G_KHN), None, ALU.mult, None, qg32.k + ["gv"], qgm.k)
                    pqa = nps()
                    for j in range(4):
                        wt_, wtk = wload(D["wukT"][l, :, j:j + 1, :], 128, 1, 256)
                        for a in range(2):
                            for cc in range(2):
                                col = (cc * 8 + 2 * j + a) * NS
                                mm(PS[pqa][:, col:col + NS], wt_[:, 0, cc * 128:(cc + 1) * 128], qgm.ap[:, a, j, :], True, True, wtk + qgm.k, [("ps", pqa)])
                    cp(QabsT.ap.rearrange("p c b h -> p c h b"), PS[pqa][:, 0:64].rearrange("p (c h b) -> p c h b", c=2, h=8), [("ps", pqa)], QabsT.k)
                    for j in range(2):
                        wq_, wqk = wload(D["wqr"][l, :, :, j * 128:(j + 1) * 128], 128, 3, 128)
                        pb = nps()
                        for cc in range(3):
                            mm(PS[pb][:, 0:n], wq_[:, cc, :], cqn.ap[:, cc, :], cc == 0, cc == 2, wqk + cqn.k, [("ps", pb)])
                        rmsnorm_fm([psrows(pb, n)], [("ps", pb)], n, bo32, 32.0, [gcol(l, G_QHR)], [krn.ap], krn.k)
                        pr2 = nps()
                        mm(PS[pr2][:, 0:n], rotf, krn.ap, True, True, krn.k + ["cst"], [("ps", pr2)])
                        ts(krt.ap, krn.ap, cosS[:, 0:1], None, ALU.mult, None, krn.k + ["tab"], krt.k)
                        stt(qr32.ap[:, j, :], PS[pr2][:, 0:n], sinS[:, 0:1], krt.ap, ALU.mult, ALU.add, [("ps", pr2), "tab"] + krt.k, qr32.k)
                    mset(qrtok.ap, 0.0, qrtok.k)
                    ptq = nps()
                    for j in range(2):
                        tr(PS[ptq][0:NS, j * 128:(j + 1) * 128], qr32.ap[:, j, :], identf, qr32.k + ["cst"], [("ps", ptq)])
                    cp(qrtok.ap[0:NS, :], PS[ptq][0:NS, 0:256], [("ps", ptq)], qrtok.k)
                    for b in range(NS):
                        pbq = nps()
                        mm(PS[pbq][:, 0:256], cstb[:, C_EB + b * 128:C_EB + (b + 1) * 128], qrtok.ap, True, True, qrtok.k + ["cstb"], [("ps", pbq)])
                        act(QrB.ap[:, b, :], PS[pbq][:, 0:256], AF.Copy, [("ps", pbq)], QrB.k)
                    pk_ = nps()
                    for m in range(4):
                        for cc in range(2):
                            mm(PS[pk_][:, m * NS:(m + 1) * NS], wukS.ap[:, cc, m * 128:(m + 1) * 128], ckvb.ap[:, cc, :], cc == 0, cc == 1, wukS.k + ckvb.k, [("ps", pk_)])
                    act(sqn.ap, PS[pk_][:, 0:4 * NS].rearrange("p (m b) -> p m b", m=4), AF.Square, [("ps", pk_)], sqn.k)
                    pn_ = nps()
                    for m in range(4):
                        mm(PS[pn_][0:NS, 0:8], sqn.ap[:, m, :], indb[:, m * 8:(m + 1) * 8], m == 0, m == 3, sqn.k + ["cstb"], [("ps", pn_)])
                    for cc in range(2):
                        mm(PS[pn_][0:NS, 8:40], ckvb.ap[:, cc, :], QabsT.ap[:, cc, :, :].rearrange("p b h -> p (b h)"), cc == 0, cc == 1, ckvb.k + QabsT.k, [("ps", pn_)])
                    ptk = nps()
                    tr(PS[ptk][0:NS, 0:32], kro.ap[0:32, :], identf[0:32, 0:32], kro.k + ["cst"], [("ps", ptk)])
                    for cc in range(2):
                        tr(PS[ptk][0:NS, 128 + cc * 128:256 + cc * 128], ckvn.ap[:, cc, :], identf, ckvn.k + ["cst"], [("ps", ptk)])
                    cp(krtok.ap[0:NS, :], PS[ptk][0:NS, 0:32], [("ps", ptk)], krtok.k)
                    mset(cnew.ap, 0.0, cnew.k)
                    mset(cnew.ap[0:NS, 256:257], 1.0, cnew.k)
                    cp(cnew.ap[0:NS, 0:256], PS[ptk][0:NS, 128:384], [("ps", ptk)], cnew.k)
                    tt(tmpR.ap[0:NS, :, :], QrB.ap[0:NS, :, :].rearrange("p b (h r) -> p (b h) r", r=32),
                       krtok.ap[0:NS, :].rearrange("p (o r) -> p o r", o=1).to_broadcast([NS, 32, 32]), ALU.mult, QrB.k + krtok.k, tmpR.k)
                    treduce(ropeN.ap[0:NS, :], tmpR.ap[0:NS, :, :], tmpR.k, ropeN.k)
                    act(n8.ap[0:NS, :], PS[pn_][0:NS, 0:8], AF.Ln, [("ps", pn_), "epsc"], n8.k, scale=1.0 / 64.0, bias=epsc[0:NS, 0:1])
                    act(n8.ap[0:NS, :], n8.ap[0:NS, :], AF.Exp, n8.k, n8.k, scale=-0.5)
                    tt(n32a.ap[0:NS, :].rearrange("p (b h) -> p b h", h=8), PS[pn_][0:NS, 8:40].rearrange("p (b h) -> p b h", h=8),
                       n8.ap[0:NS, :].rearrange("p (o h) -> p o h", o=1).to_broadcast([NS, NS, 8]), ALU.mult, [("ps", pn_)] + n8.k, n32a.k)
                    tt(n32a.ap[0:NS, :], n32a.ap[0:NS, :], ropeN.ap[0:NS, :], ALU.add, n32a.k + ropeN.k, n32a.k)
                    act(n32b.ap[0:NS, :], n32a.ap[0:NS, :], AF.Exp, n32a.k, n32b.k, scale=SM_SCALE)
                    mset(pnew.ap, 0.0, pnew.k)
                    tt(pnew.ap[0:NS, :, :].rearrange("p b h -> p (b h)"), n32b.ap[0:NS, :], pm4, ALU.mult, n32b.k + ["cst"], pnew.k)
                    PACC, PLS = 7, 6
                    def rope_gather(b):
                        krraw = krraws[b % 2]
                        for half in range(2):
                            gather(krraw.ap.rearrange("p t r -> p (t r)")[:, half * 2048:(half + 1) * 2048], D["ckr%d" % l], idx2[:, b, half:half + 1], ["idx2"], krraw.k,
                                   "krraw%d" % (b % 2))

                    def rope_dot(b, h, half):
                        krraw, ropeD = krraws[b % 2], ropeDs[b % 2]
                        tt(rtmp.ap, krraw.ap[:, half * 64:(half + 1) * 64, :],
                           QrB.ap[:, b, h * 32:(h + 1) * 32].rearrange("p (o r) -> p o r", o=1).to_broadcast([128, 64, 32]), ALU.mult,
                           krraw.k + QrB.k, rtmp.k)
                        treduce(ropeD.ap[:, half * 64:(half + 1) * 64, h], rtmp.ap, rtmp.k, ropeD.k)

                    rope_gather(0)
                    for h in range(8):
                        for half in range(2):
                            rope_dot(0, h, half)
                    for b in range(NS):
                        ropeD = ropeDs[b % 2]
                        if b + 1 < NS:
                            rope_gather(b + 1)
                        for ch in range(16):
                            if b + 1 < NS:
                                rope_dot(b + 1, ch // 2, ch % 2)
                            raw = raws[ch % 2]; cT = cTs[ch % 2]; sqb = sqbs[ch % 2]; e64a = e64as[ch % 2]; e64b = e64bs[ch % 2]; pT = pTs[ch % 2]
                            gather(raw.ap.rearrange("p t c -> p (t c)"), D["ckv%d" % l], idx16[:, b, ch:ch + 1], ["idx16"], raw.k, "raw%d" % (ch % 2))
                            ptr = [nps(), nps()]
                            for cc in range(2):
                                pvb_ = PS[ptr[cc]][:].bitcast(BF16)
                                for j in range(8):
                                    tr(pvb_[:, j * 128:(j + 1) * 128], raw.ap[:, j, cc * 128:(cc + 1) * 128], identb, raw.k + ["cstb"], [("ps", ptr[cc])])
                                if cc == 0:
                                    cp(cT.ap[:, cc, :], pvb_[:, 0:1024], [("ps", ptr[cc])], cT.k)
                                else:
                                    act(cT.ap[:, cc, :], pvb_[:, 0:1024], AF.Copy, [("ps", ptr[cc])], cT.k)
                            for half in range(2):
                                for m in range(4):
                                    pk2 = nps()
                                    for cc in range(2):
                                        mm(PS[pk2][:, :], wukS.ap[:, cc, m * 128:(m + 1) * 128], cT.ap[:, cc, half * 512:(half + 1) * 512], cc == 0, cc == 1,
                                           wukS.k + cT.k, [("ps", pk2)])
                                    act(sqb.ap[:, m, half * 512:(half + 1) * 512], PS[pk2][:, :], AF.Square, [("ps", pk2)], sqb.k)
                            pss, psd = nps(), nps()
                            for j in range(8):
                                for m in range(4):
                                    mm(PS[pss][:, j * 8:(j + 1) * 8], sqb.ap[:, m, j * 128:(j + 1) * 128], indb[:, m * 8:(m + 1) * 8], m == 0, m == 3,
                                       sqb.k + ["cstb"], [("ps", pss)])
                                for cc in range(2):
                                    mm(PS[psd][:, j * 8:(j + 1) * 8], cT.ap[:, cc, j * 128:(j + 1) * 128], QabsT.ap[:, cc, b, :], cc == 0, cc == 1,
                                       cT.k + QabsT.k, [("ps", psd)])
                            act(e64a.ap, PS[pss][:, 0:64], AF.Ln, [("ps", pss), "epsc"], e64a.k, scale=1.0 / 64.0, bias=epsc[:, 0:1])
                            act(e64a.ap, e64a.ap, AF.Exp, e64a.k, e64a.k, scale=-0.5)
                            tt(e64b.ap, PS[psd][:, 0:64], e64a.ap, ALU.mult, [("ps", psd)] + e64a.k, e64b.k)
                            tt(e64b.ap.rearrange("p (t h) -> p t h", h=8), e64b.ap.rearrange("p (t h) -> p t h", h=8), ropeD.ap[:, ch * 8:(ch + 1) * 8, :], ALU.add,
                               e64b.k + ropeD.k, e64b.k)
                            act(pT.ap, e64b.ap, AF.Exp, e64b.k, pT.k, scale=SM_SCALE)
                            for j in range(8):
                                mm(PS[PACC][0:8, 0:256], pT.ap[:, j * 8:(j + 1) * 8], raw.ap[:, j, :], ch == 0 and j == 0, False, pT.k + raw.k, [("ps", PACC)])
                                mm(PS[PLS][0:8, 0:1], pT.ap[:, j * 8:(j + 1) * 8], onesb[:, 0:1], ch == 0 and j == 0, False, pT.k + ["cstb"], [("ps", PLS)])
                        mm(PS[PACC][0:8, 0:256], pnew.ap[:, b, :], cnew.ap[:, 0:256], False, True, pnew.k + cnew.k, [("ps", PACC)])
                        mm(PS[PLS][0:8, 0:1], pnew.ap[:, b, :], cnew.ap[:, 256:257], False, True, pnew.k + cnew.k, [("ps", PLS)])
                        act(olat.ap[0:8, 0:256], PS[PACC][0:8, 0:256], AF.Copy, [("ps", PACC)], olat.k)
                        recip(rl8.ap[0:8, :], PS[PLS][0:8, 0:1], [("ps", PLS)], rl8.k)
                        ts(olatn.ap[0:8, :], olat.ap[0:8, 0:256], rl8.ap[0:8, 0:1], None, ALU.mult, None, olat.k + rl8.k, olatn.k)
                        pt2 = nps()
                        for cc in range(2):
                            tr(PS[pt2][:, cc * 8:(cc + 1) * 8], olatn.ap[0:8, cc * 128:(cc + 1) * 128], identf[0:8, 0:8], olatn.k + ["cst"], [("ps", pt2)])
                        cp(olT.ap.rearrange("p c h -> p (c h)"), PS[pt2][:, 0:16], [("ps", pt2)], olT.k)
                        pu2 = nps()
                        for h in range(8):
                            for cc in range(2):
                                mm(PS[pu2][:, h:h + 1], wuvS.ap[:, cc, (h // 2) * 128:(h // 2 + 1) * 128], olT.ap[:, cc, h:h + 1], cc == 0, cc == 1,
                                   wuvS.k + olT.k, [("ps", pu2)])
                        for h in range(8):
                            a = h % 2
                            act(oS.ap[64 * a:64 * a + 64, h // 2, b:b + 1], PS[pu2][64 * a:64 * a + 64, h:h + 1], AF.Copy, [("ps", pu2)], oS.k)
                    for j in range(4):
                        rmsnorm_fm([lambda a_, b_, j=j: oS.ap[a_:b_, j, :]], oS.k, n, bo64, 64.0, [gcol(l, G_ONP + j)], [mixed.ap[:, 2 + j, :]], mixed.k)
                else:
                    for j in range(4):
                        mset(mixed.ap[:, 2 + j, :], 0.0, mixed.k)
                AR.top = mtemp

                for dj in (range(8) if DBG['WOUT'] else []):
                    wo_, wok = wload(D["wo"][l, :, :, dj * 128:(dj + 1) * 128], 128, 8, 128)
                    pb = nps()
                    for kc in range(8):
                        mm(PS[pb][:, 0:n], wo_[:, kc, :], mixed.ap[:, kc, :], kc == 0, kc == 7, wok + mixed.k, [("ps", pb)])
                    tt(xT[:, dj, c0:c0 + n], xT[:, dj, c0:c0 + n], PS[pb][:, 0:n], ALU.add, [xk(c0), ("ps", pb)], [xk(c0)])

            AR.top = mlayer
            h2T = AR.alloc([8, NT], BF16); ffT = AR.alloc([6, NT], BF16); wdq = AR.alloc([6, 1024], BF16); sg = [AR.alloc([SEG]) for _ in range(2)]
            nbase, nbase2 = len(wstg), len(wbf)
            for _ in range(2):
                wstg.append(AR.alloc([1024])); wbf.append(AR.alloc([1024], BF16))
            wbf.append(AR.alloc([1024], BF16))
            sq8 = AR.alloc([8, SEG], BF16)
            for (c0, n) in NBLK:
                rmsnorm_fm([(lambda a, b, k=k: xT[a:b, k, c0:c0 + n]) for k in range(8)], [xk(c0)], n, onesb, 1024.0,
                           [gcol(l, G_FFN + k) for k in range(8)], [h2T.ap[:, k, c0:c0 + n] for k in range(8)], h2T.k, sq=sq8)
            fq = [(0, 6), (6, 6), (12, 5), (17, 5)] if DBG['FFN'] else []
            sgn = 0
            for (f0, nf) in fq:
                for fi in range(nf):
                    fch = f0 + fi
                    wg_, wgk = wload(D["wg"][l, :, :, fch * 128:(fch + 1) * 128], 128, 8, 128)
                    wu_, wuk_ = wload(D["wu"][l, :, :, fch * 128:(fch + 1) * 128], 128, 8, 128)
                    for (c0, n) in NBLK:
                        pg_, pu_ = nps(0, 8), nps(0, 8)
                        for k in range(8):
                            mm(PS[pg_][:, 0:n], wg_[:, k, :], h2T.ap[:, k, c0:c0 + n], k == 0, k == 7, wgk + h2T.k, [("ps", pg_)])
                        for k in range(8):
                            mm(PS[pu_][:, 0:n], wu_[:, k, :], h2T.ap[:, k, c0:c0 + n], k == 0, k == 7, wuk_ + h2T.k, [("ps", pu_)])
                        s_ = sg[sgn % 2]; sgn += 1
                        act(s_.ap[:, 0:n], PS[pg_][:, 0:n], AF.Silu, [("ps", pg_)], s_.k)
                        tt(ffT.ap[:, fi, c0:c0 + n], s_.ap[:, 0:n], PS[pu_][:, 0:n], ALU.mult, s_.k + [("ps", pu_)], ffT.k)
                for f2 in range(nf):
                    wd_, wdk = wload(D["wd"][l, :, f0 + f2:f0 + f2 + 1, :], 128, 1, 1024)
                    cp(wdq.ap[:, f2:f2 + 1, :], wd_, wdk, wdq.k, eng="pool")
                for dj in range(8):
                    for (c0, n) in NBLK:
                        pb = nps(0, 8)
                        for fi in range(nf):
                            mm(PS[pb][:, 0:n], wdq.ap[:, fi, dj * 128:(dj + 1) * 128], ffT.ap[:, fi, c0:c0 + n], fi == 0, fi == nf - 1, wdq.k + ffT.k, [("ps", pb)])
                        tt(xT[:, dj, c0:c0 + n], xT[:, dj, c0:c0 + n], PS[pb][:, 0:n], ALU.add, [xk(c0), ("ps", pb)], [xk(c0)])

            del wstg[nbase:], wbf[nbase2:]

        AR.top = mlayer
        yos = [AR.alloc([1024]) for _ in range(2)]
        for tb in range(17):
            rows = 128 if tb < 16 else NS
            yo = yos[tb % 2]
            for half in range(2):
                pb = nps()
                for k in range(4):
                    dc = half * 4 + k
                    tr(PS[pb][0:rows, k * 128:(k + 1) * 128], xT[:, dc, tb * 128:tb * 128 + rows], identf, [xk(tb * 128), "cst"], [("ps", pb)])
                cp(yo.ap[0:rows, half * 512:(half + 1) * 512], PS[pb][0:rows, :], [("ps", pb)], yo.k)
            if tb < 16:
                store(D["yp"][tb * 128:(tb + 1) * 128, :], yo.ap[0:rows, :], yo.k, "yp%d" % tb, "yo%d" % (tb % 2))
            else:
                store(D["ys"], yo.ap[0:rows, :], yo.k, "ys", "yo%d" % (tb % 2))

        if MAXOPS is not None:
            P.ops = P.ops[:MAXOPS]
        else:
            P.add("sp", lambda e: e.nop(), reads=[k for k in outkeys if FINAL_FILTER is None or FINAL_FILTER in k])
        P.emit(st)
        print("ops", len(P.ops), "arena peak words", AR.peak, "of", UW)
    return nc


def _host_consts():
    c = np.zeros((128, NCST), np.float32)
    p = np.arange(128)
    c[:, C_ID:C_ID + 128] = np.eye(128)
    for m in range(128):
        if m % 32 < 16:
            c[m + 16, C_ROT + m] = -1.0
        else:
            c[m - 16, C_ROT + m] = 1.0
    c[:, C_BO64:C_BO64 + 128] = (p[:, None] // 64 == p[None, :] // 64)
    c[:, C_BO32:C_BO32 + 128] = (p[:, None] // 32 == p[None, :] // 32)
    c[:, C_ONE:C_ONE + 128] = 1.0
    c[:, C_BD:C_BD + 128] = (p[:, None] // 64 == p[None, :] // 64)
    c[:, C_I2:C_I2 + 64] = (p[:, None] % 64 == np.arange(64)[None, :])
    for m in range(4):
        for j in range(8):
            c[:, C_IND + m * 8 + j] = (j == 2 * m + p // 64)
    c[0:32, C_CM:C_CM + 32] = (np.arange(32)[:, None] <= np.arange(32)[None, :])
    c[:, C_TRI:C_TRI + 128] = (p[:, None] <= p[None, :])
    c[:, C_MA] = (p < 64)
    c[:, C_MA + 1] = (p >= 64)
    for s4 in range(4):
        c[:, C_MS + s4] = (p // 32 == s4)
    for b in range(4):
        c[b, C_EB + b * 128:C_EB + (b + 1) * 128] = 1.0
    inv = 10000.0 ** (-np.arange(16, dtype=np.float64) / 16.0)
    c[:, C_INVF] = (inv[p % 16] / (2.0 * np.pi)).astype(np.float32)
    for b in range(4):
        c[b, C_PM + b * 8:C_PM + b * 8 + 8] = 1.0
    return c


def _pcol(v, mod=None):
    v = np.asarray(v, np.float32)
    if mod is not None:
        return v[np.arange(128) % mod][:, None]
    return np.ascontiguousarray(v.reshape(-1, 128).T)


_NC_CACHE = {}


def kernel(x_prompt, x_sample, cache_kv_latent, cache_k_rope, state_hgrn, state_conv, page_table,
           norm_mix_g, w_in, hgrn_lb, hgrn_onorm_g, mla_q_norm_g, mla_w_q_up, mla_kv_norm_g,
           mla_w_kv_up, mla_q_head_g, mla_k_head_g, mla_onorm_g, conv_w, conv_onorm_g, w_out,
           norm_ffn_g, w_gate, w_up, w_down):
    f = lambda a: np.asarray(a, np.float32)
    ncores = 8
    if "nc" not in _NC_CACHE:
        _NC_CACHE["nc"] = build_nc()
    nc = _NC_CACHE["nc"]
    w_in = f(w_in)
    win = w_in.reshape(NL, 8, 128, 2464).transpose(0, 2, 1, 3)
    win = np.ascontiguousarray(np.concatenate([win] + [win[..., W_BR:W_BR + 32]] * 4, axis=-1))
    gvt = np.zeros((128, NL, NG), np.float32)
    for l in range(NL):
        gvt[:, l, G_MIX:G_MIX + 8] = _pcol(f(norm_mix_g)[l])
        gvt[:, l, G_FFN:G_FFN + 8] = _pcol(f(norm_ffn_g)[l])
        gvt[:, l, G_QN:G_QN + 3] = _pcol(f(mla_q_norm_g)[l])
        gvt[:, l, G_KVN:G_KVN + 2] = _pcol(f(mla_kv_norm_g)[l])
        gvt[:, l, G_QHN:G_QHN + 1] = _pcol(f(mla_q_head_g)[l][:64], 64)
        gvt[:, l, G_QHR:G_QHR + 1] = _pcol(f(mla_q_head_g)[l][64:], 32)
        gvt[:, l, G_KHN:G_KHN + 1] = _pcol(f(mla_k_head_g)[l][:64], 64)
        gvt[:, l, G_KHR:G_KHR + 1] = _pcol(f(mla_k_head_g)[l][64:], 32)
        gvt[:, l, G_ON:G_ON + 8] = f(mla_onorm_g)[l].reshape(8, 64).T[np.arange(128) % 64]
        gvt[:, l, G_CON:G_CON + 2] = _pcol(f(conv_onorm_g)[l])
        gvt[:, l, G_CW:G_CW + 6] = f(conv_w)[l].reshape(3, 2, 128).transpose(2, 1, 0).reshape(128, 6)
        gvt[:, l, G_HON:G_HON + 1] = _pcol(f(hgrn_onorm_g)[l], 64)
        gvt[:, l, G_ONP:G_ONP + 4] = _pcol(f(mla_onorm_g)[l])
        gvt[0:64, l, G_QHNM] = f(mla_q_head_g)[l][:64]
        gvt[64:128, l, G_QHNM + 1] = f(mla_q_head_g)[l][:64]
    lball = np.ascontiguousarray(f(hgrn_lb).reshape(NL, 2, 128).transpose(2, 1, 0))
    kmaj = lambda w, kc: np.ascontiguousarray(w.reshape(NL, kc, 128, -1).transpose(0, 2, 1, 3))
    qup = f(mla_w_q_up)
    wqn = kmaj(qup[..., :64].reshape(NL, 384, 512), 3)
    wqr = kmaj(np.ascontiguousarray(qup[..., 64:]).reshape(NL, 384, 256), 3)
    kvup = f(mla_w_kv_up)
    wuk = kmaj(kvup[..., :64].reshape(NL, 256, 512), 2)
    wuv = kmaj(kvup[..., 64:].reshape(NL, 256, 512), 2)
    wukT = np.ascontiguousarray(kvup[..., :64].transpose(0, 2, 3, 1).reshape(NL, 4, 128, 256).transpose(0, 2, 1, 3))
    wo = kmaj(f(w_out), 8); wg = kmaj(f(w_gate), 8); wu = kmaj(f(w_up), 8); wd = kmaj(f(w_down), 22)
    cst = _host_consts()
    pt = np.asarray(page_table, np.int32)
    ckv = f(cache_kv_latent); ckr = f(cache_k_rope); shs = f(state_hgrn); scs = f(state_conv)
    npool = ckv.shape[1]
    assert npool == POOLN[0]
    in_maps = []
    for c in range(ncores):
        m = {"ckv%d" % l: ckv[l].reshape(npool * 16, 2048) for l in range(NL)}
        m.update({"ckr%d" % l: ckr[l].reshape(npool * 2, 2048) for l in range(NL)})
        m.update({"sh": np.ascontiguousarray(shs[:, 4 * c:4 * c + 4].reshape(NL, 4, 2, 128, 64)), "sc": np.ascontiguousarray(scs[:, 4 * c:4 * c + 4]),
                  "ptT": np.ascontiguousarray(pt[4 * c:4 * c + 4].T), "wukT": wukT})
        in_maps.append(m)
        in_maps[-1].update({
            "xp": np.ascontiguousarray(f(x_prompt)[c]),
            "xs": np.ascontiguousarray(f(x_sample)[4 * c:4 * c + 4, 0, :]),
            "win": win, "wqn": wqn, "wqr": wqr, "wuk": wuk, "wuv": wuv, "wo": wo, "wg": wg, "wu": wu, "wd": wd,
            "gv": gvt, "lball": lball, "cst": cst,
        })
    res = run_bass_kernel_spmd(nc, in_maps, core_ids=list(range(ncores))).results
    g = lambda k: np.stack([r[k] for r in res])
    y_prompt = g("yp")
    y_sample = g("ys").reshape(32, 1, 1024)
    kvp = g("kvp").transpose(1, 0, 2, 3)
    krp = g("krp").transpose(1, 0, 2, 3)
    hp = g("hp").transpose(1, 0, 2, 3, 4).reshape(NL, 8, 4, 64, 64)
    cpo = g("cp").transpose(1, 0, 2, 3)
    kvs = g("kvs").transpose(1, 0, 2, 3).reshape(NL, 32, 1, 256)
    krs = g("krs").transpose(1, 0, 2, 3).reshape(NL, 32, 1, 32)
    hs = g("hs").transpose(1, 0, 2, 3, 4, 5).reshape(NL, 32, 4, 64, 64)
    cs = g("cs").transpose(1, 0, 2, 3, 4).reshape(NL, 32, 2, 256)
    return tuple(np.ascontiguousarray(a, dtype=np.float32) for a in (y_prompt, y_sample, kvp, krp, hp, cpo, kvs, krs, hs, cs))
```

```python
import numpy as np
from contextlib import ExitStack
import concourse.bass as bass
import concourse.mybir as mybir
from concourse.bass_utils import run_bass_kernel_spmd

F32 = mybir.dt.float32
BF16 = mybir.dt.bfloat16
I32 = mybir.dt.int32
ALU = mybir.AluOpType
AF = mybir.ActivationFunctionType
AX = mybir.AxisListType

NL = 4
T = 2048
NS = 4
SEG = 512
EPS = 1e-6
SM_SCALE = 96 ** -0.5
NPAGE = 128
POOL = 5120
POOLN = [5120]


class _Op:
    __slots__ = ("eng", "fn", "dma", "deps", "sig", "sigval")


class Prog:
    ENGS = ("pe", "act", "dve", "pool", "sp")
    BLK = {"pe": "tensor", "act": "scalar", "dve": "vector", "pool": "gpsimd", "sp": "sync"}

    def __init__(self, nc):
        self.nc = nc
        self.ops = []
        self.last_writer = {}
        self.readers = {}

    def add(self, eng, fn, reads=(), writes=(), dma=None):
        o = _Op()
        o.eng, o.fn, o.dma, o.sig, o.sigval = eng, fn, dma, False, 0
        deps = set()
        for r in reads:
            w = self.last_writer.get(r)
            if w is not None:
                deps.add(w)
        for w in writes:
            lw = self.last_writer.get(w)
            if lw is not None:
                deps.add(lw)
            for rd in self.readers.get(w, ()):
                if rd.eng == eng and rd.dma is None and dma is None:
                    continue
                deps.add(rd)
        if eng == "pe" and dma is None:
            deps = {d for d in deps if not (d.eng == "pe" and d.dma is None)}
        for d in deps:
            d.sig = True
        o.deps = deps
        for r in reads:
            self.readers.setdefault(r, []).append(o)
        for w in writes:
            self.last_writer[w] = o
            self.readers[w] = []
        self.ops.append(o)
        return o

    def emit(self, stack):
        nc = self.nc
        cnt = {e: 0 for e in self.ENGS}
        dcnt = {}
        for o in self.ops:
            if o.dma is not None:
                dcnt[o.dma] = dcnt.get(o.dma, 0) + 16
                o.sigval = dcnt[o.dma]
            elif o.sig:
                cnt[o.eng] += 1
                o.sigval = cnt[o.eng]
        esem = {e: stack.enter_context(nc.semaphore("e_" + e)) for e in self.ENGS}
        dsem = {k: stack.enter_context(nc.semaphore("d_%d" % i)) for i, k in enumerate(dcnt)}
        block = stack.enter_context(nc.Block())
        for eng in self.ENGS:
            ops = [o for o in self.ops if o.eng == eng]

            def body(e, ops=ops, eng=eng):
                waited = {}
                for o in ops:
                    need = {}
                    for d in o.deps:
                        key = ("d", d.dma) if d.dma is not None else ("e", d.eng)
                        if d.sigval > need.get(key, 0):
                            need[key] = d.sigval
                    for key, v in need.items():
                        if waited.get(key, 0) >= v:
                            continue
                        e.wait_ge(dsem[key[1]] if key[0] == "d" else esem[key[1]], v)
                        waited[key] = v
                    ins = o.fn(e)
                    if o.dma is not None:
                        ins.then_inc(dsem[o.dma], 16)
                    elif o.sig:
                        ins.then_inc(esem[o.eng], 1)

            getattr(block, self.BLK[eng])(body)


G_MIX, G_FFN, G_QN, G_KVN, G_QHN, G_QHR, G_KHN, G_KHR, G_ON, G_CON, G_CW, G_HON, G_QHNM, G_ONP = 0, 8, 16, 19, 21, 22, 23, 24, 25, 33, 35, 41, 42, 44
NG = 48
C_ID, C_ROT, C_BO64, C_BO32, C_ONE, C_BD, C_I2, C_IND, C_CM, C_TRI = 0, 128, 256, 384, 512, 640, 768, 832, 864, 896
C_EB = 1024
NCSTB = 1536
C_INVF, C_PM, C_MA, C_MS = 1536, 1537, 1569, 1571
NCST = 1575
NLAYERS = NL
MAXOPS = None
SMIX = 'ABC'
FINAL_FILTER = None
DBG = dict(A=True, B=True, C=True, ATT=True, FFN=True, WOUT=True, S=True)

W_AQ, W_AF, W_AI, W_AG, W_BQ, W_BKV, W_BR, W_CB, W_CC, W_CH, W_KR3 = 0, 256, 512, 768, 1024, 1408, 1664, 1696, 1952, 2208, 2464
WIN_COLS = 2464 + 128


def build_nc():
    nc = bass.Bass("TRN2", target_bir_lowering=False)
    D = {}

    def din(name, shape, dt=F32):
        D[name] = nc.dram_tensor(name, list(shape), dt, kind="ExternalInput").ap()

    def dout(name, shape):
        D[name] = nc.dram_tensor(name, list(shape), F32, kind="ExternalOutput").ap()

    din("xp", [T, 1024]); din("xs", [NS, 1024])
    for l in range(NL):
        din("ckv%d" % l, [POOLN[0] * 16, 2048]); din("ckr%d" % l, [POOLN[0] * 2, 2048])
    din("sh", [NL, NS, 2, 128, 64]); din("sc", [NL, NS, 2, 256]); din("ptT", [128, NS], I32); din("wukT", [NL, 128, 4, 256])
    din("win", [NL, 128, 8, WIN_COLS]); din("wqn", [NL, 128, 3, 512]); din("wqr", [NL, 128, 3, 256])
    din("wuk", [NL, 128, 2, 512]); din("wuv", [NL, 128, 2, 512]); din("wo", [NL, 128, 8, 1024])
    din("wg", [NL, 128, 8, 2816]); din("wu", [NL, 128, 8, 2816]); din("wd", [NL, 128, 22, 1024])
    din("gv", [128, NL, NG]); din("lball", [128, 2, NL]); din("cst", [128, NCST])
    dout("yp", [T, 1024]); dout("ys", [NS, 1024]); dout("kvp", [NL, T, 256]); dout("krp", [NL, T, 32])
    dout("hp", [NL, 2, 128, 64]); dout("cp", [NL, 2, 256]); dout("kvs", [NL, NS, 256]); dout("krs", [NL, NS, 32])
    dout("hs", [NL, NS, 2, 128, 64]); dout("cs", [NL, NS, 2, 256])

    st = ExitStack()
    with st:
        P = Prog(nc)
        sbn = [0]

        def sb(shape, dt=F32, stack=st):
            sbn[0] += 1
            return stack.enter_context(nc.sbuf_tensor("t%d" % sbn[0], list(shape), dt))

        PS = [st.enter_context(nc.psum_tensor("ps%d" % i, [128, 512], F32)) for i in range(8)]
        psn = [0]

        def nps(lo=0, hi=6):
            i = lo + psn[0] % (hi - lo)
            psn[0] += 1
            return i

        outkeys = []
        dman = [0]

        def mm(out, lhsT, rhs, start, stop, rd, wr):
            P.add("pe", lambda e: e.matmul(out, lhsT=lhsT, rhs=rhs, start=start, stop=stop), reads=rd, writes=wr)

        def tr(out, in_, ident, rd, wr):
            P.add("pe", lambda e: e.transpose(out=out, in_=in_, identity=ident), reads=rd, writes=wr)

        def act(out, in_, func, rd, wr, scale=None, bias=None):
            kw = {}
            if scale is not None:
                kw["scale"] = scale
            if bias is not None:
                kw["bias"] = bias
            P.add("act", lambda e: e.activation(out=out, in_=in_, func=func, **kw), reads=rd, writes=wr)

        def tt(out, in0, in1, op, rd, wr, eng="dve"):
            P.add(eng, lambda e: e.tensor_tensor(out=out, in0=in0, in1=in1, op=op), reads=rd, writes=wr)

        def ts(out, in0, s1, s2, op0, op1, rd, wr, eng="dve"):
            if op1 is None:
                P.add(eng, lambda e: e.tensor_scalar(out=out, in0=in0, scalar1=s1, scalar2=None, op0=op0), reads=rd, writes=wr)
            else:
                P.add(eng, lambda e: e.tensor_scalar(out=out, in0=in0, scalar1=s1, scalar2=s2, op0=op0, op1=op1), reads=rd, writes=wr)

        def stt(out, in0, scalar, in1, op0, op1, rd, wr):
            P.add("dve", lambda e: e.scalar_tensor_tensor(out=out, in0=in0, scalar=scalar, in1=in1, op0=op0, op1=op1), reads=rd, writes=wr)

        def cp(out, in_, rd, wr, eng="dve"):
            P.add(eng, lambda e: e.tensor_copy(out=out, in_=in_), reads=rd, writes=wr)

        def mset(ap, v, wr, eng="dve"):
            P.add(eng, lambda e: e.memset(ap, v), writes=wr)

        def dma(out, in_, rd, wr, key=None, eng="sp", slow=False):
            if key is None:
                dman[0] += 1
                key = "dma%d" % dman[0]
            if slow:
                P.add(eng, lambda e: e.dma_start(out=out, in_=in_, allow_slow_non_contiguous=True), reads=rd, writes=wr, dma=key)
            else:
                P.add(eng, lambda e: e.dma_start(out=out, in_=in_), reads=rd, writes=wr, dma=key)

        def gather(out, table, idx_ap, rd, wr, key):
            P.add("pool", lambda e: e.indirect_dma_start(out=out, out_offset=None, in_=table,
                                                         in_offset=bass.IndirectOffsetOnAxis(ap=idx_ap, axis=0)), reads=rd, writes=wr, dma=key)

        def treduce(out, in_, rd, wr):
            P.add("dve", lambda e: e.tensor_reduce(out=out, in_=in_, axis=AX.X, op=ALU.add), reads=rd, writes=wr)

        def store(out, in_, rd, key, sem, slow=False):
            ok = "O_" + key
            outkeys.append(ok)
            if slow:
                P.add("act", lambda e: e.dma_start(out=out, in_=in_, allow_slow_non_contiguous=True), reads=rd, writes=[ok], dma="S_" + sem)
            else:
                P.add("act", lambda e: e.dma_start(out=out, in_=in_), reads=rd, writes=[ok], dma="S_" + sem)

        def scan_add(out, ones_ap, in_, rd, wr):
            P.add("dve", lambda e: e.tensor_tensor_scan(out=out, data0=ones_ap, data1=in_, initial=0.0, op0=ALU.mult, op1=ALU.add), reads=rd, writes=wr)

        def recip(out, in_, rd, wr):
            P.add("dve", lambda e: e.reciprocal(out=out, in_=in_), reads=rd, writes=wr)

        def xk(c0):
            return "xT%d" % min(c0 // SEG, 4)

        NT = T + NS
        xT = sb([128, 8, NT])
        cst = sb([128, NCST])
        cstb = sb([128, NCSTB], BF16)
        gv = sb([128, NL, NG])
        lbt = sb([128, 2, NL]); lbe = sb([128, 2, NL]); lba = sb([128, 2, NL]); oml = sb([128, 2, NL]); lbs = sb([128, 2])
        cosT = sb([128, T], BF16); sinT = sb([128, T], BF16); cosS = sb([128, 1]); sinS = sb([128, 1])
        onesf = sb([128, SEG]); epsc = sb([128, 1])
        UW = 31360
        U = sb([128, UW])

        class Buf:
            pass

        class Arena:
            def __init__(self):
                self.top = 0

            def alloc(self, shape, dt=F32):
                n = int(np.prod(shape))
                words = n if dt in (F32, I32) else (n + 1) // 2
                words = (words + 63) // 64 * 64
                off = self.top
                self.top += words
                self.peak = max(getattr(self, "peak", 0), self.top)
                assert self.top <= UW, ("arena overflow", self.top)
                ap = U[:, off:off + words]
                if dt != F32:
                    ap = ap.bitcast(dt)
                ap = ap[:, 0:n]
                if len(shape) == 2:
                    ap = ap.rearrange("p (a b) -> p a b", a=shape[0])
                elif len(shape) == 3:
                    ap = ap.rearrange("p (a b c) -> p a b c", a=shape[0], b=shape[1])
                b = Buf()
                b.ap = ap
                b.k = [("U", g) for g in range(off // 64, (off + words) // 64)]
                return b

        AR = Arena()

        dma(cst[:], D["cst"], [], ["cst"]); dma(gv[:], D["gv"], [], ["gv"]); dma(lbt[:], D["lball"], [], ["lbt"])
        cp(cstb[:], cst[:, 0:NCSTB], ["cst"], ["cstb"])
        mset(onesf[:], 1.0, ["onesf"]); mset(epsc[:], EPS, ["epsc"])
        identf = cst[:, C_ID:C_ID + 128]; identb = cstb[:, C_ID:C_ID + 128]; rotf = cst[:, C_ROT:C_ROT + 128]
        bo64 = cstb[:, C_BO64:C_BO64 + 128]; bo32 = cstb[:, C_BO32:C_BO32 + 128]; onesb = cstb[:, C_ONE:C_ONE + 128]
        bdm = cst[:, C_BD:C_BD + 128]; i2f = cst[:, C_I2:C_I2 + 64]; indb = cstb[:, C_IND:C_IND + 32]; cmf = cst[0:32, C_CM:C_CM + 32]
        invf = cst[:, C_INVF:C_INVF + 1]; pm4 = cst[0:4, C_PM:C_PM + 32]; trib = cstb[:, C_TRI:C_TRI + 128]
        bo64f = cst[:, C_BO64:C_BO64 + 128]

        ptT = sb([128, NS], I32); ptf = sb([128, NS]); io16 = sb([128, 16], I32); io16f = sb([128, 16])
        idxf = sb([128, NS, 16]); idx16 = sb([128, NS, 16], I32); idx2f = sb([128, NS, 2]); idx2 = sb([128, NS, 2], I32)
        dma(ptT[:], D["ptT"], [], ["ptT"])
        P.add("pool", lambda e: e.iota(io16[:], pattern=[[1, 16]], base=0, channel_multiplier=0), writes=["io16"])
        cp(io16f[:], io16[:], ["io16"], ["io16f"]); cp(ptf[:], ptT[:], ["ptT"], ["ptf"])
        for b in range(NS):
            ts(idxf[:, b, :], io16f[:], 0.0, ptf[:, b:b + 1], ALU.mult, ALU.add, ["io16f", "ptf"], ["idxf"])
            stt(idxf[:, b, :], idxf[:, b, :], 16.0, io16f[:], ALU.mult, ALU.add, ["idxf", "io16f"], ["idxf"])
            ts(idx2f[:, b, :], io16f[:, 0:2], 0.0, ptf[:, b:b + 1], ALU.mult, ALU.add, ["io16f", "ptf"], ["idx2f"])
            stt(idx2f[:, b, :], idx2f[:, b, :], 2.0, io16f[:, 0:2], ALU.mult, ALU.add, ["idx2f", "io16f"], ["idx2f"])
        cp(idx16[:], idxf[:], ["idxf"], ["idx16"]); cp(idx2[:], idx2f[:], ["idx2f"], ["idx2"])

        act(lbe[:], lbt[:], AF.Exp, ["lbt"], ["lbe"])
        P.add("dve", lambda e: e.tensor_reduce(out=lbs[:], in_=lbe[:], axis=AX.X, op=ALU.add), reads=["lbe"], writes=["lbs"])
        P.add("dve", lambda e: e.reciprocal(out=lbs[:], in_=lbs[:]), reads=["lbs"], writes=["lbs"])
        for pr in range(2):
            ts(lbe[:, pr, :], lbe[:, pr, :], lbs[:, pr:pr + 1], None, ALU.mult, None, ["lbe", "lbs"], ["lbe"])
        mset(lba[:, :, 0:1], 0.0, ["lba"])
        for l in range(1, NL):
            tt(lba[:, :, l:l + 1], lba[:, :, l - 1:l], lbe[:, :, l:l + 1], ALU.add, ["lba", "lbe"], ["lba"])
        ts(oml[:], lba[:], -1.0, 1.0, ALU.mult, ALU.add, ["lba"], ["oml"])

        m0 = AR.top
        rt_n = AR.alloc([T]); rt_i = AR.alloc([T], I32); rt_q = AR.alloc([T]); posP = AR.alloc([T]); posS = AR.alloc([1])

        def ropetab(dst, pos, n, off):
            tn = rt_n.ap[:, 0:n]; ti = rt_i.ap[:, 0:n]; tq = rt_q.ap[:, 0:n]
            kn, ki, kq, kp = rt_n.k, rt_i.k, rt_q.k, posP.k + posS.k
            ts(tn, pos, invf, off, ALU.mult, ALU.add, kp + ["cst"], kn)
            cp(ti, tn, kn, ki)
            cp(tq, ti, ki, kq)
            tt(tn, tn, tq, ALU.subtract, kn + kq, kn)
            ts(tq, tn, 0.5, None, ALU.is_gt, None, kn, kq)
            tt(tn, tn, tq, ALU.subtract, kn + kq, kn)
            ts(tq, tn, -0.5, None, ALU.is_lt, None, kn, kq)
            tt(tn, tn, tq, ALU.add, kn + kq, kn)
            act(dst, tn, AF.Sin, kn, ["tab"], scale=-2.0 * np.pi)

        P.add("pool", lambda e: e.iota(rt_i.ap, pattern=[[1, T]], base=0, channel_multiplier=0), writes=rt_i.k)
        cp(posP.ap, rt_i.ap, rt_i.k, posP.k)
        mset(posS.ap, float(NPAGE * 128), posS.k)
        ropetab(sinT[:], posP.ap, T, 0.5); ropetab(cosT[:], posP.ap, T, 0.75)
        ropetab(sinS[:], posS.ap, 1, 0.5); ropetab(cosS[:], posS.ap, 1, 0.75)
        AR.top = m0

        NBLK = [(i * SEG, SEG) for i in range(T // SEG)] + [(T, NS)]
        xin = [AR.alloc([1024]) for _ in range(2)]
        for tb in range(17):
            rows = 128 if tb < 16 else NS
            src = D["xp"][tb * 128:(tb + 1) * 128, :] if tb < 16 else D["xs"]
            xi = xin[tb % 2]
            dma(xi.ap[0:rows, :], src, [], xi.k, key="xin%d" % (tb % 2))
            for half in range(2):
                pb = nps()
                for k in range(4):
                    dc = half * 4 + k
                    tr(PS[pb][:, k * 128:k * 128 + rows], xi.ap[0:rows, dc * 128:(dc + 1) * 128], identf[0:rows, 0:rows],
                       xi.k + ["cst"], [("ps", pb)])
                cp(xT[:, half * 4:half * 4 + 4, tb * 128:tb * 128 + rows],
                   PS[pb][:].rearrange("p (k t) -> p k t", k=4)[:, :, 0:rows], [("ps", pb)], [xk(tb * 128)])
        AR.top = m0

        wstg = [AR.alloc([1024]) for _ in range(3)]
        wbf = [AR.alloc([1024], BF16) for _ in range(3)]
        wn = [0]

        castn = [0]

        def wcast(out, in_, rd, wr):
            e = ("act", "dve", "act", "dve", "pool")[castn[0] % 5]
            castn[0] += 1
            if e == "act":
                act(out, in_, AF.Copy, rd, wr)
            else:
                cp(out, in_, rd, wr, eng=e)

        def wload(src, kp, kc, ncols):
            i = wn[0] % len(wstg)
            i2 = wn[0] % len(wbf)
            wn[0] += 1
            n = kc * ncols
            assert n <= 1024, n
            sv = wstg[i].ap[0:kp, 0:n].rearrange("p (a b) -> p a b", a=kc)
            bv = wbf[i2].ap[0:kp, 0:n].rearrange("p (a b) -> p a b", a=kc)
            dma(sv, src, [], wstg[i].k, key="wst%d" % i)
            wcast(bv, sv, wstg[i].k, wbf[i2].k)
            return bv, wbf[i2].k

        sqT3 = AR.alloc([3, SEG], BF16); lnvT = AR.alloc([SEG]); rstdT = AR.alloc([SEG])

        def rmsnorm_fm(ins, inkeys, n, ones_lhsT, dim, gcols, outs, outkeys_, p0=0, p1=128, kfull=True, dup=False, sq=None):
            sqT = sqT3 if sq is None else sq
            pb = nps()
            r0, r1 = (0, 128) if kfull else (p0, p1)
            nsq = 1 if dup else len(ins)
            for k, a in enumerate(ins[:nsq]):
                act(sqT.ap[r0:r1, k, 0:n], a(r0, r1), AF.Square, inkeys, sqT.k)
            for k in range(nsq):
                mm(PS[pb][r0:r1, 0:n], ones_lhsT, sqT.ap[r0:r1, k, 0:n], k == 0, k == nsq - 1, sqT.k + ["cstb"], [("ps", pb)])
            act(lnvT.ap[p0:p1, 0:n], PS[pb][p0:p1, 0:n], AF.Ln, [("ps", pb), "epsc"], lnvT.k, scale=1.0 / dim, bias=epsc[p0:p1, 0:1])
            act(rstdT.ap[p0:p1, 0:n], lnvT.ap[p0:p1, 0:n], AF.Exp, lnvT.k, rstdT.k, scale=-0.5)
            for k, a in enumerate(ins):
                stt(outs[k], a(p0, p1), gcols[k], rstdT.ap[p0:p1, 0:n], ALU.mult, ALU.mult, inkeys + rstdT.k + ["gv"], outkeys_)

        def psrows(pb, n):
            return lambda a, b: PS[pb][a:b, 0:n]

        def gcol(l, c, p0=0, p1=128):
            return gv[p0:p1, l, c:c + 1]

        obs = [AR.alloc([288]) for _ in range(2)]
        obn = [0]
        mlayer = AR.top

        for l in range(NLAYERS):
            AR.top = mlayer
            uh = AR.alloc([2, SEG + 2])
            Sst = [[AR.alloc([128]) for _ in range(2)] for _ in range(2)]
            Sbf0 = [AR.alloc([128], BF16) for _ in range(2)]
            mbig = AR.top
            knT = AR.alloc([4, T], BF16); vtok = AR.alloc([16, 512], BF16); krT3 = AR.alloc([T], BF16)
            mset(uh.ap[:, :, 0:2], 0.0, uh.k)
            for pr in range(2):
                mset(Sst[pr][0].ap, 0.0, Sst[pr][0].k)
                mset(Sbf0[pr].ap, 0.0, Sbf0[pr].k)
            spp = [0, 0]
            mseg = AR.top
            for (c0, n) in NBLK:
                AR.top = mseg
                prompt = c0 < T
                hT = AR.alloc([8, n], BF16)
                mixed = AR.alloc([8, n], BF16)
                rmsnorm_fm([(lambda a, b, k=k: xT[a:b, k, c0:c0 + n]) for k in range(8)], [xk(c0)], n, onesb, 1024.0,
                           [gcol(l, G_MIX + k) for k in range(8)], [hT.ap[:, k, :] for k in range(8)], hT.k, sq=mixed)

                def proj(col0, m, wsrc=None, kc=8, rhs=None, rk=None):
                    w, wk = wload(D["win"][l, :, :, col0:col0 + m] if wsrc is None else wsrc, 128, kc, m)
                    pb = nps()
                    r_ = hT.ap if rhs is None else rhs
                    for k in range(kc):
                        mm(PS[pb][0:m, 0:n], w[:, k, :], r_[:, k, :], k == 0, k == kc - 1, wk + (hT.k if rk is None else rk), [("ps", pb)])
                    return pb

                mtemp = AR.top
                if prompt and DBG['C']:
                    cc_ = AR.alloc([2, n]); zc = AR.alloc([n]); yy = AR.alloc([n])
                    for ch in range(2):
                        pc = proj(W_CC + ch * 128, 128)
                        act(zc.ap, PS[pc][:, 0:n], AF.Copy, [("ps", pc)], zc.k)
                        ph = proj(W_CH + ch * 128, 128)
                        tt(uh.ap[:, ch, 2:2 + n], zc.ap, PS[ph][:, 0:n], ALU.mult, zc.k + [("ps", ph)], uh.k)
                        ts(yy.ap, uh.ap[:, ch, 2:2 + n], gcol(l, G_CW + ch * 3 + 2), None, ALU.mult, None, uh.k + ["gv"], yy.k)
                        stt(yy.ap, uh.ap[:, ch, 1:1 + n], gcol(l, G_CW + ch * 3 + 1), yy.ap, ALU.mult, ALU.add, uh.k + yy.k + ["gv"], yy.k)
                        stt(yy.ap, uh.ap[:, ch, 0:n], gcol(l, G_CW + ch * 3 + 0), yy.ap, ALU.mult, ALU.add, uh.k + yy.k + ["gv"], yy.k)
                        pbb = proj(W_CB + ch * 128, 128)
                        tt(cc_.ap[:, ch, :], PS[pbb][:, 0:n], yy.ap, ALU.mult, [("ps", pbb)] + yy.k, cc_.k)
                    for ch in range(2):
                        rmsnorm_fm([lambda a, b, ch=ch: cc_.ap[a:b, ch, :]], cc_.k, n, bo64, 64.0,
                                   [gcol(l, G_CON + ch)], [mixed.ap[:, 6 + ch, :]], mixed.k)
                    if c0 + n == T:
                        for ch in range(2):
                            for jj in range(2):
                                store(D["cp"][l, jj:jj + 1, ch * 128:(ch + 1) * 128].rearrange("j p -> p j"), uh.ap[:, ch, n + jj:n + jj + 1], uh.k,
                                      "cp%d_%d_%d" % (l, ch, jj), "uh", slow=True)
                    cp(uh.ap[:, :, 0:2], uh.ap[:, :, n:n + 2], uh.k, uh.k)
                elif (not prompt) and DBG['S'] and 'C' in SMIX:
                    prv = AR.alloc([2, 2, NS]); uS = AR.alloc([2, NS]); cc_ = AR.alloc([2, NS]); zc = AR.alloc([NS]); yy = AR.alloc([NS])
                    for ch in range(2):
                        for jj in range(2):
                            dma(prv.ap[:, ch, jj, :], D["sc"][l, :, jj, ch * 128:(ch + 1) * 128].rearrange("b p -> p b"), [], prv.k, key="prv", slow=True)
                    for ch in range(2):
                        pc = proj(W_CC + ch * 128, 128)
                        act(zc.ap, PS[pc][:, 0:n], AF.Copy, [("ps", pc)], zc.k)
                        ph = proj(W_CH + ch * 128, 128)
                        tt(uS.ap[:, ch, :], zc.ap, PS[ph][:, 0:n], ALU.mult, zc.k + [("ps", ph)], uS.k)
                        ts(yy.ap, uS.ap[:, ch, :], gcol(l, G_CW + ch * 3 + 2), None, ALU.mult, None, uS.k + ["gv"], yy.k)
                        stt(yy.ap, prv.ap[:, ch, 1, :], gcol(l, G_CW + ch * 3 + 1), yy.ap, ALU.mult, ALU.add, prv.k + yy.k + ["gv"], yy.k)
                        stt(yy.ap, prv.ap[:, ch, 0, :], gcol(l, G_CW + ch * 3 + 0), yy.ap, ALU.mult, ALU.add, prv.k + yy.k + ["gv"], yy.k)
                        pbb = proj(W_CB + ch * 128, 128)
                        tt(cc_.ap[:, ch, :], PS[pbb][:, 0:n], yy.ap, ALU.mult, [("ps", pbb)] + yy.k, cc_.k)
                    for ch in range(2):
                        rmsnorm_fm([lambda a, b, ch=ch: cc_.ap[a:b, ch, :]], cc_.k, n, bo64, 64.0,
                                   [gcol(l, G_CON + ch)], [mixed.ap[:, 6 + ch, :]], mixed.k)
                        store(D["cs"][l, :, 0, ch * 128:(ch + 1) * 128].rearrange("b p -> p b"), prv.ap[:, ch, 1, :], prv.k, "cs%d_%d_0" % (l, ch), "prvo", slow=True)
                        store(D["cs"][l, :, 1, ch * 128:(ch + 1) * 128].rearrange("b p -> p b"), uS.ap[:, ch, :], uS.k, "cs%d_%d_1" % (l, ch), "uSo", slow=True)
                else:
                    for ch in range(2):
                        mset(mixed.ap[:, 6 + ch, :], 0.0, mixed.k)
                AR.top = mtemp

                if prompt and DBG['A']:
                    NCH = n // 32
                    for pr in range(2):
                        AR.top = mtemp
                        q = AR.alloc([n]); f = AR.alloc([n]); lg = AR.alloc([n]); kk = AR.alloc([n]); Bc = AR.alloc([n]); tmp = f; ex = lg
                        Bp = AR.alloc([NCH + 1]); El = AR.alloc([NCH])
                        qs = AR.alloc([n], BF16); ksbm = [AR.alloc([n], BF16) for _ in range(2)]; kh = AR.alloc([n], BF16); gs = AR.alloc([n], BF16)
                        vA = AR.alloc([NCH, 128], BF16); khT = AR.alloc([NCH, 128], BF16); Asb = AR.alloc([NCH * 64], BF16)
                        Sbf = AR.alloc([NCH + 1, 128], BF16); oT = q; oN = kk
                        mset(vA.ap, 0.0, vA.k, eng="pool"); mset(khT.ap, 0.0, khT.k, eng="pool"); mset(Asb.ap, 0.0, Asb.k, eng="pool")
                        pq = proj(W_AQ + pr * 128, 128)
                        act(q.ap, PS[pq][:, 0:n], AF.Silu, [("ps", pq)], q.k)
                        pf = proj(W_AF + pr * 128, 128)
                        act(f.ap, PS[pf][:, 0:n], AF.Sigmoid, [("ps", pf)], f.k)
                        ts(f.ap, f.ap, oml[:, pr, l:l + 1], lba[:, pr, l:l + 1], ALU.mult, ALU.add, f.k + ["oml", "lba"], f.k)
                        ts(f.ap, f.ap, 1e-20, None, ALU.max, None, f.k, f.k)
                        act(lg.ap, f.ap, AF.Ln, f.k, lg.k)
                        ts(kk.ap, f.ap, -1.0, 1.0, ALU.mult, ALU.add, f.k, kk.k)
                        pg = proj(W_AG + pr * 128, 128)
                        act(gs.ap, PS[pg][:, 0:n], AF.Silu, [("ps", pg)], gs.k)
                        wv, wvk = wload(D["win"][l, :, :, W_AI + pr * 128:W_AI + pr * 128 + 128], 128, 8, 128)
                        for c4 in range(0, NCH, 4):
                            pv = nps()
                            for ci in range(4):
                                c = c4 + ci
                                for k in range(8):
                                    mm(PS[pv][0:32, ci * 128:(ci + 1) * 128], hT.ap[:, k, c * 32:(c + 1) * 32], wv[:, k, :], k == 0, k == 7,
                                       hT.k + wvk, [("ps", pv)])
                            cp(vA.ap[0:32, c4:c4 + 4, :], PS[pv][0:32, :].rearrange("p (a b) -> p a b", a=4), [("ps", pv)], vA.k)
                        scan_add(Bc.ap, onesf[:, 0:n], lg.ap, lg.k + ["onesf"], Bc.k)
                        mset(Bp.ap[:, 0:1], 0.0, Bp.k)
                        Bv = Bc.ap.rearrange("p (c t) -> p c t", t=32)
                        cp(Bp.ap[:, 1:NCH + 1], Bv[:, :, 31], Bc.k, Bp.k)
                        t3 = tmp.ap.rearrange("p (c t) -> p c t", t=32); e3 = ex.ap.rearrange("p (c t) -> p c t", t=32)
                        Bp3 = Bp.ap.rearrange("p (c o) -> p c o", o=1)
                        bprev = Bp3[:, 0:NCH, :].to_broadcast([128, NCH, 32])
                        bend = Bp3[:, 1:NCH + 1, :].to_broadcast([128, NCH, 32])
                        tt(t3, Bv, bprev, ALU.subtract, Bc.k + Bp.k, tmp.k)
                        act(ex.ap, tmp.ap, AF.Exp, tmp.k, ex.k)
                        tt(qs.ap, q.ap, ex.ap, ALU.mult, q.k + ex.k, qs.k)
                        act(ex.ap, tmp.ap, AF.Exp, tmp.k, ex.k, scale=-1.0)
                        for a in range(2):
                            stt(ksbm[a].ap, kk.ap, cst[:, C_MA + a:C_MA + a + 1], ex.ap, ALU.mult, ALU.mult, kk.k + ex.k + ["cst"], ksbm[a].k)
                        tt(t3, bend, Bv, ALU.subtract, Bc.k + Bp.k, tmp.k)
                        act(ex.ap, tmp.ap, AF.Exp, tmp.k, ex.k)
                        tt(kh.ap, kk.ap, ex.ap, ALU.mult, kk.k + ex.k, kh.k)
                        tt(El.ap, Bp.ap[:, 1:NCH + 1], Bp.ap[:, 0:NCH], ALU.subtract, Bp.k, El.k)
                        act(El.ap, El.ap, AF.Exp, El.k, El.k)
                        for c8 in range(0, NCH, 8):
                            pa = nps()
                            for ci in range(8):
                                c = c8 + ci
                                for a in range(2):
                                    mm(PS[pa][0:32, (ci * 2 + a) * 32:(ci * 2 + a + 1) * 32], ksbm[a].ap[:, c * 32:(c + 1) * 32],
                                       qs.ap[:, c * 32:(c + 1) * 32], True, True, ksbm[a].k + qs.k, [("ps", pa)])
                            tt(Asb.ap[0:32, c8 * 64:(c8 + 8) * 64].rearrange("p (a b) -> p a b", b=32),
                               PS[pa][0:32, :].rearrange("p (a b) -> p a b", b=32), cmf.rearrange("p (o t) -> p o t", o=1).to_broadcast([32, 16, 32]), ALU.mult,
                               [("ps", pa), "cst"], Asb.k)
                        for c8 in range(0, NCH, 8):
                            pt_ = nps()
                            pvb = PS[pt_][:].bitcast(BF16)
                            for ci in range(8):
                                c = c8 + ci
                                tr(pvb[0:32, ci * 128:(ci + 1) * 128], kh.ap[:, c * 32:(c + 1) * 32], identb, kh.k + ["cstb"], [("ps", pt_)])
                            cp(khT.ap[0:32, c8:c8 + 8, :], pvb[0:32, 0:1024].rearrange("p (a b) -> p a b", a=8), [("ps", pt_)], khT.k)
                        cp(Sbf.ap[:, 0, :], Sbf0[pr].ap, Sbf0[pr].k, Sbf.k)
                        for c4 in range(0, NCH, 4):
                            pu = nps()
                            for ci in range(4):
                                c = c4 + ci
                                mm(PS[pu][:, ci * 128:(ci + 1) * 128], khT.ap[:, c, :], vA.ap[:, c, :], True, True, khT.k + vA.k, [("ps", pu)])
                            for ci in range(4):
                                c = c4 + ci
                                so, sn = Sst[pr][spp[pr] % 2], Sst[pr][(spp[pr] + 1) % 2]
                                spp[pr] += 1
                                stt(sn.ap, so.ap, El.ap[:, c:c + 1], PS[pu][:, ci * 128:(ci + 1) * 128], ALU.mult, ALU.add,
                                    so.k + El.k + [("ps", pu)], sn.k)
                                tt(Sbf.ap[:, c + 1, :], sn.ap, bdm, ALU.mult, sn.k + ["cst"], Sbf.k)
                        cp(Sbf0[pr].ap, Sbf.ap[:, NCH, :], Sbf.k, Sbf0[pr].k)
                        po = [nps(), nps()]
                        for c in range(NCH):
                            for a in range(2):
                                mm(PS[po[a]][:, c * 32:(c + 1) * 32], vA.ap[:, c, :], Asb.ap[:, (c * 2 + a) * 32:(c * 2 + a + 1) * 32], True, False,
                                   vA.k + Asb.k, [("ps", po[a])])
                                mm(PS[po[a]][:, c * 32:(c + 1) * 32], Sbf.ap[:, c, :], qs.ap[:, c * 32:(c + 1) * 32], False, True,
                                   Sbf.k + qs.k, [("ps", po[a])])
                        for a in range(2):
                            act(oT.ap[64 * a:64 * a + 64, :], PS[po[a]][64 * a:64 * a + 64, 0:n], AF.Copy, [("ps", po[a])], oT.k)
                        rmsnorm_fm([lambda a, b: oT.ap[a:b, :]], oT.k, n, bo64, 64.0, [gcol(l, G_HON)], [oN.ap], oN.k)
                        tt(mixed.ap[:, pr, :], oN.ap, gs.ap, ALU.mult, oN.k + gs.k, mixed.k)
                        if c0 + n == T:
                            sl_ = Sst[pr][spp[pr] % 2]
                            for a in range(2):
                                store(D["hp"][l, pr, 64 * a:64 * a + 64, :], sl_.ap[64 * a:64 * a + 64, 64 * a:64 * a + 64], sl_.k,
                                      "hp%d_%d_%d" % (l, pr, a), "hp%d" % pr)
                elif (not prompt) and DBG['S'] and 'A' in SMIX:
                    for pr in range(2):
                        AR.top = mtemp
                        q = AR.alloc([NS]); f = AR.alloc([NS]); kk = AR.alloc([NS]); vT = AR.alloc([NS]); oT = AR.alloc([NS]); oN = AR.alloc([NS])
                        gs = AR.alloc([NS], BF16); qb = AR.alloc([NS], BF16)
                        S0 = [AR.alloc([64]) for _ in range(2)]; S1 = [AR.alloc([64]) for _ in range(2)]
                        rhsV = AR.alloc([64]); tV = AR.alloc([64]); Sbd = AR.alloc([128], BF16)
                        pq = proj(W_AQ + pr * 128, 128)
                        act(q.ap, PS[pq][:, 0:n], AF.Silu, [("ps", pq)], q.k)
                        cp(qb.ap, q.ap, q.k, qb.k)
                        pf = proj(W_AF + pr * 128, 128)
                        act(f.ap, PS[pf][:, 0:n], AF.Sigmoid, [("ps", pf)], f.k)
                        ts(f.ap, f.ap, oml[:, pr, l:l + 1], lba[:, pr, l:l + 1], ALU.mult, ALU.add, f.k + ["oml", "lba"], f.k)
                        ts(f.ap, f.ap, 1e-20, None, ALU.max, None, f.k, f.k)
                        ts(kk.ap, f.ap, -1.0, 1.0, ALU.mult, ALU.add, f.k, kk.k)
                        pg = proj(W_AG + pr * 128, 128)
                        act(gs.ap, PS[pg][:, 0:n], AF.Silu, [("ps", pg)], gs.k)
                        pv = proj(W_AI + pr * 128, 128)
                        act(vT.ap, PS[pv][:, 0:n], AF.Copy, [("ps", pv)], vT.k)
                        po1 = nps()
                        for b in range(NS):
                            s0, s1 = S0[b % 2], S1[b % 2]
                            dma(s0.ap, D["sh"][l, b, pr], [], s0.k, key="sh%d" % (b % 2))
                            ts(rhsV.ap, i2f, vT.ap[:, b:b + 1], None, ALU.mult, None, vT.k + ["cst"], rhsV.k)
                            pvb = (po1 + 1 + b % 3) % 6
                            mm(PS[pvb][:, 0:64], bo64f, rhsV.ap, True, True, rhsV.k + ["cst"], [("ps", pvb)])
                            ts(tV.ap, PS[pvb][:, 0:64], kk.ap[:, b:b + 1], None, ALU.mult, None, [("ps", pvb)] + kk.k, tV.k)
                            stt(s1.ap, s0.ap, f.ap[:, b:b + 1], tV.ap, ALU.mult, ALU.add, s0.k + f.k + tV.k, s1.k)
                            store(D["hs"][l, b, pr], s1.ap, s1.k, "hs%d_%d_%d" % (l, b, pr), "hs%d" % (b % 2))
                            ts(Sbd.ap[:, 0:64], s1.ap, cst[:, C_MA:C_MA + 1], None, ALU.mult, None, s1.k + ["cst"], Sbd.k)
                            ts(Sbd.ap[:, 64:128], s1.ap, cst[:, C_MA + 1:C_MA + 2], None, ALU.mult, None, s1.k + ["cst"], Sbd.k)
                            mm(PS[po1][:, b:b + 1], Sbd.ap, qb.ap[:, b:b + 1], True, True, Sbd.k + qb.k, [("ps", po1)])
                        act(oT.ap, PS[po1][:, 0:NS], AF.Copy, [("ps", po1)], oT.k)
                        rmsnorm_fm([lambda a, b: oT.ap[a:b, :]], oT.k, n, bo64, 64.0, [gcol(l, G_HON)], [oN.ap], oN.k)
                        tt(mixed.ap[:, pr, :], oN.ap, gs.ap, ALU.mult, oN.k + gs.k, mixed.k)
                else:
                    for pr in range(2):
                        mset(mixed.ap[:, pr, :], 0.0, mixed.k)
                AR.top = mtemp

                cqn = AR.alloc([3, n], BF16); ckvn = AR.alloc([2, n]); ckvb = AR.alloc([2, n], BF16)
                krn = AR.alloc([n]); kro = AR.alloc([n]); krt = AR.alloc([n])
                pq3 = [proj(W_BQ + j * 128, 128) for j in range(3)]
                rmsnorm_fm([psrows(pq3[j], n) for j in range(3)], [("ps", b_) for b_ in pq3], n, onesb, 384.0,
                           [gcol(l, G_QN + j) for j in range(3)], [cqn.ap[:, j, :] for j in range(3)], cqn.k)
                pk2 = [proj(W_BKV + j * 128, 128) for j in range(2)]
                rmsnorm_fm([psrows(pk2[j], n) for j in range(2)], [("ps", b_) for b_ in pk2], n, onesb, 256.0,
                           [gcol(l, G_KVN + j) for j in range(2)], [ckvn.ap[:, j, :] for j in range(2)], ckvn.k)
                cp(ckvb.ap, ckvn.ap, ckvn.k, ckvb.k, eng="pool")
                pkr = proj(W_KR3, 128)
                rmsnorm_fm([psrows(pkr, n)], [("ps", pkr)], n, bo32, 32.0, [gcol(l, G_KHR)], [krn.ap], krn.k)
                prr = nps()
                mm(PS[prr][:, 0:n], rotf, krn.ap, True, True, krn.k + ["cst"], [("ps", prr)])
                if prompt:
                    tt(kro.ap, krn.ap, cosT[:, c0:c0 + n], ALU.mult, krn.k + ["tab"], kro.k)
                    tt(krt.ap, PS[prr][:, 0:n], sinT[:, c0:c0 + n], ALU.mult, [("ps", prr), "tab"], krt.k)
                else:
                    ts(kro.ap, krn.ap, cosS[:, 0:1], None, ALU.mult, None, krn.k + ["tab"], kro.k)
                    ts(krt.ap, PS[prr][:, 0:n], sinS[:, 0:1], None, ALU.mult, None, [("ps", prr), "tab"], krt.k)
                tt(kro.ap, kro.ap, krt.ap, ALU.add, kro.k + krt.k, kro.k)
                for t0 in range(0, n, 128):
                    r = min(128, n - t0)
                    po_ = nps()
                    for k in range(2):
                        tr(PS[po_][0:r, k * 128:(k + 1) * 128], ckvn.ap[:, k, t0:t0 + r], identf, ckvn.k + ["cst"], [("ps", po_)])
                    tr(PS[po_][0:r, 256:288], kro.ap[0:32, t0:t0 + r], identf[0:32, 0:32], kro.k + ["cst"], [("ps", po_)])
                    ob = obs[obn[0] % 2]; osem = "ob%d" % (obn[0] % 2); obn[0] += 1
                    cp(ob.ap[0:r, 0:288], PS[po_][0:r, 0:288], [("ps", po_)], ob.k)
                    if prompt:
                        store(D["kvp"][l, c0 + t0:c0 + t0 + r, :], ob.ap[0:r, 0:256], ob.k, "kvp%d_%d" % (l, c0 + t0), osem)
                        store(D["krp"][l, c0 + t0:c0 + t0 + r, :], ob.ap[0:r, 256:288], ob.k, "krp%d_%d" % (l, c0 + t0), osem)
                    else:
                        store(D["kvs"][l, :, :], ob.ap[0:r, 0:256], ob.k, "kvs%d" % l, osem)
                        store(D["krs"][l, :, :], ob.ap[0:r, 256:288], ob.k, "krs%d" % l, osem)
                if prompt and DBG['B']:
                    cp(krT3.ap[:, c0:c0 + n], kro.ap, kro.k, krT3.k)
                    for j in range(4):
                        wk_, wkk = wload(D["wuk"][l, :, :, j * 128:(j + 1) * 128], 128, 2, 128)
                        pb = nps()
                        for cc in range(2):
                            mm(PS[pb][:, 0:n], wk_[:, cc, :], ckvb.ap[:, cc, :], cc == 0, cc == 1, wkk + ckvb.k, [("ps", pb)])
                        rmsnorm_fm([psrows(pb, n)], [("ps", pb)], n, bo64, 64.0, [gcol(l, G_KHN)], [knT.ap[:, j, c0:c0 + n]], knT.k)
                    wv0, wvk0 = wload(D["wuv"][l, :, 0:1, :], 128, 1, 512)
                    wv1, wvk1 = wload(D["wuv"][l, :, 1:2, :], 128, 1, 512)
                    for t4 in range(n // 128):
                        pb = nps()
                        mm(PS[pb][:, :], ckvb.ap[:, 0, t4 * 128:(t4 + 1) * 128], wv0[:, 0, :], True, False, wvk0 + ckvb.k, [("ps", pb)])
                        mm(PS[pb][:, :], ckvb.ap[:, 1, t4 * 128:(t4 + 1) * 128], wv1[:, 0, :], False, True, wvk1 + ckvb.k, [("ps", pb)])
                        act(vtok.ap[:, c0 // 128 + t4, :], PS[pb][:, :], AF.Copy, [("ps", pb)], vtok.k)
                    qnm = AR.alloc([2, 4, n], BF16); qrm = AR.alloc([8, n], BF16); qrn = krn; qra = kro; qrb = krt
                    for j in range(4):
                        wq_, wqk = wload(D["wqn"][l, :, :, j * 128:(j + 1) * 128], 128, 3, 128)
                        pb = nps()
                        for cc in range(3):
                            mm(PS[pb][:, 0:n], wq_[:, cc, :], cqn.ap[:, cc, :], cc == 0, cc == 2, wqk + cqn.k, [("ps", pb)])
                        rmsnorm_fm([psrows(pb, n), psrows(pb, n)], [("ps", pb)], n, bo64, 64.0, [gcol(l, G_QHNM), gcol(l, G_QHNM + 1)],
                                   [qnm.ap[:, 0, j, :], qnm.ap[:, 1, j, :]], qnm.k, dup=True)
                    for j in range(2):
                        wq_, wqk = wload(D["wqr"][l, :, :, j * 128:(j + 1) * 128], 128, 3, 128)
                        pb = nps()
                        for cc in range(3):
                            mm(PS[pb][:, 0:n], wq_[:, cc, :], cqn.ap[:, cc, :], cc == 0, cc == 2, wqk + cqn.k, [("ps", pb)])
                        rmsnorm_fm([psrows(pb, n)], [("ps", pb)], n, bo32, 32.0, [gcol(l, G_QHR)], [qrn.ap], qrn.k)
                        pr2 = nps()
                        mm(PS[pr2][:, 0:n], rotf, qrn.ap, True, True, qrn.k + ["cst"], [("ps", pr2)])
                        tt(qra.ap, qrn.ap, cosT[:, c0:c0 + n], ALU.mult, qrn.k + ["tab"], qra.k)
                        tt(qrb.ap, PS[pr2][:, 0:n], sinT[:, c0:c0 + n], ALU.mult, [("ps", pr2), "tab"], qrb.k)
                        tt(qra.ap, qra.ap, qrb.ap, ALU.add, qra.k + qrb.k, qra.k)
                        for s4 in range(4):
                            ts(qrm.ap[:, j * 4 + s4, :], qra.ap, cst[:, C_MS + s4:C_MS + s4 + 1], None, ALU.mult, None, qra.k + ["cst"], qrm.k)
                    PT = [AR.alloc([n], BF16) for _ in range(3)]
                    rl = krn; oNn = kro; oB = krt
                    nkb = (c0 + n) // 128
                    ptn = 0
                    for h in (range(8) if DBG['ATT'] else []):
                        j, a = h // 2, h % 2
                        pO, pL = (4, 5) if h % 2 == 0 else (6, 7)
                        def s_mm(kb):
                            qlo = max(0, kb * 128 - c0)
                            N = n - qlo
                            pS = nps(0, 4)
                            mm(PS[pS][:, 0:N], knT.ap[:, j, kb * 128:(kb + 1) * 128], qnm.ap[:, a, j, qlo:n], True, False,
                               knT.k + qnm.k, [("ps", pS)])
                            mm(PS[pS][:, 0:N], krT3.ap[:, kb * 128:(kb + 1) * 128], qrm.ap[:, h, qlo:n], False, True,
                               krT3.k + qrm.k, [("ps", pS)])
                            return pS, qlo, N

                        nxt = s_mm(0)
                        for kb in range(nkb):
                            pS, qlo, N = nxt
                            pt_b = PT[ptn % 3]; ptn += 1
                            act(pt_b.ap[:, 0:N], PS[pS][:, 0:N], AF.Exp, [("ps", pS)], pt_b.k, scale=SM_SCALE)
                            if kb + 1 < nkb:
                                nxt = s_mm(kb + 1)
                            if kb * 128 >= c0:
                                tt(pt_b.ap[:, 0:128], pt_b.ap[:, 0:128], trib, ALU.mult, pt_b.k + ["cstb"], pt_b.k, eng="pool")
                            mm(PS[pO][:, qlo:n], vtok.ap[:, kb, j * 128:(j + 1) * 128], pt_b.ap[:, 0:N], kb == 0, kb == nkb - 1, vtok.k + pt_b.k, [("ps", pO)])
                            mm(PS[pL][:, qlo:n], onesb, pt_b.ap[:, 0:N], kb == 0, kb == nkb - 1, pt_b.k + ["cstb"], [("ps", pL)])
                        recip(rl.ap, PS[pL][:, 0:n], [("ps", pL)], rl.k)
                        tt(oNn.ap, PS[pO][:, 0:n], rl.ap, ALU.mult, [("ps", pO)] + rl.k, oNn.k)
                        rmsnorm_fm([lambda a_, b_: oNn.ap[a_:b_, :]], oNn.k, n, bo64, 64.0, [gcol(l, G_ON + h)], [oB.ap], oB.k)
                        cp(mixed.ap[64 * a:64 * a + 64, 2 + j, :], oB.ap[64 * a:64 * a + 64, :], oB.k, mixed.k, eng="pool")
                elif (not prompt) and DBG['S'] and 'B' in SMIX:
                    cur = AR.top
                    AR.top = mbig
                    raws = [AR.alloc([8, 256], BF16) for _ in range(2)]
                    krraws = [AR.alloc([128, 32], BF16)]; sqbs = [AR.alloc([4, 1024], BF16)]; cTs = [AR.alloc([2, 1024], BF16)]; ropeDs = [AR.alloc([128, 8])]
                    assert AR.top <= mseg, (AR.top, mseg)
                    AR.top = cur
                    rtmp = AR.alloc([64, 32], BF16); QrB = AR.alloc([NS, 256], BF16); wukS = AR.alloc([2, 512], BF16); wuvS = AR.alloc([2, 512], BF16)
                    QabsT = AR.alloc([2, NS, 8], BF16); qg32 = AR.alloc([2, 4, NS]); qgm = AR.alloc([2, 4, NS], BF16)
                    qr32 = AR.alloc([2, NS]); qrtok = AR.alloc([256], BF16); sqn = AR.alloc([4, NS], BF16)
                    sqbs.append(AR.alloc([4, 1024], BF16)); cTs.append(AR.alloc([2, 1024], BF16)); krraws.append(AR.alloc([128, 32], BF16)); ropeDs.append(AR.alloc([128, 8]))
                    e64as = [AR.alloc([64]) for _ in range(2)]; e64bs = [AR.alloc([64]) for _ in range(2)]; pTs = [AR.alloc([64], BF16) for _ in range(2)]
                    pnew = AR.alloc([NS, 8], BF16); cnew = AR.alloc([257], BF16); krtok = AR.alloc([32]); tmpR = AR.alloc([32, 32]); ropeN = AR.alloc([32])
                    n8 = AR.alloc([8]); n32a = AR.alloc([32]); n32b = AR.alloc([32])
                    olat = AR.alloc([257]); olatn = AR.alloc([256]); rl8 = AR.alloc([1]); olT = AR.alloc([2, 8], BF16); oS = AR.alloc([4, NS])
                    for cc in range(2):
                        w_, wk_ = wload(D["wuk"][l, :, cc:cc + 1, :], 128, 1, 512)
                        cp(wukS.ap[:, cc, :], w_[:, 0, :], wk_, wukS.k, eng="pool")
                        w_, wk_ = wload(D["wuv"][l, :, cc:cc + 1, :], 128, 1, 512)
                        cp(wuvS.ap[:, cc, :], w_[:, 0, :], wk_, wuvS.k, eng="pool")
                    for j in range(4):
                        wq_, wqk = wload(D["wqn"][l, :, :, j * 128:(j + 1) * 128], 128, 3, 128)
                        pb = nps()
                        for cc in range(3):
                            mm(PS[pb][:, 0:n], wq_[:, cc, :], cqn.ap[:, cc, :], cc == 0, cc == 2, wqk + cqn.k, [("ps", pb)])
                        rmsnorm_fm([psrows(pb, n), psrows(pb, n)], [("ps", pb)], n, bo64, 64.0, [gcol(l, G_QHNM), gcol(l, G_QHNM + 1)],
                                   [qg32.ap[:, 0, j, :], qg32.ap[:, 1, j, :]], qg32.k, dup=True)
                    ts(qgm.ap, qg32.ap, gcol(l, G_KHN), None, ALU.mult, None, qg32.k + ["gv"], qgm.k)
                    pqa = nps()
                    for j in range(4):
                        wt_, wtk = wload(D["wukT"][l, :, j:j + 1, :], 128, 1, 256)
                        for a in range(2):
                            for cc in range(2):
                                col = (cc * 8 + 2 * j + a) * NS
                                mm(PS[pqa][:, col:col + NS], wt_[:, 0, cc * 128:(cc + 1) * 128], qgm.ap[:, a, j, :], True, True, wtk + qgm.k, [("ps", pqa)])
                    cp(QabsT.ap.rearrange("p c b h -> p c h b"), PS[pqa][:, 0:64].rearrange("p (c h b) -> p c h b", c=2, h=8), [("ps", pqa)], QabsT.k)
                    for j in range(2):
                        wq_, wqk = wload(D["wqr"][l, :, :, j * 128:(j + 1) * 128], 128, 3, 128)
                        pb = nps()
                        for cc in range(3):
                            mm(PS[pb][:, 0:n], wq_[:, cc, :], cqn.ap[:, cc, :], cc == 0, cc == 2, wqk + cqn.k, [("ps", pb)])
                        rmsnorm_fm([psrows(pb, n)], [("ps", pb)], n, bo32, 32.0, [gcol(l, G_QHR)], [krn.ap], krn.k)
                        pr2 = nps()
                        mm(PS[pr2][:, 0:n], rotf, krn.ap, True, True, krn.k + ["cst"], [("ps", pr2)])
                        ts(krt.ap, krn.ap, cosS[:, 0:1], None, ALU.mult, None, krn.k + ["tab"], krt.k)
                        stt(qr32.ap[:, j, :], PS[pr2][:, 0:n], sinS[:, 0:1], krt.ap, ALU.mult, ALU.add, [("ps", pr2), "tab"] + krt.k, qr32.k)
                    mset(qrtok.ap, 0.0, qrtok.k)
                    ptq = nps()
                    for j in range(2):
                        tr(PS[ptq][0:NS, j * 128:(j + 1) * 128], qr32.ap[:, j, :], identf, qr32.k + ["cst"], [("ps", ptq)])
                    cp(qrtok.ap[0:NS, :], PS[ptq][0:NS, 0:256], [("ps", ptq)], qrtok.k)
                    for b in range(NS):
                        pbq = nps()
                        mm(PS[pbq][:, 0:256], cstb[:, C_EB + b * 128:C_EB + (b + 1) * 128], qrtok.ap, True, True, qrtok.k + ["cstb"], [("ps", pbq)])
                        act(QrB.ap[:, b, :], PS[pbq][:, 0:256], AF.Copy, [("ps", pbq)], QrB.k)
                    pk_ = nps()
                    for m in range(4):
                        for cc in range(2):
                            mm(PS[pk_][:, m * NS:(m + 1) * NS], wukS.ap[:, cc, m * 128:(m + 1) * 128], ckvb.ap[:, cc, :], cc == 0, cc == 1, wukS.k + ckvb.k, [("ps", pk_)])
                    act(sqn.ap, PS[pk_][:, 0:4 * NS].rearrange("p (m b) -> p m b", m=4), AF.Square, [("ps", pk_)], sqn.k)
                    pn_ = nps()
                    for m in range(4):
                        mm(PS[pn_][0:NS, 0:8], sqn.ap[:, m, :], indb[:, m * 8:(m + 1) * 8], m == 0, m == 3, sqn.k + ["cstb"], [("ps", pn_)])
                    for cc in range(2):
                        mm(PS[pn_][0:NS, 8:40], ckvb.ap[:, cc, :], QabsT.ap[:, cc, :, :].rearrange("p b h -> p (b h)"), cc == 0, cc == 1, ckvb.k + QabsT.k, [("ps", pn_)])
                    ptk = nps()
                    tr(PS[ptk][0:NS, 0:32], kro.ap[0:32, :], identf[0:32, 0:32], kro.k + ["cst"], [("ps", ptk)])
                    for cc in range(2):
                        tr(PS[ptk][0:NS, 128 + cc * 128:256 + cc * 128], ckvn.ap[:, cc, :], identf, ckvn.k + ["cst"], [("ps", ptk)])
                    cp(krtok.ap[0:NS, :], PS[ptk][0:NS, 0:32], [("ps", ptk)], krtok.k)
                    mset(cnew.ap, 0.0, cnew.k)
                    mset(cnew.ap[0:NS, 256:257], 1.0, cnew.k)
                    cp(cnew.ap[0:NS, 0:256], PS[ptk][0:NS, 128:384], [("ps", ptk)], cnew.k)
                    tt(tmpR.ap[0:NS, :, :], QrB.ap[0:NS, :, :].rearrange("p b (h r) -> p (b h) r", r=32),
                       krtok.ap[0:NS, :].rearrange("p (o r) -> p o r", o=1).to_broadcast([NS, 32, 32]), ALU.mult, QrB.k + krtok.k, tmpR.k)
                    treduce(ropeN.ap[0:NS, :], tmpR.ap[0:NS, :, :], tmpR.k, ropeN.k)
                    act(n8.ap[0:NS, :], PS[pn_][0:NS, 0:8], AF.Ln, [("ps", pn_), "epsc"], n8.k, scale=1.0 / 64.0, bias=epsc[0:NS, 0:1])
                    act(n8.ap[0:NS, :], n8.ap[0:NS, :], AF.Exp, n8.k, n8.k, scale=-0.5)
                    tt(n32a.ap[0:NS, :].rearrange("p (b h) -> p b h", h=8), PS[pn_][0:NS, 8:40].rearrange("p (b h) -> p b h", h=8),
                       n8.ap[0:NS, :].rearrange("p (o h) -> p o h", o=1).to_broadcast([NS, NS, 8]), ALU.mult, [("ps", pn_)] + n8.k, n32a.k)
                    tt(n32a.ap[0:NS, :], n32a.ap[0:NS, :], ropeN.ap[0:NS, :], ALU.add, n32a.k + ropeN.k, n32a.k)
                    act(n32b.ap[0:NS, :], n32a.ap[0:NS, :], AF.Exp, n32a.k, n32b.k, scale=SM_SCALE)
                    mset(pnew.ap, 0.0, pnew.k)
                    tt(pnew.ap[0:NS, :, :].rearrange("p b h -> p (b h)"), n32b.ap[0:NS, :], pm4, ALU.mult, n32b.k + ["cst"], pnew.k)
                    PACC, PLS = 7, 6
                    def rope_gather(b):
                        krraw = krraws[b % 2]
                        for half in range(2):
                            gather(krraw.ap.rearrange("p t r -> p (t r)")[:, half * 2048:(half + 1) * 2048], D["ckr%d" % l], idx2[:, b, half:half + 1], ["idx2"], krraw.k,
                                   "krraw%d" % (b % 2))

                    def rope_dot(b, h, half):
                        krraw, ropeD = krraws[b % 2], ropeDs[b % 2]
                        tt(rtmp.ap, krraw.ap[:, half * 64:(half + 1) * 64, :],
                           QrB.ap[:, b, h * 32:(h + 1) * 32].rearrange("p (o r) -> p o r", o=1).to_broadcast([128, 64, 32]), ALU.mult,
                           krraw.k + QrB.k, rtmp.k)
                        treduce(ropeD.ap[:, half * 64:(half + 1) * 64, h], rtmp.ap, rtmp.k, ropeD.k)

                    rope_gather(0)
                    for h in range(8):
                        for half in range(2):
                            rope_dot(0, h, half)
                    for b in range(NS):
                        ropeD = ropeDs[b % 2]
                        if b + 1 < NS:
                            rope_gather(b + 1)
                        for ch in range(16):
                            if b + 1 < NS:
                                rope_dot(b + 1, ch // 2, ch % 2)
                            raw = raws[ch % 2]; cT = cTs[ch % 2]; sqb = sqbs[ch % 2]; e64a = e64as[ch % 2]; e64b = e64bs[ch % 2]; pT = pTs[ch % 2]
                            gather(raw.ap.rearrange("p t c -> p (t c)"), D["ckv%d" % l], idx16[:, b, ch:ch + 1], ["idx16"], raw.k, "raw%d" % (ch % 2))
                            ptr = [nps(), nps()]
                            for cc in range(2):
                                pvb_ = PS[ptr[cc]][:].bitcast(BF16)
                                for j in range(8):
                                    tr(pvb_[:, j * 128:(j + 1) * 128], raw.ap[:, j, cc * 128:(cc + 1) * 128], identb, raw.k + ["cstb"], [("ps", ptr[cc])])
                                if cc == 0:
                                    cp(cT.ap[:, cc, :], pvb_[:, 0:1024], [("ps", ptr[cc])], cT.k)
                                else:
                                    act(cT.ap[:, cc, :], pvb_[:, 0:1024], AF.Copy, [("ps", ptr[cc])], cT.k)
                            for half in range(2):
                                for m in range(4):
                                    pk2 = nps()
                                    for cc in range(2):
                                        mm(PS[pk2][:, :], wukS.ap[:, cc, m * 128:(m + 1) * 128], cT.ap[:, cc, half * 512:(half + 1) * 512], cc == 0, cc == 1,
                                           wukS.k + cT.k, [("ps", pk2)])
                                    act(sqb.ap[:, m, half * 512:(half + 1) * 512], PS[pk2][:, :], AF.Square, [("ps", pk2)], sqb.k)
                            pss, psd = nps(), nps()
                            for j in range(8):
                                for m in range(4):
                                    mm(PS[pss][:, j * 8:(j + 1) * 8], sqb.ap[:, m, j * 128:(j + 1) * 128], indb[:, m * 8:(m + 1) * 8], m == 0, m == 3,
                                       sqb.k + ["cstb"], [("ps", pss)])
                                for cc in range(2):
                                    mm(PS[psd][:, j * 8:(j + 1) * 8], cT.ap[:, cc, j * 128:(j + 1) * 128], QabsT.ap[:, cc, b, :], cc == 0, cc == 1,
                                       cT.k + QabsT.k, [("ps", psd)])
                            act(e64a.ap, PS[pss][:, 0:64], AF.Ln, [("ps", pss), "epsc"], e64a.k, scale=1.0 / 64.0, bias=epsc[:, 0:1])
                            act(e64a.ap, e64a.ap, AF.Exp, e64a.k, e64a.k, scale=-0.5)
                            tt(e64b.ap, PS[psd][:, 0:64], e64a.ap, ALU.mult, [("ps", psd)] + e64a.k, e64b.k)
                            tt(e64b.ap.rearrange("p (t h) -> p t h", h=8), e64b.ap.rearrange("p (t h) -> p t h", h=8), ropeD.ap[:, ch * 8:(ch + 1) * 8, :], ALU.add,
                               e64b.k + ropeD.k, e64b.k)
                            act(pT.ap, e64b.ap, AF.Exp, e64b.k, pT.k, scale=SM_SCALE)
                            for j in range(8):
                                mm(PS[PACC][0:8, 0:256], pT.ap[:, j * 8:(j + 1) * 8], raw.ap[:, j, :], ch == 0 and j == 0, False, pT.k + raw.k, [("ps", PACC)])
                                mm(PS[PLS][0:8, 0:1], pT.ap[:, j * 8:(j + 1) * 8], onesb[:, 0:1], ch == 0 and j == 0, False, pT.k + ["cstb"], [("ps", PLS)])
                        mm(PS[PACC][0:8, 0:256], pnew.ap[:, b, :], cnew.ap[:, 0:256], False, True, pnew.k + cnew.k, [("ps", PACC)])
                        mm(PS[PLS][0:8, 0:1], pnew.ap[:, b, :], cnew.ap[:, 256:257], False, True, pnew.k + cnew.k, [("ps", PLS)])
                        act(olat.ap[0:8, 0:256], PS[PACC][0:8, 0:256], AF.Copy, [("ps", PACC)], olat.k)
                        recip(rl8.ap[0:8, :], PS[PLS][0:8, 0:1], [("ps", PLS)], rl8.k)
                        ts(olatn.ap[0:8, :], olat.ap[0:8, 0:256], rl8.ap[0:8, 0:1], None, ALU.mult, None, olat.k + rl8.k, olatn.k)
                        pt2 = nps()
                        for cc in range(2):
                            tr(PS[pt2][:, cc * 8:(cc + 1) * 8], olatn.ap[0:8, cc * 128:(cc + 1) * 128], identf[0:8, 0:8], olatn.k + ["cst"], [("ps", pt2)])
                        cp(olT.ap.rearrange("p c h -> p (c h)"), PS[pt2][:, 0:16], [("ps", pt2)], olT.k)
                        pu2 = nps()
                        for h in range(8):
                            for cc in range(2):
                                mm(PS[pu2][:, h:h + 1], wuvS.ap[:, cc, (h // 2) * 128:(h // 2 + 1) * 128], olT.ap[:, cc, h:h + 1], cc == 0, cc == 1,
                                   wuvS.k + olT.k, [("ps", pu2)])
                        for h in range(8):
                            a = h % 2
                            act(oS.ap[64 * a:64 * a + 64, h // 2, b:b + 1], PS[pu2][64 * a:64 * a + 64, h:h + 1], AF.Copy, [("ps", pu2)], oS.k)
                    for j in range(4):
                        rmsnorm_fm([lambda a_, b_, j=j: oS.ap[a_:b_, j, :]], oS.k, n, bo64, 64.0, [gcol(l, G_ONP + j)], [mixed.ap[:, 2 + j, :]], mixed.k)
                else:
                    for j in range(4):
                        mset(mixed.ap[:, 2 + j, :], 0.0, mixed.k)
                AR.top = mtemp

                for dj in (range(8) if DBG['WOUT'] else []):
                    wo_, wok = wload(D["wo"][l, :, :, dj * 128:(dj + 1) * 128], 128, 8, 128)
                    pb = nps()
                    for kc in range(8):
                        mm(PS[pb][:, 0:n], wo_[:, kc, :], mixed.ap[:, kc, :], kc == 0, kc == 7, wok + mixed.k, [("ps", pb)])
                    tt(xT[:, dj, c0:c0 + n], xT[:, dj, c0:c0 + n], PS[pb][:, 0:n], ALU.add, [xk(c0), ("ps", pb)], [xk(c0)])

            AR.top = mlayer
            h2T = AR.alloc([8, NT], BF16); ffT = AR.alloc([6, NT], BF16); wdq = AR.alloc([6, 1024], BF16); sg = [AR.alloc([SEG]) for _ in range(2)]
            nbase, nbase2 = len(wstg), len(wbf)
            for _ in range(2):
                wstg.append(AR.alloc([1024])); wbf.append(AR.alloc([1024], BF16))
            wbf.append(AR.alloc([1024], BF16))
            sq8 = AR.alloc([8, SEG], BF16)
            for (c0, n) in NBLK:
                rmsnorm_fm([(lambda a, b, k=k: xT[a:b, k, c0:c0 + n]) for k in range(8)], [xk(c0)], n, onesb, 1024.0,
                           [gcol(l, G_FFN + k) for k in range(8)], [h2T.ap[:, k, c0:c0 + n] for k in range(8)], h2T.k, sq=sq8)
            fq = [(0, 6), (6, 6), (12, 5), (17, 5)] if DBG['FFN'] else []
            sgn = 0
            for (f0, nf) in fq:
                for f2 in range(nf):
                    i = wn[0] % len(wstg)
                    wn[0] += 1
                    sv = wstg[i].ap[:, 0:1024]
                    dma(sv, D["wd"][l, :, f0 + f2, :], [], wstg[i].k, key="wst%d" % i)
                    wcast(wdq.ap[:, f2, :], sv, wstg[i].k, wdq.k)
                for fi in range(nf):
                    fch = f0 + fi
                    wg_, wgk = wload(D["wg"][l, :, :, fch * 128:(fch + 1) * 128], 128, 8, 128)
                    wu_, wuk_ = wload(D["wu"][l, :, :, fch * 128:(fch + 1) * 128], 128, 8, 128)
                    for (c0, n) in NBLK:
                        pg_, pu_ = nps(0, 8), nps(0, 8)
                        for k in range(8):
                            mm(PS[pg_][:, 0:n], wg_[:, k, :], h2T.ap[:, k, c0:c0 + n], k == 0, k == 7, wgk + h2T.k, [("ps", pg_)])
                        for k in range(8):
                            mm(PS[pu_][:, 0:n], wu_[:, k, :], h2T.ap[:, k, c0:c0 + n], k == 0, k == 7, wuk_ + h2T.k, [("ps", pu_)])
                        s_ = sg[sgn % 2]; sgn += 1
                        act(s_.ap[:, 0:n], PS[pg_][:, 0:n], AF.Silu, [("ps", pg_)], s_.k)
                        tt(ffT.ap[:, fi, c0:c0 + n], s_.ap[:, 0:n], PS[pu_][:, 0:n], ALU.mult, s_.k + [("ps", pu_)], ffT.k)
                for dj in range(8):
                    for (c0, n) in NBLK:
                        pb = nps(0, 8)
                        for fi in range(nf):
                            mm(PS[pb][:, 0:n], wdq.ap[:, fi, dj * 128:(dj + 1) * 128], ffT.ap[:, fi, c0:c0 + n], fi == 0, fi == nf - 1, wdq.k + ffT.k, [("ps", pb)])
                        tt(xT[:, dj, c0:c0 + n], xT[:, dj, c0:c0 + n], PS[pb][:, 0:n], ALU.add, [xk(c0), ("ps", pb)], [xk(c0)])

            del wstg[nbase:], wbf[nbase2:]

        AR.top = mlayer
        yos = [AR.alloc([1024]) for _ in range(2)]
        for tb in range(17):
            rows = 128 if tb < 16 else NS
            yo = yos[tb % 2]
            for half in range(2):
                pb = nps()
                for k in range(4):
                    dc = half * 4 + k
                    tr(PS[pb][0:rows, k * 128:(k + 1) * 128], xT[:, dc, tb * 128:tb * 128 + rows], identf, [xk(tb * 128), "cst"], [("ps", pb)])
                cp(yo.ap[0:rows, half * 512:(half + 1) * 512], PS[pb][0:rows, :], [("ps", pb)], yo.k)
            if tb < 16:
                store(D["yp"][tb * 128:(tb + 1) * 128, :], yo.ap[0:rows, :], yo.k, "yp%d" % tb, "yo%d" % (tb % 2))
            else:
                store(D["ys"], yo.ap[0:rows, :], yo.k, "ys", "yo%d" % (tb % 2))

        if MAXOPS is not None:
            P.ops = P.ops[:MAXOPS]
        else:
            P.add("sp", lambda e: e.nop(), reads=[k for k in outkeys if FINAL_FILTER is None or FINAL_FILTER in k])
        P.emit(st)
        print("ops", len(P.ops), "arena peak words", AR.peak, "of", UW)
    return nc


def _host_consts():
    c = np.zeros((128, NCST), np.float32)
    p = np.arange(128)
    c[:, C_ID:C_ID + 128] = np.eye(128)
    for m in range(128):
        if m % 32 < 16:
            c[m + 16, C_ROT + m] = -1.0
        else:
            c[m - 16, C_ROT + m] = 1.0
    c[:, C_BO64:C_BO64 + 128] = (p[:, None] // 64 == p[None, :] // 64)
    c[:, C_BO32:C_BO32 + 128] = (p[:, None] // 32 == p[None, :] // 32)
    c[:, C_ONE:C_ONE + 128] = 1.0
    c[:, C_BD:C_BD + 128] = (p[:, None] // 64 == p[None, :] // 64)
    c[:, C_I2:C_I2 + 64] = (p[:, None] % 64 == np.arange(64)[None, :])
    for m in range(4):
        for j in range(8):
            c[:, C_IND + m * 8 + j] = (j == 2 * m + p // 64)
    c[0:32, C_CM:C_CM + 32] = (np.arange(32)[:, None] <= np.arange(32)[None, :])
    c[:, C_TRI:C_TRI + 128] = (p[:, None] <= p[None, :])
    c[:, C_MA] = (p < 64)
    c[:, C_MA + 1] = (p >= 64)
    for s4 in range(4):
        c[:, C_MS + s4] = (p // 32 == s4)
    for b in range(4):
        c[b, C_EB + b * 128:C_EB + (b + 1) * 128] = 1.0
    inv = 10000.0 ** (-np.arange(16, dtype=np.float64) / 16.0)
    c[:, C_INVF] = (inv[p % 16] / (2.0 * np.pi)).astype(np.float32)
    for b in range(4):
        c[b, C_PM + b * 8:C_PM + b * 8 + 8] = 1.0
    return c


def _pcol(v, mod=None):
    v = np.asarray(v, np.float32)
    if mod is not None:
        return v[np.arange(128) % mod][:, None]
    return np.ascontiguousarray(v.reshape(-1, 128).T)


_NC_CACHE = {}


def kernel(x_prompt, x_sample, cache_kv_latent, cache_k_rope, state_hgrn, state_conv, page_table,
           norm_mix_g, w_in, hgrn_lb, hgrn_onorm_g, mla_q_norm_g, mla_w_q_up, mla_kv_norm_g,
           mla_w_kv_up, mla_q_head_g, mla_k_head_g, mla_onorm_g, conv_w, conv_onorm_g, w_out,
           norm_ffn_g, w_gate, w_up, w_down):
    f = lambda a: np.asarray(a, np.float32)
    ncores = 8
    if "nc" not in _NC_CACHE:
        _NC_CACHE["nc"] = build_nc()
    nc = _NC_CACHE["nc"]
    w_in = f(w_in)
    win = w_in.reshape(NL, 8, 128, 2464).transpose(0, 2, 1, 3)
    win = np.ascontiguousarray(np.concatenate([win] + [win[..., W_BR:W_BR + 32]] * 4, axis=-1))
    gvt = np.zeros((128, NL, NG), np.float32)
    for l in range(NL):
        gvt[:, l, G_MIX:G_MIX + 8] = _pcol(f(norm_mix_g)[l])
        gvt[:, l, G_FFN:G_FFN + 8] = _pcol(f(norm_ffn_g)[l])
        gvt[:, l, G_QN:G_QN + 3] = _pcol(f(mla_q_norm_g)[l])
        gvt[:, l, G_KVN:G_KVN + 2] = _pcol(f(mla_kv_norm_g)[l])
        gvt[:, l, G_QHN:G_QHN + 1] = _pcol(f(mla_q_head_g)[l][:64], 64)
        gvt[:, l, G_QHR:G_QHR + 1] = _pcol(f(mla_q_head_g)[l][64:], 32)
        gvt[:, l, G_KHN:G_KHN + 1] = _pcol(f(mla_k_head_g)[l][:64], 64)
        gvt[:, l, G_KHR:G_KHR + 1] = _pcol(f(mla_k_head_g)[l][64:], 32)
        gvt[:, l, G_ON:G_ON + 8] = f(mla_onorm_g)[l].reshape(8, 64).T[np.arange(128) % 64]
        gvt[:, l, G_CON:G_CON + 2] = _pcol(f(conv_onorm_g)[l])
        gvt[:, l, G_CW:G_CW + 6] = f(conv_w)[l].reshape(3, 2, 128).transpose(2, 1, 0).reshape(128, 6)
        gvt[:, l, G_HON:G_HON + 1] = _pcol(f(hgrn_onorm_g)[l], 64)
        gvt[:, l, G_ONP:G_ONP + 4] = _pcol(f(mla_onorm_g)[l])
        gvt[0:64, l, G_QHNM] = f(mla_q_head_g)[l][:64]
        gvt[64:128, l, G_QHNM + 1] = f(mla_q_head_g)[l][:64]
    lball = np.ascontiguousarray(f(hgrn_lb).reshape(NL, 2, 128).transpose(2, 1, 0))
    kmaj = lambda w, kc: np.ascontiguousarray(w.reshape(NL, kc, 128, -1).transpose(0, 2, 1, 3))
    qup = f(mla_w_q_up)
    wqn = kmaj(qup[..., :64].reshape(NL, 384, 512), 3)
    wqr = kmaj(np.ascontiguousarray(qup[..., 64:]).reshape(NL, 384, 256), 3)
    kvup = f(mla_w_kv_up)
    wuk = kmaj(kvup[..., :64].reshape(NL, 256, 512), 2)
    wuv = kmaj(kvup[..., 64:].reshape(NL, 256, 512), 2)
    wukT = np.ascontiguousarray(kvup[..., :64].transpose(0, 2, 3, 1).reshape(NL, 4, 128, 256).transpose(0, 2, 1, 3))
    wo = kmaj(f(w_out), 8); wg = kmaj(f(w_gate), 8); wu = kmaj(f(w_up), 8); wd = kmaj(f(w_down), 22)
    cst = _host_consts()
    pt = np.asarray(page_table, np.int32)
    ckv = f(cache_kv_latent); ckr = f(cache_k_rope); shs = f(state_hgrn); scs = f(state_conv)
    npool = ckv.shape[1]
    assert npool == POOLN[0]
    in_maps = []
    for c in range(ncores):
        m = {"ckv%d" % l: ckv[l].reshape(npool * 16, 2048) for l in range(NL)}
        m.update({"ckr%d" % l: ckr[l].reshape(npool * 2, 2048) for l in range(NL)})
        m.update({"sh": np.ascontiguousarray(shs[:, 4 * c:4 * c + 4].reshape(NL, 4, 2, 128, 64)), "sc": np.ascontiguousarray(scs[:, 4 * c:4 * c + 4]),
                  "ptT": np.ascontiguousarray(pt[4 * c:4 * c + 4].T), "wukT": wukT})
        in_maps.append(m)
        in_maps[-1].update({
            "xp": np.ascontiguousarray(f(x_prompt)[c]),
            "xs": np.ascontiguousarray(f(x_sample)[4 * c:4 * c + 4, 0, :]),
            "win": win, "wqn": wqn, "wqr": wqr, "wuk": wuk, "wuv": wuv, "wo": wo, "wg": wg, "wu": wu, "wd": wd,
            "gv": gvt, "lball": lball, "cst": cst,
        })
    res = run_bass_kernel_spmd(nc, in_maps, core_ids=list(range(ncores))).results
    g = lambda k: np.stack([r[k] for r in res])
    y_prompt = g("yp")
    y_sample = g("ys").reshape(32, 1, 1024)
    kvp = g("kvp").transpose(1, 0, 2, 3)
    krp = g("krp").transpose(1, 0, 2, 3)
    hp = g("hp").transpose(1, 0, 2, 3, 4).reshape(NL, 8, 4, 64, 64)
    cpo = g("cp").transpose(1, 0, 2, 3)
    kvs = g("kvs").transpose(1, 0, 2, 3).reshape(NL, 32, 1, 256)
    krs = g("krs").transpose(1, 0, 2, 3).reshape(NL, 32, 1, 32)
    hs = g("hs").transpose(1, 0, 2, 3, 4, 5).reshape(NL, 32, 4, 64, 64)
    cs = g("cs").transpose(1, 0, 2, 3, 4).reshape(NL, 32, 2, 256)
    return tuple(np.ascontiguousarray(a, dtype=np.float32) for a in (y_prompt, y_sample, kvp, krp, hp, cpo, kvs, krs, hs, cs))
```
